# Optimizing a Trainium2 kernel written in Bass

```python
import jax
import jax.numpy as jnp
from jax import lax
import numpy as np

D_MODEL = 1024
BATCH = 4
SEQ = 8192
DEPTH = 2

GRID_W = 64
CTX_LEN = 256
N_EVEN = (DEPTH + 1) // 2
N_ODD = DEPTH // 2
NORM_EPS = 1e-6

RW_HEAD = 64
RW_DIM = D_MODEL // 2
RW_HEADS = RW_DIM // RW_HEAD
RW_W_LORA = 64
RW_A_LORA = 64
RW_G_LORA = 128
RW_COLS = 3 * RW_DIM + RW_W_LORA + RW_A_LORA + RW_G_LORA
RW_GN_EPS = 64e-5

NA_HEAD = 64
NA_DIM = D_MODEL - RW_DIM
NA_HEADS = NA_DIM // NA_HEAD
NA_ROWS = 8
NA_COLS = 16
NA_SCALE = NA_HEAD ** -0.5
EVEN_COLS = RW_COLS + 3 * NA_DIM

HG_KEY = 128
HG_HEADS = D_MODEL // HG_KEY
HG_KDIM = HG_HEADS * HG_KEY
HG_VAL = D_MODEL // HG_HEADS
HG_VDIM = HG_HEADS * HG_VAL
HG_CHUNK = 64
ODD_COLS = 3 * HG_KDIM + 2 * HG_VDIM

MLP_HIDDEN = 4 * D_MODEL

kernel_name = "hybrid_rwkv7_natten_hgrn2_prefix_dit"


def rmsnorm(x, g, eps=NORM_EPS):
    xf = x.astype(jnp.float32)
    y = xf * lax.rsqrt(jnp.mean(xf * xf, axis=-1, keepdims=True) + eps)
    return (y * g.astype(jnp.float32)).astype(x.dtype)


def modulate(h, shift, scale):
    return h * (1 + scale) + shift


def sq_relu_mlp(h, w1, w2):
    return jnp.square(jax.nn.relu(h @ w1)) @ w2


def bidir_token_shift(p, mu_prev, mu_next):
    zero = jnp.zeros_like(p[:, :1])
    prev = jnp.concatenate([zero, p[:, :-1]], axis=1)
    nxt = jnp.concatenate([p[:, 1:], zero], axis=1)
    return p + (prev - p) * mu_prev + (nxt - p) * mu_next


def _scan_order(z):
    z = jnp.stack([z[0], jnp.flip(z[1], axis=1)])
    return jnp.moveaxis(z, 2, 0)


def _rwkv7_step(S, inp):
    r, w, k, v, kk, a = inp
    s_kk = jnp.einsum('dbhvk,dbhk->dbhv', S, kk)
    S = S * w[..., None, :] - s_kk[..., None] * (kk * a)[..., None, :] + v[..., None] * k[..., None, :]
    return S, jnp.einsum('dbhvk,dbhk->dbhv', S, r)


def head_group_norm(o, g, b):
    of = o.astype(jnp.float32)
    mu = jnp.mean(of, axis=-1, keepdims=True)
    var = jnp.mean(jnp.square(of - mu), axis=-1, keepdims=True)
    y = (of - mu) * lax.rsqrt(var + RW_GN_EPS)
    B, T = o.shape[:2]
    return y.reshape(B, T, -1) * g + b


def rwkv7_mix(p, S0, mu_prev, mu_next, w0, w_up, a0, a_up, g_up, k_k, k_a, r_k, ln_g, ln_b):
    B, T, _ = p.shape
    p = bidir_token_shift(p, mu_prev, mu_next)
    r = p[..., :RW_DIM]
    k = p[..., RW_DIM:2 * RW_DIM]
    v = p[..., 2 * RW_DIM:3 * RW_DIM]
    off = 3 * RW_DIM
    w_lo = p[..., off:off + RW_W_LORA]
    a_lo = p[..., off + RW_W_LORA:off + RW_W_LORA + RW_A_LORA]
    g_lo = p[..., off + RW_W_LORA + RW_A_LORA:]
    w_log = -jax.nn.softplus(-(w0[:, None, None, :] + jnp.einsum('btl,dlc->dbtc', jnp.tanh(w_lo), w_up))) - 0.5
    decay = jnp.exp(-jnp.exp(w_log.astype(jnp.float32)))
    a = jax.nn.sigmoid(a0[:, None, None, :] + jnp.einsum('btl,dlc->dbtc', a_lo, a_up))
    g = jax.nn.sigmoid(g_lo) @ g_up
    heads = lambda z: z.reshape(z.shape[:-1] + (RW_HEADS, RW_HEAD))
    both = lambda z: jnp.stack([z, z])
    kk = heads(k * k_k)
    kk = kk / jnp.maximum(jnp.linalg.norm(kk, axis=-1, keepdims=True), 1e-12)
    k_d = heads(k[None] * (1 + (a - 1) * k_a))
    r_h, v_h = heads(r), heads(v)
    xs = (_scan_order(both(r_h)), _scan_order(heads(decay)), _scan_order(k_d),
          _scan_order(both(v_h)), _scan_order(both(kk)), _scan_order(heads(a)))
    S_fin, o = lax.scan(_rwkv7_step, S0, xs)
    o = jnp.moveaxis(o[:, 0] + jnp.flip(o[:, 1], axis=0), 0, 1)
    o = head_group_norm(o, ln_g, ln_b)
    bonus = jnp.sum(r_h[None] * k_d * r_k, axis=(0, -1))[..., None] * v_h
    out = (o + bonus.reshape(B, T, RW_DIM)) * g
    return out.astype(p.dtype), S_fin


def neighbourhood_attention(q, k, v, k_ctx, v_ctx, rpb):
    B, T, H, dh = q.shape
    rows = T // GRID_W
    kr = min(NA_ROWS, rows)
    q_g = q.reshape(B, rows, GRID_W, H, dh).transpose(1, 0, 3, 2, 4)
    k_g = k.reshape(B, rows, GRID_W, H, dh).transpose(0, 3, 1, 2, 4)
    v_g = v.reshape(B, rows, GRID_W, H, dh).transpose(0, 3, 1, 2, 4)
    kc = k_ctx.transpose(0, 2, 1, 3)
    vc = v_ctx.transpose(0, 2, 1, 3)
    cols = jnp.arange(GRID_W)
    col_start = jnp.clip(cols - NA_COLS // 2, 0, GRID_W - NA_COLS)
    col_idx = col_start[:, None] + jnp.arange(NA_COLS)
    col_rel = col_idx - cols[:, None] + NA_COLS - 1
    n_win = kr * NA_COLS

    def one_row(args):
        r, q_r = args
        r0 = jnp.clip(r - kr // 2, 0, rows - kr)
        row_rel = r0 + jnp.arange(kr) - r + NA_ROWS - 1
        bias = rpb[:, row_rel[:, None, None], col_rel[None, :, :]]
        k_win = lax.dynamic_slice_in_dim(k_g, r0, kr, axis=2)[:, :, :, col_idx]
        v_win = lax.dynamic_slice_in_dim(v_g, r0, kr, axis=2)[:, :, :, col_idx]
        s_win = jnp.einsum('bhqd,bhrqcd->bhqrc', q_r, k_win) * NA_SCALE + bias.transpose(0, 2, 1, 3)[None]
        s_ctx = jnp.einsum('bhqd,bhkd->bhqk', q_r, kc) * NA_SCALE
        s = jnp.concatenate([s_win.reshape(B, H, GRID_W, n_win), s_ctx], axis=-1).astype(jnp.float32)
        pr = jax.nn.softmax(s, axis=-1).astype(v.dtype)
        p_win = pr[..., :n_win].reshape(B, H, GRID_W, kr, NA_COLS)
        return (jnp.einsum('bhqrc,bhrqcd->bhqd', p_win, v_win)
                + jnp.einsum('bhqk,bhkd->bhqd', pr[..., n_win:], vc))

    o = lax.map(one_row, (jnp.arange(rows), q_g))
    return o.transpose(1, 0, 3, 2, 4).reshape(B, T, H * dh)


def context_attention(q, k, v):
    B, L, H, dh = q.shape
    s = jnp.einsum('bqhd,bkhd->bhqk', q, k).astype(jnp.float32) * NA_SCALE
    pr = jax.nn.softmax(s, axis=-1).astype(v.dtype)
    return jnp.einsum('bhqk,bkhd->bqhd', pr, v).reshape(B, L, H * dh)


def hgrn2_inputs(p, lb):
    B, T, _ = p.shape
    q = jax.nn.silu(p[..., :HG_KDIM])
    f = p[..., HG_KDIM:3 * HG_KDIM].reshape(B, T, 2, HG_KDIM).astype(jnp.float32)
    i = p[..., 3 * HG_KDIM:3 * HG_KDIM + HG_VDIM]
    gate = p[..., 3 * HG_KDIM + HG_VDIM:]
    forget = lb + (1 - lb) * jax.nn.sigmoid(f)
    log_f = jnp.log(forget)
    k = 1 - forget
    bh = lambda z, d: z.reshape(B, T, HG_HEADS, d).transpose(0, 2, 1, 3)
    dirs = lambda zf, zb: jnp.stack([zf, jnp.flip(zb, axis=2)])
    qh, ih = bh(q, HG_KEY), bh(i, HG_VAL)
    return (dirs(qh, qh), dirs(bh(k[:, :, 0], HG_KEY), bh(k[:, :, 1], HG_KEY)), dirs(ih, ih),
            dirs(bh(log_f[:, :, 0], HG_KEY), bh(log_f[:, :, 1], HG_KEY)), gate)


def _hgrn2_chunk_step(S, inp):
    qc, kc, vc, gc = inp
    b = jnp.cumsum(gc, axis=-2)
    inter = jnp.einsum('dbhck,dbhkv->dbhcv', qc * jnp.exp(b), S)
    lower_tri = jnp.tril(jnp.ones((HG_CHUNK, HG_CHUNK), dtype=bool))[:, :, None]
    dec = jnp.exp(jnp.where(lower_tri, b[..., :, None, :] - b[..., None, :, :], -jnp.inf))
    att = jnp.einsum('dbhtk,dbhsk,dbhtsk->dbhts', qc, kc, dec)
    o = inter + jnp.einsum('dbhts,dbhsv->dbhtv', att, vc)
    b_last = b[..., -1:, :]
    S = jnp.exp(b_last[..., 0, :])[..., None] * S + jnp.einsum('dbhsk,dbhsv->dbhkv', kc * jnp.exp(b_last - b), vc)
    return S, o


def hgrn2_chunk_scan(S0, q, k, v, g):
    T = q.shape[3]
    nc = T // HG_CHUNK
    chunks = lambda z: jnp.moveaxis(z.reshape(z.shape[:3] + (nc, HG_CHUNK, z.shape[-1])), 3, 0)
    S_fin, o = lax.scan(_hgrn2_chunk_step, S0, (chunks(q), chunks(k), chunks(v), chunks(g)))
    o = jnp.moveaxis(o, 0, 3)
    return o.reshape(o.shape[:3] + (T, o.shape[-1])), S_fin


def hgrn2_final_state(k, v, g):
    b = jnp.cumsum(g, axis=-2)
    return jnp.einsum('dbhtk,dbhtv->dbhkv', k * jnp.exp(b[..., -1:, :] - b), v)


def hgrn2_readout(o, gate, norm_g):
    o = o[0] + jnp.flip(o[1], axis=2)
    B, H, T, V = o.shape
    o = o.transpose(0, 2, 1, 3).reshape(B, T, H * V)
    return (rmsnorm(o, norm_g) * jax.nn.silu(gate)).astype(gate.dtype)


def even_mixer(a_lat, a_ctx, last, w_in, w_out, mu_prev, mu_next, w0, w_up, a0, a_up, g_up,
               k_k, k_a, r_k, ln_g, ln_b, rpb):
    B = a_lat.shape[0]
    p_lat = a_lat @ w_in
    p_ctx = a_ctx @ w_in
    rw = (mu_prev, mu_next, w0, w_up, a0, a_up, g_up, k_k, k_a, r_k, ln_g, ln_b)
    S0 = jnp.zeros((2, B, RW_HEADS, RW_HEAD, RW_HEAD), jnp.float32)
    rw_ctx, S_ctx = rwkv7_mix(p_ctx[..., :RW_COLS], S0, *rw)
    rw_lat, _ = rwkv7_mix(p_lat[..., :RW_COLS], S_ctx, *rw)

    def na_heads(p):
        z = p[..., RW_COLS:]
        sh = z.shape[:2] + (NA_HEADS, NA_HEAD)
        return (z[..., :NA_DIM].reshape(sh), z[..., NA_DIM:2 * NA_DIM].reshape(sh),
                z[..., 2 * NA_DIM:].reshape(sh))

    qc, kc, vc = na_heads(p_ctx)
    ql, kl, vl = na_heads(p_lat)
    na_lat = neighbourhood_attention(ql, kl, vl, kc, vc, rpb)
    y_lat = jnp.concatenate([rw_lat, na_lat], axis=-1) @ w_out
    if last:
        return y_lat, None
    na_ctx = context_attention(qc, kc, vc)
    y_ctx = jnp.concatenate([rw_ctx, na_ctx], axis=-1) @ w_out
    return y_lat, y_ctx


def odd_mixer(a_lat, a_ctx, last, w_in, w_out, lb, norm_g):
    ql, kl, vl, gl, gate_l = hgrn2_inputs(a_lat @ w_in, lb)
    qc, kc, vc, gc, gate_c = hgrn2_inputs(a_ctx @ w_in, lb)
    if last:
        S_ctx = hgrn2_final_state(kc, vc, gc)
    else:
        S0 = jnp.zeros(kc.shape[:3] + (HG_KEY, HG_VAL), jnp.float32)
        o_ctx, S_ctx = hgrn2_chunk_scan(S0, qc, kc, vc, gc)
    o_lat, _ = hgrn2_chunk_scan(S_ctx, ql, kl, vl, gl)
    y_lat = hgrn2_readout(o_lat, gate_l, norm_g) @ w_out
    if last:
        return y_lat, None
    return y_lat, hgrn2_readout(o_ctx, gate_c, norm_g) @ w_out


def setup_inputs(seed: int = 0) -> dict:
    key = jax.random.key(seed)
    ks = iter(jax.random.split(key, 40))
    f32 = jnp.float32
    D = D_MODEL

    def nrm(shape, std):
        return std * jax.random.normal(next(ks), shape, f32)

    def gain(shape, s=0.05):
        return 1.0 + s * jax.random.normal(next(ks), shape, f32)

    return {
        "x": nrm((BATCH, SEQ, D), 1.0),
        "c": nrm((BATCH, D), 1.0),
        "ctx": nrm((BATCH, CTX_LEN, D), 1.0),
        "c_ctx": nrm((D,), 1.0),
        "norm_mix_g": gain((DEPTH, D)),
        "norm_mlp_g": gain((DEPTH, D)),
        "ada_w": nrm((DEPTH, D, 6 * D), 0.5 * D ** -0.5),
        "ada_b": nrm((DEPTH, 6 * D), 0.02),
        "mlp_w1": nrm((DEPTH, D, MLP_HIDDEN), D ** -0.5),
        "mlp_w2": nrm((DEPTH, MLP_HIDDEN, D), MLP_HIDDEN ** -0.5),
        "ev_w_in": nrm((N_EVEN, D, EVEN_COLS), D ** -0.5),
        "ev_w_out": nrm((N_EVEN, RW_DIM + NA_DIM, D), (RW_DIM + NA_DIM) ** -0.5),
        "rw_mu_prev": jax.random.uniform(next(ks), (N_EVEN, RW_COLS), f32, 0.0, 0.5),
        "rw_mu_next": jax.random.uniform(next(ks), (N_EVEN, RW_COLS), f32, 0.0, 0.5),
        "rw_w0": jax.random.uniform(next(ks), (N_EVEN, 2, RW_DIM), f32, -3.0, 1.0),
        "rw_w_up": nrm((N_EVEN, 2, RW_W_LORA, RW_DIM), 0.1),
        "rw_a0": nrm((N_EVEN, 2, RW_DIM), 0.1),
        "rw_a_up": nrm((N_EVEN, 2, RW_A_LORA, RW_DIM), 0.5 * RW_A_LORA ** -0.5),
        "rw_g_up": nrm((N_EVEN, RW_G_LORA, RW_DIM), RW_G_LORA ** -0.5),
        "rw_k_k": 0.85 + nrm((N_EVEN, RW_DIM), 0.05),
        "rw_k_a": gain((N_EVEN, RW_DIM)),
        "rw_r_k": nrm((N_EVEN, RW_HEADS, RW_HEAD), 0.1),
        "rw_ln_g": gain((N_EVEN, RW_DIM)),
        "rw_ln_b": nrm((N_EVEN, RW_DIM), 0.02),
        "na_rpb": nrm((N_EVEN, NA_HEADS, 2 * NA_ROWS - 1, 2 * NA_COLS - 1), 0.1),
        "od_w_in": nrm((N_ODD, D, ODD_COLS), D ** -0.5),
        "od_w_out": nrm((N_ODD, HG_VDIM, D), HG_VDIM ** -0.5),
        "hg_lower": gain((DEPTH, HG_KDIM), 0.1),
        "hg_norm_g": gain((N_ODD, HG_VDIM)),
        "final_norm_g": gain((D,)),
    }


def reference(x, c, ctx, c_ctx, norm_mix_g, norm_mlp_g, ada_w, ada_b, mlp_w1, mlp_w2,
              ev_w_in, ev_w_out, rw_mu_prev, rw_mu_next, rw_w0, rw_w_up, rw_a0, rw_a_up, rw_g_up,
              rw_k_k, rw_k_a, rw_r_k, rw_ln_g, rw_ln_b, na_rpb, od_w_in, od_w_out, hg_lower,
              hg_norm_g, final_norm_g):
    sc = jax.nn.silu(c)
    scc = jax.nn.silu(c_ctx)
    lb_sm = jax.nn.softmax(hg_lower.astype(jnp.float32), axis=0)
    lb_all = jnp.cumsum(lb_sm, axis=0) - lb_sm[0]
    h_lat, h_ctx = x, ctx
    for l in range(DEPTH):
        last = l == DEPTH - 1
        m = sc @ ada_w[l] + ada_b[l]
        mc = scc @ ada_w[l] + ada_b[l]
        sh1, s1, g1, sh2, s2, g2 = jnp.split(m[:, None, :], 6, axis=-1)
        csh1, cs1, cg1, csh2, cs2, cg2 = jnp.split(mc, 6)
        a_lat = modulate(rmsnorm(h_lat, norm_mix_g[l]), sh1, s1)
        a_ctx = modulate(rmsnorm(h_ctx, norm_mix_g[l]), csh1, cs1)
        if l % 2 == 0:
            e = l // 2
            y_lat, y_ctx = even_mixer(a_lat, a_ctx, last, ev_w_in[e], ev_w_out[e], rw_mu_prev[e],
                                      rw_mu_next[e], rw_w0[e], rw_w_up[e], rw_a0[e], rw_a_up[e],
                                      rw_g_up[e], rw_k_k[e], rw_k_a[e], rw_r_k[e], rw_ln_g[e],
                                      rw_ln_b[e], na_rpb[e])
        else:
            o = l // 2
            y_lat, y_ctx = odd_mixer(a_lat, a_ctx, last, od_w_in[o], od_w_out[o], lb_all[l], hg_norm_g[o])
        h_lat = h_lat + g1 * y_lat
        h_lat = h_lat + g2 * sq_relu_mlp(modulate(rmsnorm(h_lat, norm_mlp_g[l]), sh2, s2), mlp_w1[l], mlp_w2[l])
        if not last:
            h_ctx = h_ctx + cg1 * y_ctx
            h_ctx = h_ctx + cg2 * sq_relu_mlp(modulate(rmsnorm(h_ctx, norm_mlp_g[l]), csh2, cs2), mlp_w1[l], mlp_w2[l])
    return rmsnorm(h_lat, final_norm_g)
```

```python
import numpy as np
from contextlib import ExitStack
import concourse.bass as bass
import concourse.mybir as mybir
from concourse.bass_utils import run_bass_kernel_spmd

F32 = mybir.dt.float32
BF16 = mybir.dt.bfloat16
AF = mybir.ActivationFunctionType
ALU = mybir.AluOpType
AX = mybir.AxisListType

QUEUES = ["pe", "act", "dve", "pool", "sp"]
COMPUTE = ["pe", "act", "dve", "pool"]
NS_DMA = 8


class V:
    __slots__ = ("ap", "key")

    def __init__(self, ap, key=None):
        self.ap = ap
        self.key = key

    def __getitem__(self, idx):
        return V(self.ap[idx], self.key)

    def k(self, key):
        return V(self.ap, key)

    def re(self, pat, **kw):
        return V(self.ap.rearrange(pat, **kw), self.key)


class Op:
    __slots__ = ("q", "fn", "deps", "dma", "marked", "cnt", "sem_i", "semval")


def _ap(x):
    return x.ap if isinstance(x, V) else x


class Prog:
    def __init__(self, nc):
        self.nc = nc
        self.phase = 0
        self.total = 0
        self.es = ExitStack()
        self.csem = {q: self.es.enter_context(nc.semaphore(f"c_{q}")) for q in COMPUTE}
        self.dsem = {q: [self.es.enter_context(nc.semaphore(f"d_{q}{i}")) for i in range(NS_DMA)] for q in QUEUES}
        self.cbase = {q: 0 for q in COMPUTE}
        self.dbase = {q: 0 for q in QUEUES}
        self._reset()

    def close(self):
        self.es.close()

    def _reset(self):
        self.q = {e: [] for e in QUEUES}
        self.last_w = {}
        self.readers = {}

    def add(self, q, fn, reads=(), writes=(), dma=False):
        op = Op()
        op.q = q; op.fn = fn; op.dma = dma; op.marked = False; op.cnt = 0
        deps = {}
        rk = [x.key for x in reads if isinstance(x, V) and x.key is not None]
        wk = [x.key for x in writes if isinstance(x, V) and x.key is not None]
        for k in rk:
            w = self.last_w.get(k)
            if w is not None:
                deps[w] = True
        for k in wk:
            w = self.last_w.get(k)
            if w is not None and w not in deps:
                deps[w] = False
            for r in self.readers.get(k, ()):
                if r not in deps:
                    deps[r] = False
        deps.pop(op, None)
        for k in rk:
            self.readers.setdefault(k, []).append(op)
        for k in wk:
            self.last_w[k] = op
            self.readers[k] = []
        op.deps = deps
        self.q[q].append(op)
        self.total += 1
        return op

    def flush(self):
        nc = self.nc
        self.phase += 1
        ph = self.phase
        for q in QUEUES:
            for op in self.q[q]:
                need = []
                for d, raw in op.deps.items():
                    if d.dma or op.dma:
                        need.append(d)
                    elif d.q != op.q:
                        need.append(d)
                    elif raw and op.q != "pe":
                        need.append(d)
                op.deps = need
                for d in need:
                    d.marked = True
        for q in COMPUTE:
            comp = [o for o in self.q[q] if not o.dma]
            if comp:
                comp[-1].marked = True
        fin = {}
        for q in COMPUTE:
            c = self.cbase[q]
            for o in self.q[q]:
                if not o.dma and o.marked:
                    c += 1
                    o.cnt = c
            fin[q] = c
            self.cbase[q] = c
        dfin = {}
        for q in QUEUES:
            i = self.dbase[q]
            for o in self.q[q]:
                if o.dma:
                    o.sem_i = i % NS_DMA
                    o.semval = 16 * (i // NS_DMA + 1)
                    i += 1
            dfin[q] = i
            self.dbase[q] = i
        csem = self.csem
        dsem = self.dsem
        with ExitStack() as es:
            block = es.enter_context(nc.Block())
            bname = {"pe": "tensor", "act": "scalar", "dve": "vector", "pool": "gpsimd", "sp": "sync"}
            for q in QUEUES:
                ops = self.q[q]

                def body(eng, q=q, ops=ops):
                    known = {}

                    def wait(sem, val):
                        kk = id(sem)
                        if known.get(kk, 0) < val:
                            eng.wait_ge(sem, val)
                            known[kk] = val
                    for op in ops:
                        for d in op.deps:
                            if d.dma:
                                wait(dsem[d.q][d.sem_i], d.semval)
                            else:
                                wait(csem[d.q], d.cnt)
                        if op.dma:
                            if op.semval > 16:
                                wait(dsem[q][op.sem_i], op.semval - 16)
                            ins = op.fn(eng)
                            ins.then_inc(dsem[q][op.sem_i], 16)
                        else:
                            ins = op.fn(eng)
                            if op.marked:
                                ins.then_inc(csem[q], 1)
                    for q2 in COMPUTE:
                        if fin[q2] > 0:
                            wait(csem[q2], fin[q2])
                    for q2 in QUEUES:
                        n = dfin[q2]
                        for i in range(min(n, NS_DMA)):
                            cntv = (n - i + NS_DMA - 1) // NS_DMA
                            wait(dsem[q2][i], 16 * cntv)
                getattr(block, bname[q])(body)
        self._reset()

    def dma(self, out, in_, q="sp", **kw):
        o, i = _ap(out), _ap(in_)
        return self.add(q, lambda e: e.dma_start(out=o, in_=i, **kw), reads=[in_], writes=[out], dma=True)

    def mm(self, out, lhsT, rhs, start=True, stop=True):
        o, l, r = _ap(out), _ap(lhsT), _ap(rhs)
        rd = [lhsT, rhs]
        if not start:
            rd.append(out)
        return self.add("pe", lambda e: e.matmul(o, l, r, start=start, stop=stop), reads=rd, writes=[out])

    def tr(self, out, in_, ident):
        o, i, d = _ap(out), _ap(in_), _ap(ident)
        return self.add("pe", lambda e: e.transpose(o, i, d), reads=[in_, ident], writes=[out])

    def act(self, out, in_, func, bias=None, scale=None, accum_out=None, q="act"):
        o, i = _ap(out), _ap(in_)
        kw = {}
        rd = [in_]
        wr = [out]
        if bias is not None:
            kw["bias"] = _ap(bias); rd.append(bias)
        if scale is not None:
            kw["scale"] = _ap(scale); rd.append(scale)
        if accum_out is not None:
            kw["accum_out"] = _ap(accum_out); wr.append(accum_out)
        return self.add(q, lambda e: e.activation(o, i, func, **kw), reads=rd, writes=wr)

    def tt(self, out, in0, in1, op, q="dve"):
        o, a, b = _ap(out), _ap(in0), _ap(in1)
        return self.add(q, lambda e: e.tensor_tensor(o, a, b, op), reads=[in0, in1], writes=[out])

    def ts(self, out, in0, s1, s2, op0, op1=None, accum_out=None, q="dve"):
        o, a = _ap(out), _ap(in0)
        rd = [in0, s1, s2]
        wr = [out]
        kw = {}
        if op1 is not None:
            kw["op1"] = op1
        if accum_out is not None:
            kw["accum_out"] = _ap(accum_out); wr.append(accum_out)
        return self.add(q, lambda e: e.tensor_scalar(o, a, _ap(s1), _ap(s2), op0, **kw), reads=rd, writes=wr)

    def stt(self, out, in0, scalar, in1, op0, op1, q="dve"):
        o, a, b = _ap(out), _ap(in0), _ap(in1)
        return self.add(q, lambda e: e.scalar_tensor_tensor(o, a, _ap(scalar), b, op0, op1),
                        reads=[in0, scalar, in1], writes=[out])

    def copy(self, out, in_, q="dve"):
        o, i = _ap(out), _ap(in_)
        if q == "act":
            return self.add(q, lambda e: e.copy(o, i), reads=[in_], writes=[out])
        return self.add(q, lambda e: e.tensor_copy(o, i), reads=[in_], writes=[out])

    def memset(self, out, val, q="pool"):
        o = _ap(out)
        return self.add(q, lambda e: e.memset(o, val), reads=[], writes=[out])

    def reduce(self, out, in_, op, axis=None, q="dve"):
        o, i = _ap(out), _ap(in_)
        ax = axis if axis is not None else AX.X
        return self.add(q, lambda e: e.tensor_reduce(o, i, ax, op), reads=[in_], writes=[out])

    def recip(self, out, in_):
        o, i = _ap(out), _ap(in_)
        return self.add("dve", lambda e: e.reciprocal(o, i), reads=[in_], writes=[out])


class Alloc:
    def __init__(self, nc):
        self.nc = nc
        self.n = 0

    def sb(self, es, shape, dt=F32, name=None):
        self.n += 1
        nm = f"{name or 't'}_{self.n}"
        t = es.enter_context(self.nc.sbuf_tensor(nm, list(shape), dt))
        return V(t[:], nm)

    def ps(self, es, shape, dt=F32, name=None):
        self.n += 1
        nm = f"{name or 'p'}_{self.n}"
        t = es.enter_context(self.nc.psum_tensor(nm, list(shape), dt))
        return V(t[:], nm)


class Ring:
    def __init__(self, items):
        self.items = items
        self.i = 0

    def next(self):
        x = self.items[self.i % len(self.items)]
        self.i += 1
        return x


D = 1024
KC = 8
NA_MASK = -30000.0


class Cx:
    pass


def load_w_bf16(C, es_outer, dst, src_ap, ncols, stage_ring, blk=512):
    P = C.P
    srcv = src_ap.rearrange("(c p) n -> p c n", p=128)
    i = 0
    for c0 in range(0, ncols, blk):
        n = min(blk, ncols - c0)
        st = stage_ring.next()
        P.dma(st[:, :, 0:n], V(srcv[:, :, c0:c0 + n]), q=("sp" if i % 2 == 0 else "act"))
        for kc in range(8):
            P.copy(dst[:, kc, c0:c0 + n], st[:, kc, 0:n], q=("dve", "pool", "act")[(i * 8 + kc) % 3])
        i += 1


def phase_consts(C):
    P, A, d = C.P, C.A, C.d
    with ExitStack() as es:
        cT = A.sb(es, [128, 8, 2]); sc = A.sb(es, [128, 8, 2])
        P.dma(cT, V(d["cT"]))
        P.act(sc, cT, AF.Silu)
        adab = A.sb(es, [128, 2, 48, 2]); P.dma(adab, V(d["adabT"]))
        gm = A.sb(es, [128, 2, 2, 8])
        P.dma(gm[:, :, 0, :], V(d["gmixT"])); P.dma(gm[:, :, 1, :], V(d["gmlpT"]))
        ident = A.sb(es, [128, 128]); P.dma(ident, V(d["ident"]))
        ones = A.sb(es, [128, 128]); P.memset(ones, 1.0)
        mT = A.sb(es, [128, 2, 48, 2])
        ring = Ring([A.sb(es, [128, 8, 512]) for _ in range(2)])
        pm = A.ps(es, [128, 4, 2])
        for l in range(2):
            wv = d["adaw"][l].rearrange("(c p) n -> p c n", p=128)
            for cb in range(12):
                w = ring.next()
                P.dma(w, V(wv[:, :, cb * 512:(cb + 1) * 512]), q=("sp" if cb % 2 == 0 else "act"))
                for j in range(4):
                    for kc in range(8):
                        P.mm(pm[:, j, :], w[:, kc, j * 128:(j + 1) * 128], sc[:, kc, :], start=(kc == 0), stop=(kc == 7))
                P.tt(mT[:, l, cb * 4:(cb + 1) * 4, :], pm, adab[:, l, cb * 4:(cb + 1) * 4, :], ALU.add)
        aff = A.sb(es, [128, 2, 2, 4, 8])
        for l in range(2):
            for s in range(2):
                P.stt(aff[:, l, s, 0, :], mT[:, l, 8:16, s], 1.0, gm[:, l, 0, :], ALU.add, ALU.mult)
                P.copy(aff[:, l, s, 1, :], mT[:, l, 0:8, s])
                P.stt(aff[:, l, s, 2, :], mT[:, l, 32:40, s], 1.0, gm[:, l, 1, :], ALU.add, ALU.mult)
                P.copy(aff[:, l, s, 3, :], mT[:, l, 24:32, s])
        P.dma(V(d["aff"]), aff, q="pool")
        dring = Ring([A.sb(es, [128, 128]) for _ in range(4)])
        gring = Ring([A.sb(es, [128, 1024]) for _ in range(2)])
        pb = A.ps(es, [128, 1024])
        for l in range(2):
            for s in range(2):
                for g, base in ((0, 16), (1, 40)):
                    for c in range(8):
                        dg = dring.next()
                        P.ts(dg, ident, mT[:, l, base + c, s:s + 1], None, ALU.mult)
                        P.mm(pb[:, c * 128:(c + 1) * 128], ones, dg)
                    gb = gring.next()
                    P.copy(gb, pb, q="act")
                    P.dma(V(d["gateB"][l, s, g]), gb, q="pool")
        P.flush()


def norm_a(C, xt, small, junk, xn):
    P = C.P
    ss = small[:, 0:1]; rs = small[:, 1:2]
    P.memset(ss, 0.0)
    P.act(junk, xt, AF.Square, accum_out=ss)
    P.ts(rs, ss, 1.0 / 1024, 1e-6, ALU.mult, ALU.add)
    P.act(rs, rs, AF.Sqrt)
    P.recip(rs, rs)
    P.ts(xn, xt, rs, None, ALU.mult)


def norm_b(C, xn, aT_dst, affv, which, ident, pt):
    P = C.P
    for c in range(8):
        P.tr(pt[:, c, :], xn[:, c * 128:(c + 1) * 128], ident)
    for c in range(8):
        g = affv[:, 2 * which, c:c + 1]; b = affv[:, 2 * which + 1, c:c + 1]
        if c % 2 == 0:
            P.ts(aT_dst[:, c, :], pt[:, c, :], g, b, ALU.mult, ALU.add)
        else:
            P.act(aT_dst[:, c, :], pt[:, c, :], AF.Identity, bias=b, scale=g)


def norm_to_aT(C, xt, aT_dst, affv, which, ident, pt, small, junk, xn):
    norm_a(C, xt, small, junk, xn)
    norm_b(C, xn, aT_dst, affv, which, ident, pt)


def phase_inproj0(C):
    P, A, d = C.P, C.A, C.d
    with ExitStack() as es:
        W = A.sb(es, [128, 8, 3328], BF16)
        sring = Ring([A.sb(es, [128, 8, 512]) for _ in range(2)])
        load_w_bf16(C, es, W, d["w_in0"], 3328, sring)
        ident = A.sb(es, [128, 128]); P.dma(ident, V(d["ident"]))
        affall = A.sb(es, [128, 2, 2, 4, 8]); P.dma(affall, V(d["aff"]))
        xring = Ring([A.sb(es, [128, 1024]) for _ in range(2)])
        xnring = Ring([A.sb(es, [128, 1024]) for _ in range(2)])
        junk = A.sb(es, [128, 1024])
        smring = Ring([A.sb(es, [128, 2]) for _ in range(4)])
        aring = Ring([A.sb(es, [128, 8, 512], BF16) for _ in range(2)])
        ptring = Ring([A.ps(es, [128, 8, 128]) for _ in range(1)])
        pfring = Ring([A.ps(es, [128, 512]) for _ in range(4)])
        st32 = Ring([A.sb(es, [128, 512]) for _ in range(4)])
        st16 = Ring([A.sb(es, [128, 512], BF16) for _ in range(4)])
        for s, (X, Tn, sfx) in enumerate(((d["x"], C.T, ""), (d["ctx"], C.L, "_c"))):
            affv = affall[:, 0, s]
            sts = list(range(0, Tn, 512))
            pend = {}

            def prep_st(t0):
                nt = min(512, Tn - t0)
                aT = aring.next()
                for i in range(nt // 128):
                    xt = xring.next()
                    P.dma(xt, V(X[t0 + i * 128:t0 + (i + 1) * 128, :]))
                    norm_to_aT(C, xt, aT[:, :, i * 128:(i + 1) * 128], affv, 0, ident, ptring.next(),
                               smring.next(), junk, xnring.next())
                pend[t0] = aT

            prep_st(sts[0])
            for sti, t0 in enumerate(sts):
                nt = min(512, Tn - t0)
                aT = pend.pop(t0)
                fm = [(1536, 64, "lo", 0), (1600, 64, "lo", 64), (1664, 128, "lo", 128)]
                fm += [(1792 + j * 128, 128, "q", j * 128) for j in range(4)]
                fm += [(2304 + j * 128, 128, "k", j * 128) for j in range(4)]
                for (c0, ncol, kind, r0) in fm:
                    pf = pfring.next()
                    for kc in range(8):
                        P.mm(pf[0:ncol, 0:nt], W[:, kc, c0:c0 + ncol], aT[:, kc, 0:nt], start=(kc == 0), stop=(kc == 7))
                    if kind == "lo":
                        st = st32.next()
                        P.copy(st[0:ncol, 0:nt], pf[0:ncol, 0:nt], q="act")
                        P.dma(V(d["lo_raw" + sfx][r0:r0 + ncol, t0:t0 + nt]), st[0:ncol, 0:nt], q="pool")
                    elif kind == "q":
                        st = st16.next()
                        P.act(st[0:ncol, 0:nt], pf[0:ncol, 0:nt], AF.Copy, scale=0.125)
                        P.dma(V(d["qT" + sfx][r0:r0 + ncol, t0:t0 + nt]), st[0:ncol, 0:nt], q="pool")
                    else:
                        st = st16.next()
                        P.copy(st[0:ncol, 0:nt], pf[0:ncol, 0:nt], q="dve")
                        P.dma(V(d["kT" + sfx][r0:r0 + ncol, t0:t0 + nt]), st[0:ncol, 0:nt], q="pool")
                if sti + 1 < len(sts):
                    prep_st(sts[sti + 1])
                for i in range(nt // 128):
                    for g in range(4):
                        c0 = g * 512 if g < 3 else 2816
                        pf = pfring.next()
                        for kc in range(8):
                            P.mm(pf, aT[:, kc, i * 128:(i + 1) * 128], W[:, kc, c0:c0 + 512], start=(kc == 0), stop=(kc == 7))
                        r0 = t0 + i * 128
                        if g < 3:
                            st = st32.next()
                            P.copy(st, pf, q=("dve" if g % 2 == 0 else "act"))
                            P.dma(V(d["rkv_raw" + sfx][r0:r0 + 128, g * 512:(g + 1) * 512]), st, q="pool")
                        else:
                            st = st16.next()
                            P.copy(st, pf, q="dve")
                            P.dma(V(d["vna" + sfx][r0:r0 + 128, :]), st, q="pool")
        P.flush()


def na_pairs(rows):
    out = []
    for i in range(rows // 2):
        r = 2 * i
        r0a = min(max(r - 4, 0), rows - 8); r0b = min(max(r - 3, 0), rows - 8)
        u0 = min(r0a, rows - 9)
        out.append((u0, (r0a - u0, r0a - r + 7, r0b - u0, r0b - (r + 1) + 7)))
    return out


def na_bias2(rpb, rows):
    pairs = na_pairs(rows)
    keys = []
    for _, k in pairs:
        if k not in keys:
            keys.append(k)
    H = rpb.shape[0]
    out = np.full((H, 128, len(keys), 576), NA_MASK, np.float32)
    q = np.arange(64)
    cs = np.clip(q - 8, 0, 48)
    for si, (da, oa, db, ob) in enumerate(keys):
        for half, (dd, oo) in enumerate(((da, oa), (db, ob))):
            for kr in range(8):
                j = dd + kr
                for kc in range(16):
                    kcol = cs + kc
                    out[:, half * 64 + q, si, j * 64 + kcol] = rpb[:, oo + kr, kcol - q + 15]
    return out, keys


def na_gen(C, es):
    P, A, d = C.P, C.A, C.d
    T, L = C.T, C.L
    rows = T // 64
    pairs = na_pairs(rows)
    keys = []
    for _, k in pairs:
        if k not in keys:
            keys.append(k)
    ns = len(keys)
    if True:
        sb = lambda shape, dt=F32: A.sb(es, shape, dt)
        ident = sb([128, 128]); P.dma(ident, V(d["ident"]))
        identb = sb([128, 128], BF16); P.copy(identb, ident)
        qT = [sb([64, T], BF16) for _ in range(2)]
        kT = [sb([64, T], BF16) for _ in range(2)]
        v64 = [sb([64, rows, 64], BF16) for _ in range(2)]
        qcT = [sb([64, L], BF16) for _ in range(2)]
        kcT = [sb([64, L], BF16) for _ in range(2)]
        vc64 = [sb([64, L // 64, 64], BF16) for _ in range(2)]
        nabf = sb([128, ns, 576])
        nabb = [sb([128, ns, 576], BF16) for _ in range(2)]
        psr = Ring([A.ps(es, [128, 1024]) for _ in range(2)])
        ptp = A.ps(es, [64, 16, 128], BF16)
        pot = A.ps(es, [128, 512])
        por = Ring([V(pot.ap[:, k * 64:(k + 1) * 64], "po%d" % k) for k in range(4)])
        pexr = Ring([sb([128, 832], BF16) for _ in range(3)])
        ptsr = Ring([sb([64, 13, 128], BF16) for _ in range(2)])
        smr = Ring([sb([128, 4]) for _ in range(6)])
        RB = 16
        ostr = Ring([sb([128, RB, 64]) for _ in range(3)])
        vview = d["vna"].rearrange("(r p) (h e) -> p r h e", p=64, e=64)
        vcview = d["vna_c"].rearrange("(r p) (h e) -> p r h e", p=64, e=64)
        oview = d["mix0"].rearrange("(i p) c -> p i c", p=128)
        ocview = d["mix0_c"].rearrange("(i p) c -> p i c", p=128)

        def loads(h):
            b = h % 2
            P.dma(qT[b], V(d["qT"][h * 64:(h + 1) * 64, :]))
            P.dma(kT[b], V(d["kT"][h * 64:(h + 1) * 64, :]), q="act")
            P.dma(v64[b], V(vview[:, :, h, :]))
            P.dma(qcT[b], V(d["qT_c"][h * 64:(h + 1) * 64, :]), q="act")
            P.dma(kcT[b], V(d["kT_c"][h * 64:(h + 1) * 64, :]))
            P.dma(vc64[b], V(vcview[:, :, h, :]), q="act")
            P.dma(nabf, V(d["nab2"][h]))
            P.copy(nabb[b], nabf, q="pool")

        units = []
        for h in range(8):
            b = h % 2
            npair = len(pairs)
            for i, (u0, key) in enumerate(pairs):
                si = keys.index(key)
                ql = qT[b][:, i * 128:(i + 1) * 128]
                mms = [(0, 512, [(ql, kT[b][:, u0 * 64:u0 * 64 + 512]), (identb, nabb[b][:, si, 0:512])]),
                       (512, 64, [(ql, kT[b][:, (u0 + 8) * 64:(u0 + 9) * 64]), (identb, nabb[b][:, si, 512:576])]),
                       (576, L, [(ql, kcT[b][:, :])])]
                vch = [v64[b][:, u0 + j, :] for j in range(9)] + [vc64[b][:, j, :] for j in range(L // 64)]
                units.append(dict(h=h, first=(i == 0), mms=mms, nkeys=576 + L, vch=vch, slot=i % RB,
                                  newst=(i % RB == 0),
                                  store=((oview, (i // RB) * RB, i % RB + 1) if (i % RB == RB - 1 or i == npair - 1) else None)))
            nc_ = L // 128
            for i in range(nc_):
                ql = qcT[b][:, i * 128:(i + 1) * 128]
                mms = [(0, L, [(ql, kcT[b][:, :])])]
                vch = [vc64[b][:, j, :] for j in range(L // 64)]
                units.append(dict(h=h, first=False, mms=mms, nkeys=L, vch=vch, slot=i, newst=(i == 0),
                                  store=((ocview, 0, nc_) if i == nc_ - 1 else None)))

        def s1(u):
            ps = psr.next(); u["ps"] = ps
            for (c0, ncol, mm) in u["mms"]:
                for j, (l, r) in enumerate(mm):
                    P.mm(ps[:, c0:c0 + ncol], l, r, start=(j == 0), stop=(j == len(mm) - 1))
            sm = smr.next(); u["sm"] = sm
            nk = u["nkeys"]
            P.reduce(sm[:, 0:1], ps[:, 0:nk], ALU.max)
            P.ts(sm[:, 1:2], sm[:, 0:1], -1.0, None, ALU.mult, q="pool")
            P.memset(sm[:, 2:3], 0.0)
            pe_ = pexr.next(); u["pexp"] = pe_
            P.act(pe_[:, 0:nk], ps[:, 0:nk], AF.Exp, bias=sm[:, 1:2], accum_out=sm[:, 2:3])

        def s2(u):
            nch = u["nkeys"] // 64
            pe_ = u["pexp"]
            for j in range(nch):
                P.tr(ptp[:, j, :], pe_[:, j * 64:(j + 1) * 64], identb)
            pts = ptsr.next(); u["pts"] = pts
            hlf = (nch + 1) // 2
            P.copy(pts[:, 0:hlf, :], ptp[:, 0:hlf, :], q="dve")
            P.copy(pts[:, hlf:nch, :], ptp[:, hlf:nch, :], q="act")

        cur_st = {}

        def s3(u):
            nch = u["nkeys"] // 64
            po = por.next(); pts = u["pts"]; sm = u["sm"]
            for j in range(nch):
                P.mm(po, pts[:, j, :], u["vch"][j], start=(j == 0), stop=(j == nch - 1))
            P.recip(sm[:, 3:4], sm[:, 2:3])
            if u["newst"]:
                cur_st["t"] = ostr.next()
            ost = cur_st["t"]
            P.ts(ost[:, u["slot"], :], po, sm[:, 3:4], None, ALU.mult)
            if u["store"] is not None:
                view, i0, n = u["store"]
                hh = u["h"]
                P.dma(V(view[:, i0:i0 + n, 512 + hh * 64:512 + (hh + 1) * 64]), ost[:, 0:n, :], q="pool")

        loads(0)
        n = len(units)
        for it in range(n + 2):
            if it < n:
                u = units[it]
                if u["first"] and u["h"] + 1 < 8:
                    pass
                s1(u)
                if it >= 3 and units[it - 3]["first"] and units[it - 3]["h"] + 1 < 8:
                    loads(units[it - 3]["h"] + 1)
            if 0 <= it - 1 < n:
                s2(units[it - 1])
            if 0 <= it - 2 < n:
                s3(units[it - 2])
            yield


def declare(C, dbg):
    nc = C.nc
    T, L = C.T, C.L
    d = {}

    def din(name, shape, dt=F32):
        d[name] = nc.dram_tensor(name, list(shape), dt, kind="ExternalInput").ap()

    def scr(name, shape, dt=F32):
        kind = "ExternalOutput" if name in dbg else "Internal"
        d[name] = nc.dram_tensor(name, list(shape), dt, kind=kind).ap()

    din("x", [T, D]); din("ctx", [L, D]); din("cT", [128, 8, 2])
    din("adaw", [2, D, 6 * D]); din("adabT", [128, 2, 48, 2])
    din("gmixT", [128, 2, 8]); din("gmlpT", [128, 2, 8])
    din("ident", [128, 128])
    din("w_in0", [D, 3328]); din("nab2", list(na_bias2(np.zeros((8, 15, 31), np.float32), T // 64)[0].shape))
    din("mupB", [128, 1536]); din("munB", [128, 1536]); din("rwB", [128, 6, 512])
    din("w_up", [64, 2, 512]); din("a_up", [64, 2, 512]); din("g_up", [128, 512]); din("brow", [1, 4, 512])
    din("mulo", [128, 2, 2]); din("cm", [2, 128, 9, 128])
    din("w_out0", [D, D]); din("w_out1", [D, D]); din("mlp_w1", [2, D, 4096]); din("mlp_w2", [2, 4096, D])
    din("w_in1", [D, 5120]); din("fnormB", [128, D])
    d["out"] = nc.dram_tensor("out", [T, D], F32, kind="ExternalOutput").ap()
    scr("mix1", [T, D]); scr("hmid1", [T, D]); scr("hg_o0", [T, D]); scr("hg_o1", [T, D])
    din("hglB", [128, 2, D]); din("hgnB", [128, D]); din("cmh", [2, 128, 3, 128])
    scr("aff", [128, 2, 2, 4, 8]); scr("gateB", [2, 2, 2, 128, D])
    for sfx, Tn in (("", T), ("_c", L)):
        scr("lo_raw" + sfx, [256, Tn]); scr("rkv_raw" + sfx, [Tn, 1536])
        scr("qT" + sfx, [512, Tn], BF16); scr("kT" + sfx, [512, Tn], BF16); scr("vna" + sfx, [Tn, 512], BF16)
        scr("mix0" + sfx, [Tn, D])
        scr("rw_g" + sfx, [Tn, 512]); scr("rw_sh" + sfx, [Tn, 1536]); scr("rw_kk" + sfx, [Tn, 512]); scr("rw_lo" + sfx, [128, Tn])
        for dr_ in range(2):
            scr("rw_o%d" % dr_ + sfx, [Tn, 512]); scr("rw_b%d" % dr_ + sfx, [Tn, 512])
        scr("hmid0" + sfx, [Tn, D]); scr("h1" + sfx, [Tn, D]); scr("p1" + sfx, [Tn, 5120])
    C.d = d


def build(T, L, phases, dbg=()):
    nc = bass.Bass("TRN2", target_bir_lowering=False)
    C = Cx()
    C.nc = nc; C.P = Prog(nc); C.A = Alloc(nc); C.T = T; C.L = L
    declare(C, dbg)
    for ph in phases:
        ph(C)
    return nc, C


def fm_cols(v):
    v = np.asarray(v, np.float32)
    lead = v.shape[:-1]
    return np.ascontiguousarray(np.moveaxis(v.reshape(lead + (8, 128)), -1, 0))


def na_bias(rpb, rows):
    H = rpb.shape[0]
    out = np.full((H, 64, 8, 512), NA_MASK, np.float32)
    q = np.arange(64)
    cs = np.clip(q - 8, 0, 48)
    reps = [0, 1, 2, 3, 4, rows - 3, rows - 2, rows - 1]
    for si, r in enumerate(reps):
        r0 = min(max(r - 4, 0), rows - 8)
        for kr in range(8):
            rr = r0 + kr - r + 7
            for kc in range(16):
                kcol = cs + kc
                out[:, q, si, kr * 64 + kcol] = rpb[:, rr, kcol - q + 15]
    return out


def rw_consts():
    u = np.arange(128)[:, None]; t = np.arange(128)[None, :]
    cm = np.zeros((2, 128, 9, 128), np.float32)
    for dr in range(2):
        if dr == 0:
            bef = (u < t); befeq = (u <= t); half = (u <= 63); aft = (u > t)
        else:
            bef = (u > t); befeq = (u >= t); half = (u >= 64); aft = (u < t)
        cm[dr, :, 0, :] = befeq.astype(np.float32) - half.astype(np.float32)
        cm[dr, :, 1, :] = bef.astype(np.float32) - half.astype(np.float32)
        cm[dr, :, 2, :] = aft
        cm[dr, :, 3, :] = bef
        cm[dr, :, 4, :] = befeq
        cm[dr, :, 5, :] = -bef.astype(np.float32)
        cm[dr, :, 6, :] = -(bef.T).astype(np.float32)
        cm[dr, :, 7, :] = np.broadcast_to(half, (128, 128))
        cm[dr, :, 8, :] = 1.0
    return cm


def prep_inputs(inp, b, T, L):
    f = lambda a: np.ascontiguousarray(np.asarray(a, np.float32))
    m = {}
    m["x"] = f(inp["x"][b][:T]); m["ctx"] = f(inp["ctx"][b][:L])
    cs = np.stack([np.asarray(inp["c"][b], np.float32), np.asarray(inp["c_ctx"], np.float32)], 0)
    m["cT"] = np.ascontiguousarray(np.transpose(fm_cols(cs), (0, 2, 1)))
    m["adaw"] = f(inp["ada_w"])
    ab = fm_cols(np.asarray(inp["ada_b"], np.float32).reshape(2, 6, D))
    ab = ab.reshape(128, 2, 48)
    m["adabT"] = np.ascontiguousarray(np.repeat(ab[:, :, :, None], 2, axis=3))
    m["gmixT"] = fm_cols(inp["norm_mix_g"]); m["gmlpT"] = fm_cols(inp["norm_mlp_g"])
    m["ident"] = np.eye(128, dtype=np.float32)
    m["w_in0"] = f(inp["ev_w_in"][0])
    m["nab2"] = na_bias2(np.asarray(inp["na_rpb"][0], np.float32), T // 64)[0]
    rep = lambda v: np.ascontiguousarray(np.broadcast_to(np.asarray(v, np.float32).reshape(1, -1), (128, np.asarray(v).size)))
    mup = np.asarray(inp["rw_mu_prev"][0], np.float32); mun = np.asarray(inp["rw_mu_next"][0], np.float32)
    m["mupB"] = rep(mup[:1536]); m["munB"] = rep(mun[:1536])
    ka = np.asarray(inp["rw_k_a"][0], np.float32)
    m["rwB"] = np.ascontiguousarray(np.stack([rep(inp["rw_k_k"][0]), rep(ka), rep(inp["rw_r_k"][0].reshape(-1)),
                                              rep(inp["rw_ln_g"][0]), rep(inp["rw_ln_b"][0]), rep(ka)], axis=1))
    m["w_up"] = np.ascontiguousarray(np.transpose(np.asarray(inp["rw_w_up"][0], np.float32), (1, 0, 2)))
    m["a_up"] = np.ascontiguousarray(np.transpose(np.asarray(inp["rw_a_up"][0], np.float32), (1, 0, 2)))
    m["g_up"] = f(inp["rw_g_up"][0])
    m["brow"] = np.ascontiguousarray(np.concatenate([np.asarray(inp["rw_w0"][0], np.float32),
                                                     np.asarray(inp["rw_a0"][0], np.float32)], 0)[None])
    mulo = np.zeros((128, 2, 2), np.float32)
    mulo[:, 0, 0] = mup[1536:1664]; mulo[:, 0, 1] = mun[1536:1664]
    mulo[:, 1, 0] = mup[1664:1792]; mulo[:, 1, 1] = mun[1664:1792]
    m["mulo"] = mulo
    m["cm"] = rw_consts()
    m["w_out0"] = f(inp["ev_w_out"][0]); m["w_out1"] = f(inp["od_w_out"][0])
    m["mlp_w1"] = f(inp["mlp_w1"]); m["mlp_w2"] = f(inp["mlp_w2"]); m["w_in1"] = f(inp["od_w_in"][0])
    m["fnormB"] = rep(inp["final_norm_g"])
    m["hglB"] = np.ascontiguousarray(np.stack([rep(inp["hg_lower"][0]), rep(inp["hg_lower"][1])], axis=1))
    m["hgnB"] = rep(inp["hg_norm_g"][0]); m["cmh"] = hg_consts()
    return m


CDEC = -0.6065306597126334


def bc(v, shape, axis):
    return V(v.ap.unsqueeze(axis).to_broadcast(list(shape)), v.key)


def interleave(gens):
    gens = list(gens)
    while gens:
        for g in list(gens):
            try:
                next(g)
            except StopIteration:
                gens.remove(g)


def rwpre_gen(C, es, NB):
    P, A, d = C.P, C.A, C.d
    sb = lambda shape, dt=F32: A.sb(es, shape, dt)
    mupB = sb([128, 1536]); munB = sb([128, 1536]); c0B = sb([128, 1536])
    P.dma(mupB, V(d["mupB"])); P.dma(munB, V(d["munB"]))
    P.tt(c0B, mupB, munB, ALU.add, q="pool")
    P.ts(c0B, c0B, -1.0, 1.0, ALU.mult, ALU.add, q="pool")
    kkB = sb([128, 512]); P.dma(kkB, V(d["rwB"][:, 0, :]))
    gup = sb([128, 512]); P.dma(gup, V(d["g_up"]))
    mulo = sb([128, 2, 3]); P.dma(mulo[:, :, 1:3], V(d["mulo"]))
    P.tt(mulo[:, :, 0:1], mulo[:, :, 1:2], mulo[:, :, 2:3], ALU.add, q="pool")
    P.ts(mulo[:, :, 0:1], mulo[:, :, 0:1], -1.0, 1.0, ALU.mult, ALU.add, q="pool")
    NG = 3
    curR = [sb([128, 512]) for _ in range(NG)]; prvR = [sb([128, 512]) for _ in range(NG)]; nxtR = [sb([128, 512]) for _ in range(NG)]
    shR = [sb([128, 512]) for _ in range(NG)]
    tA = sb([128, 512]); tB = sb([128, 512])
    loAR = [sb([128, 130]) for _ in range(2)]; loGR = [sb([128, 130]) for _ in range(2)]
    loAs = sb([128, 128]); loGs = sb([128, 128])
    kk = sb([128, 512]); gsb = sb([128, 512]); hs = sb([128, 8])
    pg = A.ps(es, [128, 512])
    h8 = lambda v: v.re("p (h e) -> p h e", e=64)
    units = []
    for seq, Tn in ((1, C.L), (0, C.T)):
        for t0 in range(0, Tn, 128):
            for g in range(3):
                units.append((seq, Tn, t0, g))

    def loads(ui):
        seq, Tn, t0, g = units[ui]
        sfx = "_c" if seq else ""
        rkv = d["rkv_raw" + sfx]
        sl = slice(g * 512, (g + 1) * 512)
        cur = curR[ui % NG]; prv = prvR[ui % NG]; nxt = nxtR[ui % NG]
        P.dma(cur, V(rkv[t0:t0 + 128, sl]))
        if t0 == 0:
            P.memset(prv, 0.0)
            P.dma(prv[1:128, :], V(rkv[0:127, sl]))
        else:
            P.dma(prv, V(rkv[t0 - 1:t0 + 127, sl]))
        if t0 + 128 == Tn:
            P.memset(nxt, 0.0)
            P.dma(nxt[0:127, :], V(rkv[t0 + 1:t0 + 128, sl]))
        else:
            P.dma(nxt, V(rkv[t0 + 1:t0 + 129, sl]))
        if g == 0:
            lo = d["lo_raw" + sfx]
            ti = t0 // 128
            loA = loAR[ti % 2]; loG = loGR[ti % 2]
            lo_a = max(t0 - 1, 0); lo_b = min(t0 + 129, Tn)
            c_a = lo_a - (t0 - 1); c_b = c_a + (lo_b - lo_a)
            if c_a > 0 or c_b < 130:
                P.memset(loA, 0.0); P.memset(loG, 0.0)
            P.dma(loA[:, c_a:c_b], V(lo[0:128, lo_a:lo_b]))
            P.dma(loG[:, c_a:c_b], V(lo[128:256, lo_a:lo_b]))

    loads(0)
    if len(units) > 1:
        loads(1)
    for ui, (seq, Tn, t0, g) in enumerate(units):
        sfx = "_c" if seq else ""
        if ui + 2 < len(units):
            loads(ui + 2)
        cur = curR[ui % NG]; prv = prvR[ui % NG]; nxt = nxtR[ui % NG]; sh = shR[ui % NG]
        sl = slice(g * 512, (g + 1) * 512)
        P.tt(tA, cur, c0B[:, sl], ALU.mult, q="pool")
        P.tt(tB, prv, mupB[:, sl], ALU.mult, q="dve")
        P.tt(tA, tA, tB, ALU.add, q="pool")
        P.tt(tB, nxt, munB[:, sl], ALU.mult, q="dve")
        P.tt(sh, tA, tB, ALU.add, q="pool")
        P.dma(V(d["rw_sh" + sfx][t0:t0 + 128, sl]), sh)
        yield
        if g == 1:
            P.tt(kk, sh, kkB, ALU.mult, q="pool")
            P.tt(tA, kk, kk, ALU.mult, q="dve")
            P.reduce(hs, h8(tA), ALU.add)
            P.act(hs, hs, AF.Sqrt)
            P.ts(hs, hs, 1e-12, None, ALU.max)
            P.recip(hs, hs)
            P.tt(h8(kk), h8(kk), bc(hs, [128, 8, 64], 2), ALU.mult)
            P.dma(V(d["rw_kk" + sfx][t0:t0 + 128, :]), kk)
            yield
        if g == 0:
            ti = t0 // 128
            loA = loAR[ti % 2]; loG = loGR[ti % 2]
            for (src, dst, gi) in ((loA, loAs, 0), (loG, loGs, 1)):
                P.ts(dst, src[:, 1:129], mulo[:, gi, 0:1], None, ALU.mult)
                P.stt(dst, src[:, 0:128], mulo[:, gi, 1:2], dst, ALU.mult, ALU.add)
                P.stt(dst, src[:, 2:130], mulo[:, gi, 2:3], dst, ALU.mult, ALU.add)
            P.act(loAs[0:64, :], loAs[0:64, :], AF.Tanh)
            P.act(loGs, loGs, AF.Sigmoid)
            P.dma(V(d["rw_lo" + sfx][:, t0:t0 + 128]), loAs)
            yield
            P.mm(pg, loGs, gup)
            P.copy(gsb, pg, q="act")
            P.dma(V(d["rw_g" + sfx][t0:t0 + 128, :]), gsb)
            yield


def phase_na_pre(C):
    with ExitStack() as es:
        g1 = na_gen(C, es)
        g2 = rwpre_gen(C, es, 1)
        done1 = done2 = False
        while not (done1 and done2):
            if not done1:
                try:
                    next(g1)
                except StopIteration:
                    done1 = True
            for _ in range(2):
                if not done2:
                    try:
                        next(g2)
                    except StopIteration:
                        done2 = True
        C.P.flush()


def phase_rwkv(C):
    P, A, d = C.P, C.A, C.d
    with ExitStack() as es:
        sb = lambda shape, dt=F32: A.sb(es, shape, dt)
        ident = sb([128, 128]); P.dma(ident, V(d["ident"]))
        identb = sb([128, 128], BF16); P.copy(identb, ident)
        pB = sb([128, 6, 512]); P.dma(pB, V(d["rwB"]))
        kkB, kaB, rkB, lngB, lnbB, omkaB = [pB[:, i, :] for i in range(6)]
        P.ts(omkaB, kaB, -1.0, 1.0, ALU.mult, ALU.add, q="pool")
        wup = sb([64, 2, 512]); P.dma(wup, V(d["w_up"]))
        aup = sb([128, 2, 512]); P.dma(aup[64:128], V(d["a_up"]), q="act")
        brow = sb([1, 4, 512]); P.dma(brow, V(d["brow"]), q="act")
        ones1 = sb([1, 128]); P.memset(ones1, 1.0)
        psr = Ring([A.ps(es, [128, 1024]) for _ in range(4)])
        h8 = lambda v: v.re("p (h e) -> p h e", e=64)

        def stream(dr):
            cm = sb([128, 9, 128]); P.dma(cm, V(d["cm"][dr]))
            shR = [sb([128, 1536]) for _ in range(2)]; tA = sb([128, 512]); tB = sb([128, 512]); loR = [sb([128, 128]) for _ in range(2)]
            sw = sb([128, 512]); ad = sb([128, 512]); kkR = [sb([128, 512]) for _ in range(2)]; kd = sb([128, 512]); bt = sb([128, 512])
            hs = sb([128, 8])
            xr = sb([128, 512], BF16); xa = sb([128, 512], BF16); xb = sb([128, 512], BF16); xk = sb([128, 512], BF16)
            BH = sb([128, 512], BF16); KH = sb([128, 512], BF16); vb = sb([128, 512], BF16)
            ART = sb([64, 8, 256], BF16); BWT = sb([64, 8, 128], BF16); KWT = sb([64, 8, 128], BF16)
            Qr = Ring([sb([128, 8, 128], BF16) for _ in range(2)]); Xr = Ring([sb([128, 8, 128], BF16) for _ in range(2)])
            Acc = sb([128, 8, 128]); Accb = sb([128, 8, 128], BF16)
            ArbT = sb([128, 8, 128], BF16); AakT = sb([128, 8, 128], BF16); ArkT = sb([128, 8, 128], BF16)
            WtotB = sb([64, 8, 64]); WmidB = sb([64, 8, 64])
            bon = sb([128, 512])
            ST = [sb([64, 8, 64]) for _ in range(2)]
            S0m = sb([64, 8, 64], BF16); Stmp = sb([64, 8, 64]); Ysb = sb([128, 8, 64]); Usb = sb([128, 8, 64], BF16); Ot = sb([128, 512])
            P.memset(ST[0], 0.0)
            si = 0
            work = []
            for seq, Tn in ((1, C.L), (0, C.T)):
                nt = Tn // 128
                order = range(nt) if dr == 0 else range(nt - 1, -1, -1)
                work += [("_c" if seq else "", ti * 128) for ti in order]

            def issue_loads(wi):
                sfx_, t0_ = work[wi]
                P.dma(shR[wi % 2], V(d["rw_sh" + sfx_][t0_:t0_ + 128, :]), q="sp")
                P.dma(kkR[wi % 2], V(d["rw_kk" + sfx_][t0_:t0_ + 128, :]), q="sp")
                P.dma(loR[wi % 2], V(d["rw_lo" + sfx_][:, t0_:t0_ + 128]), q="sp")

            issue_loads(0)
            for wi, (sfx, t0) in enumerate(work):
                if True:
                    sh = shR[wi % 2]; kk = kkR[wi % 2]; lo = loR[wi % 2]
                    if wi + 1 < len(work):
                        issue_loads(wi + 1)
                    r_, k_, v_ = sh[:, 0:512], sh[:, 512:1024], sh[:, 1024:1536]
                    yield
                    pz = psr.next()
                    P.mm(pz[:, 0:512], ones1, brow[:, dr, :], start=True, stop=False)
                    P.mm(pz[:, 0:512], lo[0:64, :], wup[:, dr, :], start=False, stop=True)
                    P.mm(pz[:, 512:1024], ones1, brow[:, 2 + dr, :], start=True, stop=False)
                    P.mm(pz[:, 512:1024], lo[64:128, :], aup[64:128, dr, :], start=False, stop=True)
                    P.act(sw, pz[:, 0:512], AF.Sigmoid)
                    P.act(ad, pz[:, 512:1024], AF.Sigmoid)
                    P.copy(vb, v_, q="act")
                    yield
                    P.tt(tA, ad, kaB, ALU.mult, q="dve")
                    P.tt(tA, tA, omkaB, ALU.add, q="dve")
                    P.tt(kd, k_, tA, ALU.mult, q="pool")
                    P.tt(bt, kk, ad, ALU.mult, q="pool")
                    yield
                    P.tt(tB, r_, kd, ALU.mult)
                    P.tt(tB, tB, rkB, ALU.mult)
                    P.reduce(hs, h8(tB), ALU.add)
                    P.tt(h8(bon), h8(v_), bc(hs, [128, 8, 64], 2), ALU.mult)
                    P.dma(V(d["rw_b%d" % dr + sfx][t0:t0 + 128, :]), bon, q="pool")
                    yield
                    pl = psr.next(); pl2 = psr.next()
                    P.mm(pl[:, 0:512], cm[:, 0, :], sw)
                    P.mm(pl[:, 512:1024], cm[:, 1, :], sw)
                    P.mm(pl2[:, 0:512], cm[:, 2, :], sw)
                    for h in range(8):
                        P.mm(pl2[0:64, 512 + h * 64:512 + (h + 1) * 64], sw[:, h * 64:(h + 1) * 64], cm[:, 8, 0:64])
                    P.act(tA, pl[:, 0:512], AF.Exp, scale=CDEC)
                    P.tt(xr, r_, tA, ALU.mult, q="pool")
                    P.act(tA, pl[:, 0:512], AF.Exp, scale=-CDEC)
                    P.tt(xb, bt, tA, ALU.mult, q="dve")
                    P.tt(xk, kd, tA, ALU.mult, q="pool")
                    P.act(tB, pl[:, 512:1024], AF.Exp, scale=CDEC)
                    P.tt(xa, kk, tB, ALU.mult, q="pool")
                    P.act(tB, pl2[:, 0:512], AF.Exp, scale=CDEC)
                    P.tt(BH, bt, tB, ALU.mult, q="dve")
                    P.tt(KH, kd, tB, ALU.mult, q="pool")
                    P.act(WtotB.re("p h e -> p (h e)"), pl2[0:64, 512:1024], AF.Exp, scale=CDEC)
                    yield
                    pw = psr.next()
                    for h in range(8):
                        P.mm(pw[0:64, h * 64:(h + 1) * 64], sw[:, h * 64:(h + 1) * 64], cm[:, 7, 0:64])
                    P.act(WmidB.re("p h e -> p (h e)"), pw[0:64, 0:512], AF.Exp, scale=CDEC)
                    yield
                    for (src, dst, off, eng) in ((xa, ART, 0, "dve"), (xr, ART, 128, "act"), (xb, BWT, 0, "dve"), (xk, KWT, 0, "act")):
                        pt = psr.next()
                        ptb = V(pt.ap.bitcast(BF16), pt.key)
                        for h in range(8):
                            P.tr(ptb[0:64, h * 128:(h + 1) * 128], src[:, h * 64:(h + 1) * 64], identb)
                        P.copy(dst[:, :, off:off + 128], ptb[0:64, 0:1024].re("p (h t) -> p h t", t=128), q=eng)
                        yield
                    Q = Qr.next(); X = Xr.next()
                    for hp in range(4):
                        pa = psr.next(); pn = psr.next()
                        for j in range(2):
                            h = hp * 2 + j
                            P.mm(pa[:, j * 256:(j + 1) * 256], BWT[:, h, :], ART[:, h, :])
                            P.mm(pa[:, 512 + j * 256:512 + (j + 1) * 256], KWT[:, h, :], ART[:, h, :])
                            P.mm(pn[:, j * 128:(j + 1) * 128], ART[:, h, 0:128], BWT[:, h, :])
                        hsl = slice(hp * 2, hp * 2 + 2)
                        pav = pa.re("p (a j c t) -> p a j c t", a=2, j=2, c=2)
                        P.tt(Q[:, hsl, :], pav[:, 0, :, 0, :], bc(cm[:, 5, :], [128, 2, 128], 1), ALU.mult)
                        P.tt(ArbT[:, hsl, :], pav[:, 0, :, 1, :], bc(cm[:, 4, :], [128, 2, 128], 1), ALU.mult)
                        P.tt(AakT[:, hsl, :], pav[:, 1, :, 0, :], bc(cm[:, 3, :], [128, 2, 128], 1), ALU.mult)
                        P.tt(ArkT[:, hsl, :], pav[:, 1, :, 1, :], bc(cm[:, 4, :], [128, 2, 128], 1), ALU.mult)
                        P.tt(X[:, hsl, :], pn[:, 0:256].re("p (j t) -> p j t", j=2), bc(cm[:, 6, :], [128, 2, 128], 1), ALU.mult)
                        yield
                    P.tt(Acc, Q, bc(ident, [128, 8, 128], 1), ALU.add, q="pool")
                    P.copy(Accb, Acc, q="act")
                    for lev in range(6):
                        px = psr.next()
                        for h in range(8):
                            P.mm(px[:, h * 128:(h + 1) * 128], Q[:, h, :], X[:, h, :])
                        Xn = Xr.next()
                        P.copy(Xn.re("p h t -> p (h t)"), px, q="act")
                        if lev < 5:
                            pq = psr.next()
                            for h in range(8):
                                P.mm(pq[:, h * 128:(h + 1) * 128], X[:, h, :], Q[:, h, :])
                            Qn = Qr.next()
                            P.copy(Qn.re("p h t -> p (h t)"), pq, q="dve")
                        yield
                        pc = psr.next()
                        for h in range(8):
                            P.mm(pc[:, h * 128:(h + 1) * 128], Xn[:, h, :], Accb[:, h, :])
                        P.tt(Acc.re("p h t -> p (h t)"), Acc.re("p h t -> p (h t)"), pc, ALU.add)
                        if lev < 5:
                            P.copy(Accb, Acc, q="act")
                        X = Xn
                        if lev < 5:
                            Q = Qn
                        yield
                    S_in = ST[si % 2]; S_out = ST[(si + 1) % 2]
                    si += 1
                    P.tt(S0m, S_in, WmidB, ALU.mult)
                    P.tt(Stmp, S_in, WtotB, ALU.mult, q="pool")
                    py = psr.next()
                    for h in range(8):
                        P.mm(py[:, h * 64:(h + 1) * 64], ART[:, h, 0:128], S0m[:, h, :], start=True, stop=False)
                        P.mm(py[:, h * 64:(h + 1) * 64], AakT[:, h, :], vb[:, h * 64:(h + 1) * 64], start=False, stop=True)
                    P.copy(Ysb.re("p h e -> p (h e)"), py[:, 0:512], q="act")
                    yield
                    pu = psr.next()
                    for h in range(8):
                        P.mm(pu[:, h * 64:(h + 1) * 64], Acc[:, h, :], Ysb[:, h, :])
                    P.ts(Usb.re("p h e -> p (h e)"), pu[:, 0:512], -1.0, None, ALU.mult)
                    yield
                    pss = psr.next()
                    for h in range(8):
                        P.mm(pss[0:64, h * 64:(h + 1) * 64], BH[:, h * 64:(h + 1) * 64], Usb[:, h, :], start=True, stop=False)
                        P.mm(pss[0:64, h * 64:(h + 1) * 64], KH[:, h * 64:(h + 1) * 64], vb[:, h * 64:(h + 1) * 64], start=False, stop=True)
                    P.tt(S_out.re("p h e -> p (h e)"), Stmp.re("p h e -> p (h e)"), pss[0:64, 0:512], ALU.add)
                    po = psr.next()
                    for h in range(8):
                        P.mm(po[:, h * 64:(h + 1) * 64], ART[:, h, 128:256], S0m[:, h, :], start=True, stop=False)
                        P.mm(po[:, h * 64:(h + 1) * 64], ArbT[:, h, :], Usb[:, h, :], start=False, stop=False)
                        P.mm(po[:, h * 64:(h + 1) * 64], ArkT[:, h, :], vb[:, h * 64:(h + 1) * 64], start=False, stop=True)
                    P.copy(Ot, po[:, 0:512], q="act")
                    P.dma(V(d["rw_o%d" % dr + sfx][t0:t0 + 128, :]), Ot, q="pool")
                    yield

        interleave([stream(0), stream(1)])
        P.flush()


def phase_rwkv_post(C):
    P, A, d = C.P, C.A, C.d
    with ExitStack() as es:
        sb = lambda shape, dt=F32: A.sb(es, shape, dt)
        pB = sb([128, 6, 512]); P.dma(pB, V(d["rwB"]))
        lngB = pB[:, 3, :]; lnbB = pB[:, 4, :]
        NB = 2
        R = lambda: Ring([sb([128, 512]) for _ in range(NB)])
        o0R, o1R, b0R, b1R, gR, resR, cenR = R(), R(), R(), R(), R(), R(), R()
        hsR = Ring([sb([128, 8, 2]) for _ in range(NB)])
        h8 = lambda v: v.re("p (h e) -> p h e", e=64)
        for seq, Tn in ((1, C.L), (0, C.T)):
            sfx = "_c" if seq else ""
            for t0 in range(0, Tn, 128):
                o0, o1, b0, b1, g_, res, cen, hs = (o0R.next(), o1R.next(), b0R.next(), b1R.next(), gR.next(),
                                                     resR.next(), cenR.next(), hsR.next())
                P.dma(o0, V(d["rw_o0" + sfx][t0:t0 + 128, :]))
                P.dma(o1, V(d["rw_o1" + sfx][t0:t0 + 128, :]), q="act")
                P.dma(b0, V(d["rw_b0" + sfx][t0:t0 + 128, :]))
                P.dma(b1, V(d["rw_b1" + sfx][t0:t0 + 128, :]), q="act")
                P.dma(g_, V(d["rw_g" + sfx][t0:t0 + 128, :]))
                P.tt(res, o0, o1, ALU.add, q="pool")
                P.tt(b0, b0, b1, ALU.add, q="pool")
                P.reduce(hs[:, :, 0], h8(res), ALU.add)
                P.ts(hs[:, :, 0], hs[:, :, 0], 1.0 / 64, None, ALU.mult)
                P.tt(h8(cen), h8(res), bc(hs[:, :, 0], [128, 8, 64], 2), ALU.subtract)
                P.tt(res, cen, cen, ALU.mult, q="pool")
                P.reduce(hs[:, :, 1], h8(res), ALU.add)
                P.ts(hs[:, :, 1], hs[:, :, 1], 1.0 / 64, 64e-5, ALU.mult, ALU.add)
                P.act(hs[:, :, 1], hs[:, :, 1], AF.Sqrt)
                P.recip(hs[:, :, 1], hs[:, :, 1])
                P.tt(h8(cen), h8(cen), bc(hs[:, :, 1], [128, 8, 64], 2), ALU.mult)
                P.tt(cen, cen, lngB, ALU.mult, q="pool")
                P.tt(cen, cen, lnbB, ALU.add, q="dve")
                P.tt(cen, cen, b0, ALU.add, q="pool")
                P.tt(res, cen, g_, ALU.mult, q="dve")
                P.dma(V(d["mix0" + sfx][t0:t0 + 128, 0:512]), res, q="pool")
        P.flush()


def phase_outproj(C, l):
    P, A, d = C.P, C.A, C.d
    with ExitStack() as es:
        W = A.sb(es, [128, 8, 1024], BF16)
        sring = Ring([A.sb(es, [128, 8, 512]) for _ in range(2)])
        load_w_bf16(C, es, W, d["w_out%d" % l], 1024, sring)
        ident = A.sb(es, [128, 128]); P.dma(ident, V(d["ident"]))
        gB = A.sb(es, [128, 1024])
        mring = Ring([A.sb(es, [128, 1024]) for _ in range(2)])
        hring = Ring([A.sb(es, [128, 1024]) for _ in range(2)])
        oring = Ring([A.sb(es, [128, 1024]) for _ in range(2)])
        mT = Ring([A.sb(es, [128, 8, 128], BF16) for _ in range(2)])
        ptr = Ring([A.ps(es, [128, 8, 128]) for _ in range(2)])
        pyr = Ring([A.ps(es, [128, 512]) for _ in range(4)])
        seqs = [(0, C.T, "")] + ([(1, C.L, "_c")] if l == 0 else [])
        for s, Tn, sfx in seqs:
            hin = d["x" if s == 0 else "ctx"] if l == 0 else d["h1" + sfx]
            P.dma(gB, V(d["gateB"][l, s, 0]))
            for t0 in range(0, Tn, 128):
                mt = mring.next(); ht = hring.next(); ot = oring.next(); mt_T = mT.next(); pt = ptr.next()
                P.dma(mt, V(d["mix%d" % l + sfx][t0:t0 + 128, :]))
                P.dma(ht, V(hin[t0:t0 + 128, :]), q="act")
                for c in range(8):
                    P.tr(pt[:, c, :], mt[:, c * 128:(c + 1) * 128], ident)
                P.copy(mt_T[:, 0:4, :], pt[:, 0:4, :], q="dve")
                P.copy(mt_T[:, 4:8, :], pt[:, 4:8, :], q="act")
                for half in range(2):
                    py = pyr.next()
                    sl = slice(half * 512, (half + 1) * 512)
                    for kc in range(8):
                        P.mm(py, mt_T[:, kc, :], W[:, kc, sl], start=(kc == 0), stop=(kc == 7))
                    P.tt(ot[:, sl], py, gB[:, sl], ALU.mult)
                    P.tt(ot[:, sl], ot[:, sl], ht[:, sl], ALU.add, q="pool")
                P.dma(V(d["hmid%d" % l + sfx][t0:t0 + 128, :]), ot, q="pool")
        P.flush()


def phase_mlp(C, l):
    P, A, d = C.P, C.A, C.d
    last = (l == 1)
    with ExitStack() as es:
        W1 = A.sb(es, [128, 8, 4096], BF16)
        W2 = A.sb(es, [128, 32, 1024], BF16)
        with ExitStack() as es2:
            sring = Ring([A.sb(es2, [128, 8, 256]) for _ in range(2)])
            load_w_bf16(C, es2, W1, d["mlp_w1"][l], 4096, sring, blk=256)
            w2v = d["mlp_w2"][l].rearrange("(c p) n -> p c n", p=128)
            for i in range(16):
                st = sring.next()
                stv = st.re("p c n -> p (c n)").re("p (c n) -> p c n", c=2)
                P.dma(stv, V(w2v[:, 2 * i:2 * i + 2, :]), q=("sp" if i % 2 == 0 else "act"))
                for j in range(2):
                    P.copy(W2[:, 2 * i + j, :], stv[:, j, :], q=("dve", "pool", "act")[(2 * i + j) % 3])
            P.flush()
        with ExitStack() as es2:
            sb = lambda shape, dt=F32: A.sb(es2, shape, dt)
            ident = sb([128, 128]); P.dma(ident, V(d["ident"]))
            affall = sb([128, 2, 2, 4, 8]); P.dma(affall, V(d["aff"]))
            gB = sb([128, 1024]); fnB = sb([128, 1024])
            if last:
                P.dma(fnB, V(d["fnormB"]))
            xring = Ring([sb([128, 1024]) for _ in range(3)])
            xnring = Ring([sb([128, 1024]) for _ in range(3)])
            junk = sb([128, 1024], BF16)
            smring = Ring([sb([128, 2]) for _ in range(6)])
            aring = Ring([sb([128, 8, 128], BF16) for _ in range(2)])
            hTr = Ring([sb([128, 32, 128], BF16) for _ in range(2)])
            rl = Ring([sb([128, 512]) for _ in range(2)])
            ptr = Ring([A.ps(es2, [128, 8, 128]) for _ in range(1)])
            phr = Ring([A.ps(es2, [128, 512]) for _ in range(2)])
            pyr = Ring([A.ps(es2, [128, 512]) for _ in range(2)])
            seqs = [(0, C.T, "")] + ([(1, C.L, "_c")] if l == 0 else [])
            for s, Tn, sfx in seqs:
                affv = affall[:, l, s]
                P.dma(gB, V(d["gateB"][l, s, 1]))
                tiles = list(range(0, Tn, 128))
                st = {}

                def prep_a(t0):
                    xt = xring.next(); xn = xnring.next()
                    P.dma(xt, V(d["hmid%d" % l + sfx][t0:t0 + 128, :]))
                    norm_a(C, xt, smring.next(), junk, xn)
                    st[t0] = [xt, xn, None]

                def prep_b(t0):
                    aT = aring.next()
                    norm_b(C, st[t0][1], aT, affv, 1, ident, ptr.next())
                    st[t0][2] = aT

                prep_a(tiles[0]); prep_b(tiles[0])
                for ti, t0 in enumerate(tiles):
                    xt, xn, aT = st.pop(t0)
                    hT = hTr.next()
                    nxt_t = tiles[ti + 1] if ti + 1 < len(tiles) else None
                    if nxt_t is not None:
                        prep_a(nxt_t)
                    for hq in range(8):
                        ph = phr.next()
                        for j in range(4):
                            hc = hq * 4 + j
                            for kc in range(8):
                                P.mm(ph[:, j * 128:(j + 1) * 128], W1[:, kc, hc * 128:(hc + 1) * 128], aT[:, kc, :],
                                     start=(kc == 0), stop=(kc == 7))
                        r = rl.next()
                        P.act(r, ph, AF.Relu)
                        P.tt(hT[:, hq * 4:(hq + 1) * 4, :].re("p c n -> p (c n)"), r, r, ALU.mult, q=("pool" if hq % 2 == 0 else "dve"))
                    if nxt_t is not None:
                        prep_b(nxt_t)
                    for half in range(2):
                        py = pyr.next()
                        sl = slice(half * 512, (half + 1) * 512)
                        for hc in range(32):
                            P.mm(py, hT[:, hc, :], W2[:, hc, sl], start=(hc == 0), stop=(hc == 31))
                        P.tt(xn[:, sl], py, gB[:, sl], ALU.mult)
                        P.tt(xn[:, sl], xn[:, sl], xt[:, sl], ALU.add, q="pool")
                    if not last:
                        P.dma(V(d["h1" + sfx][t0:t0 + 128, :]), xn, q="pool")
                    else:
                        sm = smring.next()
                        ss = sm[:, 0:1]; rs = sm[:, 1:2]
                        P.memset(ss, 0.0)
                        P.act(junk, xn, AF.Square, accum_out=ss)
                        P.ts(rs, ss, 1.0 / 1024, 1e-6, ALU.mult, ALU.add)
                        P.act(rs, rs, AF.Sqrt)
                        P.recip(rs, rs)
                        P.stt(xn, xn, rs, fnB, ALU.mult, ALU.mult)
                        P.dma(V(d["out"][t0:t0 + 128, :]), xn, q="pool")
            P.flush()


def phase_inproj1(C):
    P, A, d = C.P, C.A, C.d
    with ExitStack() as es:
        W = A.sb(es, [128, 8, 5120], BF16)
        sring = Ring([A.sb(es, [128, 8, 256]) for _ in range(2)])
        load_w_bf16(C, es, W, d["w_in1"], 5120, sring, blk=256)
        ident = A.sb(es, [128, 128]); P.dma(ident, V(d["ident"]))
        affall = A.sb(es, [128, 2, 2, 4, 8]); P.dma(affall, V(d["aff"]))
        xring = Ring([A.sb(es, [128, 1024]) for _ in range(3)])
        xnring = Ring([A.sb(es, [128, 1024]) for _ in range(3)])
        junk = A.sb(es, [128, 1024], BF16)
        smring = Ring([A.sb(es, [128, 2]) for _ in range(6)])
        aring = Ring([A.sb(es, [128, 8, 128], BF16) for _ in range(2)])
        ptr = Ring([A.ps(es, [128, 8, 128]) for _ in range(1)])
        pfr = Ring([A.ps(es, [128, 512]) for _ in range(4)])
        st32 = Ring([A.sb(es, [128, 512]) for _ in range(4)])
        for s, (Tn, sfx) in enumerate(((C.T, ""), (C.L, "_c"))):
            affv = affall[:, 1, s]
            tiles = list(range(0, Tn, 128))
            st = {}

            def prep_a(t0):
                xt = xring.next(); xn = xnring.next()
                P.dma(xt, V(d["h1" + sfx][t0:t0 + 128, :]))
                norm_a(C, xt, smring.next(), junk, xn)
                st[t0] = xn

            def prep_b(t0):
                aT = aring.next()
                norm_b(C, st[t0], aT, affv, 0, ident, ptr.next())
                st[t0] = aT

            prep_a(tiles[0]); prep_b(tiles[0])
            for ti, t0 in enumerate(tiles):
                aT = st.pop(t0)
                nxt_t = tiles[ti + 1] if ti + 1 < len(tiles) else None
                if nxt_t is not None:
                    prep_a(nxt_t)
                for g in range(10):
                    if g == 5 and nxt_t is not None:
                        prep_b(nxt_t)
                    if s == 1 and (g < 2 or g >= 8):
                        continue
                    pf = pfr.next()
                    for kc in range(8):
                        P.mm(pf, aT[:, kc, :], W[:, kc, g * 512:(g + 1) * 512], start=(kc == 0), stop=(kc == 7))
                    stg = st32.next()
                    P.copy(stg, pf, q=("dve" if g % 2 == 0 else "act"))
                    P.dma(V(d["p1" + sfx][t0:t0 + 128, g * 512:(g + 1) * 512]), stg, q="pool")
        P.flush()


def hg_consts():
    u = np.arange(128)[:, None]; t = np.arange(128)[None, :]
    same = (u // 64) == (t // 64)
    cm = np.zeros((2, 128, 3, 128), np.float32)
    for dr in range(2):
        if dr == 0:
            befeq = (u <= t); aft = (u > t)
        else:
            befeq = (u >= t); aft = (u < t)
        cm[dr, :, 0, :] = same & befeq
        cm[dr, :, 1, :] = same & aft
        cm[dr, :, 2, :] = same & befeq
    return cm


def phase_hgrn(C):
    P, A, d = C.P, C.A, C.d
    with ExitStack() as es:
        sb = lambda shape, dt=F32: A.sb(es, shape, dt)
        ident = sb([128, 128]); P.dma(ident, V(d["ident"]))
        identb = sb([128, 128], BF16); P.copy(identb, ident)
        ones = sb([128, 128]); P.memset(ones, 1.0)
        lbB = sb([128, 1024]); omlbB = sb([128, 1024])
        with ExitStack() as es2:
            hgl = A.sb(es2, [128, 2, 1024]); P.dma(hgl, V(d["hglB"]))
            P.tt(lbB, hgl[:, 1, :], hgl[:, 0, :], ALU.subtract)
            P.act(lbB, lbB, AF.Sigmoid)
            P.ts(omlbB, lbB, -1.0, 1.0, ALU.mult, ALU.add, q="pool")
            P.flush()
        psr = Ring([A.ps(es, [128, 1024]) for _ in range(4)])
        fl = lambda v: v.re("p h e -> p (h e)")

        def stream(dr):
            cm = sb([128, 3, 128]); P.dma(cm, V(d["cmh"][dr]))
            pqR = [sb([128, 1024]) for _ in range(2)]; pfR = [sb([128, 1024]) for _ in range(2)]; piR = [sb([128, 1024]) for _ in range(2)]
            fg = sb([128, 1024]); gl = sb([128, 1024]); kx = sb([128, 1024]); ex = sb([128, 1024])
            qt_ = sb([128, 1024], BF16); kt_ = sb([128, 1024], BF16); kh = sb([128, 1024], BF16); pib = sb([128, 1024], BF16)
            QT = sb([128, 8, 128], BF16); KT = sb([128, 8, 128], BF16); QT0 = sb([128, 8, 128], BF16); QT1 = sb([128, 8, 128], BF16)
            attT = sb([128, 8, 128], BF16)
            Sbf = [sb([128, 8, 128], BF16) for _ in range(2)]
            P.memset(QT0, 0.0); P.memset(QT1, 0.0); P.memset(attT, 0.0)
            Wt = [sb([128, 8, 128]) for _ in range(2)]
            S = [sb([128, 8, 128]) for _ in range(3)]
            Stmp = sb([128, 8, 128]); Ot = sb([128, 1024])
            P.memset(S[0], 0.0)
            si = 0
            work = []
            for seq, Tn in ((1, C.L), (0, C.T)):
                nt = Tn // 128
                tiles = range(nt) if dr == 0 else range(nt - 1, -1, -1)
                work += [(seq, ti * 128) for ti in tiles]
            c0 = 1024 + dr * 1024

            def issue_loads(wi):
                seq_, t0_ = work[wi]
                p1_ = d["p1" + ("_c" if seq_ else "")]
                if seq_ == 0:
                    P.dma(pqR[wi % 2], V(p1_[t0_:t0_ + 128, 0:1024]), q="sp")
                P.dma(pfR[wi % 2], V(p1_[t0_:t0_ + 128, c0:c0 + 1024]), q="sp")
                P.dma(piR[wi % 2], V(p1_[t0_:t0_ + 128, 3072:4096]), q="sp")

            issue_loads(0)
            for wi, (seq, t0) in enumerate(work):
                if True:
                    want_o = (seq == 0)
                    pq = pqR[wi % 2]; pf = pfR[wi % 2]; pi = piR[wi % 2]
                    Sl = [S[si % 3], S[(si + 1) % 3], S[(si + 2) % 3]]
                    si += 2
                    if wi + 1 < len(work):
                        issue_loads(wi + 1)
                    yield
                    P.copy(pib, pi, q="act")
                    P.act(fg, pf, AF.Sigmoid)
                    P.tt(fg, fg, omlbB, ALU.mult, q="pool")
                    P.tt(fg, fg, lbB, ALU.add, q="dve")
                    yield
                    P.act(gl, fg, AF.Ln)
                    P.ts(kx, fg, -1.0, 1.0, ALU.mult, ALU.add, q="pool")
                    yield
                    pl = psr.next(); pd = psr.next()
                    for hf in range(2):
                        sl = slice(hf * 512, (hf + 1) * 512)
                        if want_o:
                            P.mm(pl[:, sl], cm[:, 0, :], gl[:, sl])
                        P.mm(pd[:, sl], cm[:, 1, :], gl[:, sl])
                    P.act(ex, pd, AF.Exp)
                    P.tt(kh, kx, ex, ALU.mult, q="dve")
                    if want_o:
                        P.act(fg, pq, AF.Silu)
                        P.act(ex, pl, AF.Exp)
                        P.tt(qt_, fg, ex, ALU.mult, q="pool")
                        P.act(ex, pl, AF.Exp, scale=-1.0)
                        P.tt(kt_, kx, ex, ALU.mult, q="pool")
                    yield
                    order = (0, 1) if dr == 0 else (1, 0)
                    for c in order:
                        tsl = slice(c * 64, (c + 1) * 64)
                        pw = psr.next()
                        for h in range(8):
                            hs = slice(h * 128, (h + 1) * 128)
                            P.mm(pw[:, hs], gl[tsl, hs], ones[tsl, :])
                        P.act(fl(Wt[c]), pw, AF.Exp)
                        yield
                    if want_o:
                        for (src, dst, eng) in ((qt_, QT, "dve"), (kt_, KT, "act")):
                            pt = psr.next()
                            ptb = V(pt.ap.bitcast(BF16), pt.key)
                            for h in range(8):
                                P.tr(ptb[:, h * 128:(h + 1) * 128], src[:, h * 128:(h + 1) * 128], identb)
                            P.copy(fl(dst), ptb[:, 0:1024], q=eng)
                            yield
                        P.copy(QT0[:, :, 0:64], QT[:, :, 0:64], q="dve")
                        P.copy(QT1[:, :, 64:128], QT[:, :, 64:128], q="pool")
                        for c in (0, 1):
                            tsl = slice(c * 64, (c + 1) * 64)
                            pa = psr.next()
                            for h in range(8):
                                P.mm(pa[:, h * 64:(h + 1) * 64], KT[:, h, :], QT[:, h, tsl])
                            P.tt(attT[tsl, :, tsl], pa[tsl, 0:512].re("p (h t) -> p h t", t=64),
                                 bc(cm[tsl, 2, tsl], [64, 8, 64], 1), ALU.mult)
                            yield
                    QTc = [QT0, QT1]
                    for i, c in enumerate(order):
                        tsl = slice(c * 64, (c + 1) * 64)
                        pk = psr.next()
                        for h in range(8):
                            hs = slice(h * 128, (h + 1) * 128)
                            P.mm(pk[:, hs], kh[tsl, hs], pib[tsl, hs])
                        if want_o:
                            P.copy(Sbf[i], Sl[i], q="act")
                        P.tt(Stmp, Sl[i], Wt[c], ALU.mult, q=("dve" if i == 0 else "pool"))
                        P.tt(fl(Sl[i + 1]), fl(Stmp), pk, ALU.add)
                        yield
                    if want_o:
                        po = psr.next()
                        for h in range(8):
                            hs = slice(h * 128, (h + 1) * 128)
                            P.mm(po[:, hs], QTc[order[0]][:, h, :], Sbf[0][:, h, :], start=True, stop=False)
                            P.mm(po[:, hs], attT[:, h, :], pib[:, hs], start=False, stop=False)
                            P.mm(po[:, hs], QTc[order[1]][:, h, :], Sbf[1][:, h, :], start=False, stop=True)
                        P.copy(Ot, po, q="act")
                        P.dma(V(d["hg_o%d" % dr][t0:t0 + 128, :]), Ot, q="pool")
                        yield

        interleave([stream(0), stream(1)])
        P.flush()


def phase_hgrn_post(C):
    P, A, d = C.P, C.A, C.d
    with ExitStack() as es:
        sb = lambda shape, dt=F32: A.sb(es, shape, dt)
        hgnB = sb([128, 1024]); P.dma(hgnB, V(d["hgnB"]))
        NB = 2
        R = lambda: Ring([sb([128, 1024]) for _ in range(NB)])
        o0R, o1R, gR, jR = R(), R(), R(), R()
        smR = Ring([sb([128, 2]) for _ in range(4)])
        for t0 in range(0, C.T, 128):
            o0, o1, pg, junk, sm = o0R.next(), o1R.next(), gR.next(), jR.next(), smR.next()
            P.dma(o0, V(d["hg_o0"][t0:t0 + 128, :]))
            P.dma(o1, V(d["hg_o1"][t0:t0 + 128, :]), q="act")
            P.dma(pg, V(d["p1"][t0:t0 + 128, 4096:5120]))
            P.tt(o0, o0, o1, ALU.add, q="pool")
            ss = sm[:, 0:1]; rs = sm[:, 1:2]
            P.memset(ss, 0.0)
            P.act(junk, o0, AF.Square, accum_out=ss)
            P.ts(rs, ss, 1.0 / 1024, 1e-6, ALU.mult, ALU.add)
            P.act(rs, rs, AF.Sqrt)
            P.recip(rs, rs)
            P.stt(o0, o0, rs, hgnB, ALU.mult, ALU.mult)
            P.act(pg, pg, AF.Silu)
            P.tt(o1, o0, pg, ALU.mult, q="pool")
            P.dma(V(d["mix1"][t0:t0 + 128, :]), o1, q="pool")
        P.flush()


ALL_PHASES = None


def all_phases():
    return [phase_consts, phase_inproj0, phase_na_pre, phase_rwkv, phase_rwkv_post,
            lambda C: phase_outproj(C, 0), lambda C: phase_mlp(C, 0), phase_inproj1, phase_hgrn, phase_hgrn_post,
            lambda C: phase_outproj(C, 1), lambda C: phase_mlp(C, 1)]


REAL_RANKS = [0, 1, 4, 5]


def run(inputs, T, L, nb, dbg=()):
    nc, C = build(T, L, all_phases(), dbg=dbg)
    maps = [prep_inputs(inputs, b, T, L) for b in range(nb)]
    if nb == 4:
        zero = {k: np.zeros_like(v) for k, v in maps[0].items()}
        full = [zero] * 8
        full = list(full)
        for b, rk in enumerate(REAL_RANKS):
            full[rk] = maps[b]
        res = run_bass_kernel_spmd(nc, full, core_ids=list(range(8)))
        res.results = [res.results[rk] for rk in REAL_RANKS]
        return res
    res = run_bass_kernel_spmd(nc, maps, core_ids=list(range(nb)))
    return res


def kernel(**inputs):
    T = inputs["x"].shape[1]; L = inputs["ctx"].shape[1]; B = inputs["x"].shape[0]
    res = run(inputs, T, L, B)
    return np.stack([np.asarray(r["out"], np.float32) for r in res.results[:B]], axis=0)
```

```python
import numpy as np
from contextlib import ExitStack
import concourse.bass as bass
import concourse.mybir as mybir
from concourse.bass_utils import run_bass_kernel_spmd

F32 = mybir.dt.float32
BF16 = mybir.dt.bfloat16
AF = mybir.ActivationFunctionType
ALU = mybir.AluOpType
AX = mybir.AxisListType

QUEUES = ["pe", "act", "dve", "pool", "sp"]
COMPUTE = ["pe", "act", "dve", "pool"]
NS_DMA = 8


class V:
    __slots__ = ("ap", "key")

    def __init__(self, ap, key=None):
        self.ap = ap
        self.key = key

    def __getitem__(self, idx):
        return V(self.ap[idx], self.key)

    def k(self, key):
        return V(self.ap, key)

    def re(self, pat, **kw):
        return V(self.ap.rearrange(pat, **kw), self.key)


class Op:
    __slots__ = ("q", "fn", "deps", "dma", "marked", "cnt", "sem_i", "semval")


def _ap(x):
    return x.ap if isinstance(x, V) else x


class Prog:
    def __init__(self, nc):
        self.nc = nc
        self.phase = 0
        self.total = 0
        self.es = ExitStack()
        self.csem = {q: self.es.enter_context(nc.semaphore(f"c_{q}")) for q in COMPUTE}
        self.dsem = {q: [self.es.enter_context(nc.semaphore(f"d_{q}{i}")) for i in range(NS_DMA)] for q in QUEUES}
        self.cbase = {q: 0 for q in COMPUTE}
        self.dbase = {q: 0 for q in QUEUES}
        self._reset()

    def close(self):
        self.es.close()

    def _reset(self):
        self.q = {e: [] for e in QUEUES}
        self.last_w = {}
        self.readers = {}

    def add(self, q, fn, reads=(), writes=(), dma=False):
        op = Op()
        op.q = q; op.fn = fn; op.dma = dma; op.marked = False; op.cnt = 0
        deps = {}
        rk = [x.key for x in reads if isinstance(x, V) and x.key is not None]
        wk = [x.key for x in writes if isinstance(x, V) and x.key is not None]
        for k in rk:
            w = self.last_w.get(k)
            if w is not None:
                deps[w] = True
        for k in wk:
            w = self.last_w.get(k)
            if w is not None and w not in deps:
                deps[w] = False
            for r in self.readers.get(k, ()):
                if r not in deps:
                    deps[r] = False
        deps.pop(op, None)
        for k in rk:
            self.readers.setdefault(k, []).append(op)
        for k in wk:
            self.last_w[k] = op
            self.readers[k] = []
        op.deps = deps
        self.q[q].append(op)
        self.total += 1
        return op

    def flush(self):
        nc = self.nc
        self.phase += 1
        ph = self.phase
        for q in QUEUES:
            for op in self.q[q]:
                need = []
                for d, raw in op.deps.items():
                    if d.dma or op.dma:
                        need.append(d)
                    elif d.q != op.q:
                        need.append(d)
                    elif raw and op.q != "pe":
                        need.append(d)
                op.deps = need
                for d in need:
                    d.marked = True
        for q in COMPUTE:
            comp = [o for o in self.q[q] if not o.dma]
            if comp:
                comp[-1].marked = True
        fin = {}
        for q in COMPUTE:
            c = self.cbase[q]
            for o in self.q[q]:
                if not o.dma and o.marked:
                    c += 1
                    o.cnt = c
            fin[q] = c
            self.cbase[q] = c
        dfin = {}
        for q in QUEUES:
            i = self.dbase[q]
            for o in self.q[q]:
                if o.dma:
                    o.sem_i = i % NS_DMA
                    o.semval = 16 * (i // NS_DMA + 1)
                    i += 1
            dfin[q] = i
            self.dbase[q] = i
        csem = self.csem
        dsem = self.dsem
        with ExitStack() as es:
            block = es.enter_context(nc.Block())
            bname = {"pe": "tensor", "act": "scalar", "dve": "vector", "pool": "gpsimd", "sp": "sync"}
            for q in QUEUES:
                ops = self.q[q]

                def body(eng, q=q, ops=ops):
                    known = {}

                    def wait(sem, val):
                        kk = id(sem)
                        if known.get(kk, 0) < val:
                            eng.wait_ge(sem, val)
                            known[kk] = val
                    for op in ops:
                        for d in op.deps:
                            if d.dma:
                                wait(dsem[d.q][d.sem_i], d.semval)
                            else:
                                wait(csem[d.q], d.cnt)
                        if op.dma:
                            if op.semval > 16:
                                wait(dsem[q][op.sem_i], op.semval - 16)
                            ins = op.fn(eng)
                            ins.then_inc(dsem[q][op.sem_i], 16)
                        else:
                            ins = op.fn(eng)
                            if op.marked:
                                ins.then_inc(csem[q], 1)
                    for q2 in COMPUTE:
                        if fin[q2] > 0:
                            wait(csem[q2], fin[q2])
                    for q2 in QUEUES:
                        n = dfin[q2]
                        for i in range(min(n, NS_DMA)):
                            cntv = (n - i + NS_DMA - 1) // NS_DMA
                            wait(dsem[q2][i], 16 * cntv)
                getattr(block, bname[q])(body)
        self._reset()

    def dma(self, out, in_, q="sp", **kw):
        o, i = _ap(out), _ap(in_)
        return self.add(q, lambda e: e.dma_start(out=o, in_=i, **kw), reads=[in_], writes=[out], dma=True)

    def mm(self, out, lhsT, rhs, start=True, stop=True):
        o, l, r = _ap(out), _ap(lhsT), _ap(rhs)
        rd = [lhsT, rhs]
        if not start:
            rd.append(out)
        return self.add("pe", lambda e: e.matmul(o, l, r, start=start, stop=stop), reads=rd, writes=[out])

    def tr(self, out, in_, ident):
        o, i, d = _ap(out), _ap(in_), _ap(ident)
        return self.add("pe", lambda e: e.transpose(o, i, d), reads=[in_, ident], writes=[out])

    def act(self, out, in_, func, bias=None, scale=None, accum_out=None, q="act"):
        o, i = _ap(out), _ap(in_)
        kw = {}
        rd = [in_]
        wr = [out]
        if bias is not None:
            kw["bias"] = _ap(bias); rd.append(bias)
        if scale is not None:
            kw["scale"] = _ap(scale); rd.append(scale)
        if accum_out is not None:
            kw["accum_out"] = _ap(accum_out); wr.append(accum_out)
        return self.add(q, lambda e: e.activation(o, i, func, **kw), reads=rd, writes=wr)

    def tt(self, out, in0, in1, op, q="dve"):
        o, a, b = _ap(out), _ap(in0), _ap(in1)
        return self.add(q, lambda e: e.tensor_tensor(o, a, b, op), reads=[in0, in1], writes=[out])

    def ts(self, out, in0, s1, s2, op0, op1=None, accum_out=None, q="dve"):
        o, a = _ap(out), _ap(in0)
        rd = [in0, s1, s2]
        wr = [out]
        kw = {}
        if op1 is not None:
            kw["op1"] = op1
        if accum_out is not None:
            kw["accum_out"] = _ap(accum_out); wr.append(accum_out)
        return self.add(q, lambda e: e.tensor_scalar(o, a, _ap(s1), _ap(s2), op0, **kw), reads=rd, writes=wr)

    def stt(self, out, in0, scalar, in1, op0, op1, q="dve"):
        o, a, b = _ap(out), _ap(in0), _ap(in1)
        return self.add(q, lambda e: e.scalar_tensor_tensor(o, a, _ap(scalar), b, op0, op1),
                        reads=[in0, scalar, in1], writes=[out])

    def copy(self, out, in_, q="dve"):
        o, i = _ap(out), _ap(in_)
        if q == "act":
            return self.add(q, lambda e: e.copy(o, i), reads=[in_], writes=[out])
        return self.add(q, lambda e: e.tensor_copy(o, i), reads=[in_], writes=[out])

    def memset(self, out, val, q="pool"):
        o = _ap(out)
        return self.add(q, lambda e: e.memset(o, val), reads=[], writes=[out])

    def reduce(self, out, in_, op, axis=None, q="dve"):
        o, i = _ap(out), _ap(in_)
        ax = axis if axis is not None else AX.X
        return self.add(q, lambda e: e.tensor_reduce(o, i, ax, op), reads=[in_], writes=[out])

    def recip(self, out, in_):
        o, i = _ap(out), _ap(in_)
        return self.add("dve", lambda e: e.reciprocal(o, i), reads=[in_], writes=[out])


class Alloc:
    def __init__(self, nc):
        self.nc = nc
        self.n = 0

    def sb(self, es, shape, dt=F32, name=None):
        self.n += 1
        nm = f"{name or 't'}_{self.n}"
        t = es.enter_context(self.nc.sbuf_tensor(nm, list(shape), dt))
        return V(t[:], nm)

    def ps(self, es, shape, dt=F32, name=None):
        self.n += 1
        nm = f"{name or 'p'}_{self.n}"
        t = es.enter_context(self.nc.psum_tensor(nm, list(shape), dt))
        return V(t[:], nm)


class Ring:
    def __init__(self, items):
        self.items = items
        self.i = 0

    def next(self):
        x = self.items[self.i % len(self.items)]
        self.i += 1
        return x


D = 1024
KC = 8
NA_MASK = -30000.0


class Cx:
    pass


def load_w_bf16(C, es_outer, dst, src_ap, ncols, stage_ring, blk=512):
    P = C.P
    srcv = src_ap.rearrange("(c p) n -> p c n", p=128)
    i = 0
    for c0 in range(0, ncols, blk):
        n = min(blk, ncols - c0)
        st = stage_ring.next()
        P.dma(st[:, :, 0:n], V(srcv[:, :, c0:c0 + n]), q=("sp" if i % 2 == 0 else "act"))
        for kc in range(8):
            P.copy(dst[:, kc, c0:c0 + n], st[:, kc, 0:n], q=("dve", "pool", "act")[(i * 8 + kc) % 3])
        i += 1


def phase_consts(C):
    P, A, d = C.P, C.A, C.d
    with ExitStack() as es:
        cT = A.sb(es, [128, 8, 2]); sc = A.sb(es, [128, 8, 2])
        P.dma(cT, V(d["cT"]))
        P.act(sc, cT, AF.Silu)
        adab = A.sb(es, [128, 2, 48, 2]); P.dma(adab, V(d["adabT"]))
        gm = A.sb(es, [128, 2, 2, 8])
        P.dma(gm[:, :, 0, :], V(d["gmixT"])); P.dma(gm[:, :, 1, :], V(d["gmlpT"]))
        ident = A.sb(es, [128, 128]); P.dma(ident, V(d["ident"]))
        ones = A.sb(es, [128, 128]); P.memset(ones, 1.0)
        mT = A.sb(es, [128, 2, 48, 2])
        ring = Ring([A.sb(es, [128, 8, 512]) for _ in range(2)])
        pm = A.ps(es, [128, 4, 2])
        for l in range(2):
            wv = d["adaw"][l].rearrange("(c p) n -> p c n", p=128)
            for cb in range(12):
                w = ring.next()
                P.dma(w, V(wv[:, :, cb * 512:(cb + 1) * 512]), q=("sp" if cb % 2 == 0 else "act"))
                for j in range(4):
                    for kc in range(8):
                        P.mm(pm[:, j, :], w[:, kc, j * 128:(j + 1) * 128], sc[:, kc, :], start=(kc == 0), stop=(kc == 7))
                P.tt(mT[:, l, cb * 4:(cb + 1) * 4, :], pm, adab[:, l, cb * 4:(cb + 1) * 4, :], ALU.add)
        aff = A.sb(es, [128, 2, 2, 4, 8])
        for l in range(2):
            for s in range(2):
                P.stt(aff[:, l, s, 0, :], mT[:, l, 8:16, s], 1.0, gm[:, l, 0, :], ALU.add, ALU.mult)
                P.copy(aff[:, l, s, 1, :], mT[:, l, 0:8, s])
                P.stt(aff[:, l, s, 2, :], mT[:, l, 32:40, s], 1.0, gm[:, l, 1, :], ALU.add, ALU.mult)
                P.copy(aff[:, l, s, 3, :], mT[:, l, 24:32, s])
        P.dma(V(d["aff"]), aff, q="pool")
        dring = Ring([A.sb(es, [128, 128]) for _ in range(4)])
        gring = Ring([A.sb(es, [128, 1024]) for _ in range(2)])
        pb = A.ps(es, [128, 1024])
        for l in range(2):
            for s in range(2):
                for g, base in ((0, 16), (1, 40)):
                    for c in range(8):
                        dg = dring.next()
                        P.ts(dg, ident, mT[:, l, base + c, s:s + 1], None, ALU.mult)
                        P.mm(pb[:, c * 128:(c + 1) * 128], ones, dg)
                    gb = gring.next()
                    P.copy(gb, pb, q="act")
                    P.dma(V(d["gateB"][l, s, g]), gb, q="pool")
        P.flush()


def norm_a(C, xt, small, junk, xn):
    P = C.P
    ss = small[:, 0:1]; rs = small[:, 1:2]
    P.memset(ss, 0.0)
    P.act(junk, xt, AF.Square, accum_out=ss)
    P.ts(rs, ss, 1.0 / 1024, 1e-6, ALU.mult, ALU.add)
    P.act(rs, rs, AF.Sqrt)
    P.recip(rs, rs)
    P.ts(xn, xt, rs, None, ALU.mult)


def norm_b(C, xn, aT_dst, affv, which, ident, pt):
    P = C.P
    for c in range(8):
        P.tr(pt[:, c, :], xn[:, c * 128:(c + 1) * 128], ident)
    for c in range(8):
        g = affv[:, 2 * which, c:c + 1]; b = affv[:, 2 * which + 1, c:c + 1]
        if c % 2 == 0:
            P.ts(aT_dst[:, c, :], pt[:, c, :], g, b, ALU.mult, ALU.add)
        else:
            P.act(aT_dst[:, c, :], pt[:, c, :], AF.Identity, bias=b, scale=g)


def norm_to_aT(C, xt, aT_dst, affv, which, ident, pt, small, junk, xn):
    norm_a(C, xt, small, junk, xn)
    norm_b(C, xn, aT_dst, affv, which, ident, pt)


def phase_inproj0(C):
    P, A, d = C.P, C.A, C.d
    with ExitStack() as es:
        W = A.sb(es, [128, 8, 3328], BF16)
        sring = Ring([A.sb(es, [128, 8, 512]) for _ in range(2)])
        load_w_bf16(C, es, W, d["w_in0"], 3328, sring)
        ident = A.sb(es, [128, 128]); P.dma(ident, V(d["ident"]))
        affall = A.sb(es, [128, 2, 2, 4, 8]); P.dma(affall, V(d["aff"]))
        xring = Ring([A.sb(es, [128, 1024]) for _ in range(2)])
        xnring = Ring([A.sb(es, [128, 1024]) for _ in range(2)])
        junk = A.sb(es, [128, 1024])
        smring = Ring([A.sb(es, [128, 2]) for _ in range(4)])
        aring = Ring([A.sb(es, [128, 8, 512], BF16) for _ in range(2)])
        ptring = Ring([A.ps(es, [128, 8, 128]) for _ in range(1)])
        pfring = Ring([A.ps(es, [128, 512]) for _ in range(4)])
        st32 = Ring([A.sb(es, [128, 512]) for _ in range(4)])
        st16 = Ring([A.sb(es, [128, 512], BF16) for _ in range(4)])
        for s, (X, Tn, sfx) in enumerate(((d["x"], C.T, ""), (d["ctx"], C.L, "_c"))):
            affv = affall[:, 0, s]
            sts = list(range(0, Tn, 512))
            pend = {}

            def prep_st(t0):
                nt = min(512, Tn - t0)
                aT = aring.next()
                for i in range(nt // 128):
                    xt = xring.next()
                    P.dma(xt, V(X[t0 + i * 128:t0 + (i + 1) * 128, :]))
                    norm_to_aT(C, xt, aT[:, :, i * 128:(i + 1) * 128], affv, 0, ident, ptring.next(),
                               smring.next(), junk, xnring.next())
                pend[t0] = aT

            prep_st(sts[0])
            for sti, t0 in enumerate(sts):
                nt = min(512, Tn - t0)
                aT = pend.pop(t0)
                fm = [(1536, 64, "lo", 0), (1600, 64, "lo", 64), (1664, 128, "lo", 128)]
                fm += [(1792 + j * 128, 128, "q", j * 128) for j in range(4)]
                fm += [(2304 + j * 128, 128, "k", j * 128) for j in range(4)]
                for (c0, ncol, kind, r0) in fm:
                    pf = pfring.next()
                    for kc in range(8):
                        P.mm(pf[0:ncol, 0:nt], W[:, kc, c0:c0 + ncol], aT[:, kc, 0:nt], start=(kc == 0), stop=(kc == 7))
                    if kind == "lo":
                        st = st32.next()
                        P.copy(st[0:ncol, 0:nt], pf[0:ncol, 0:nt], q="act")
                        P.dma(V(d["lo_raw" + sfx][r0:r0 + ncol, t0:t0 + nt]), st[0:ncol, 0:nt], q="pool")
                    elif kind == "q":
                        st = st16.next()
                        P.act(st[0:ncol, 0:nt], pf[0:ncol, 0:nt], AF.Copy, scale=0.125)
                        P.dma(V(d["qT" + sfx][r0:r0 + ncol, t0:t0 + nt]), st[0:ncol, 0:nt], q="pool")
                    else:
                        st = st16.next()
                        P.copy(st[0:ncol, 0:nt], pf[0:ncol, 0:nt], q="dve")
                        P.dma(V(d["kT" + sfx][r0:r0 + ncol, t0:t0 + nt]), st[0:ncol, 0:nt], q="pool")
                if sti + 1 < len(sts):
                    prep_st(sts[sti + 1])
                for i in range(nt // 128):
                    for g in range(4):
                        c0 = g * 512 if g < 3 else 2816
                        pf = pfring.next()
                        for kc in range(8):
                            P.mm(pf, aT[:, kc, i * 128:(i + 1) * 128], W[:, kc, c0:c0 + 512], start=(kc == 0), stop=(kc == 7))
                        r0 = t0 + i * 128
                        if g < 3:
                            st = st32.next()
                            P.copy(st, pf, q=("dve" if g % 2 == 0 else "act"))
                            P.dma(V(d["rkv_raw" + sfx][r0:r0 + 128, g * 512:(g + 1) * 512]), st, q="pool")
                        else:
                            st = st16.next()
                            P.copy(st, pf, q="dve")
                            P.dma(V(d["vna" + sfx][r0:r0 + 128, :]), st, q="pool")
        P.flush()


def na_pairs(rows):
    out = []
    for i in range(rows // 2):
        r = 2 * i
        r0a = min(max(r - 4, 0), rows - 8); r0b = min(max(r - 3, 0), rows - 8)
        u0 = min(r0a, rows - 9)
        out.append((u0, (r0a - u0, r0a - r + 7, r0b - u0, r0b - (r + 1) + 7)))
    return out


def na_bias2(rpb, rows):
    pairs = na_pairs(rows)
    keys = []
    for _, k in pairs:
        if k not in keys:
            keys.append(k)
    H = rpb.shape[0]
    out = np.full((H, 128, len(keys), 576), NA_MASK, np.float32)
    q = np.arange(64)
    cs = np.clip(q - 8, 0, 48)
    for si, (da, oa, db, ob) in enumerate(keys):
        for half, (dd, oo) in enumerate(((da, oa), (db, ob))):
            for kr in range(8):
                j = dd + kr
                for kc in range(16):
                    kcol = cs + kc
                    out[:, half * 64 + q, si, j * 64 + kcol] = rpb[:, oo + kr, kcol - q + 15]
    return out, keys


def na_gen(C, es):
    P, A, d = C.P, C.A, C.d
    T, L = C.T, C.L
    rows = T // 64
    pairs = na_pairs(rows)
    keys = []
    for _, k in pairs:
        if k not in keys:
            keys.append(k)
    ns = len(keys)
    if True:
        sb = lambda shape, dt=F32: A.sb(es, shape, dt)
        ident = sb([128, 128]); P.dma(ident, V(d["ident"]))
        identb = sb([128, 128], BF16); P.copy(identb, ident)
        qT = [sb([64, T], BF16) for _ in range(2)]
        kT = [sb([64, T], BF16) for _ in range(2)]
        v64 = [sb([64, rows, 64], BF16) for _ in range(2)]
        qcT = [sb([64, L], BF16) for _ in range(2)]
        kcT = [sb([64, L], BF16) for _ in range(2)]
        vc64 = [sb([64, L // 64, 64], BF16) for _ in range(2)]
        nabf = sb([128, ns, 576])
        nabb = [sb([128, ns, 576], BF16) for _ in range(2)]
        psr = Ring([A.ps(es, [128, 1024]) for _ in range(2)])
        ptp = A.ps(es, [64, 16, 128], BF16)
        pot = A.ps(es, [128, 512])
        por = Ring([V(pot.ap[:, k * 64:(k + 1) * 64], "po%d" % k) for k in range(4)])
        pexr = Ring([sb([128, 832], BF16) for _ in range(3)])
        ptsr = Ring([sb([64, 13, 128], BF16) for _ in range(2)])
        smr = Ring([sb([128, 4]) for _ in range(6)])
        RB = 16
        ostr = Ring([sb([128, RB, 64]) for _ in range(3)])
        vview = d["vna"].rearrange("(r p) (h e) -> p r h e", p=64, e=64)
        vcview = d["vna_c"].rearrange("(r p) (h e) -> p r h e", p=64, e=64)
        oview = d["mix0"].rearrange("(i p) c -> p i c", p=128)
        ocview = d["mix0_c"].rearrange("(i p) c -> p i c", p=128)

        def loads(h):
            b = h % 2
            P.dma(qT[b], V(d["qT"][h * 64:(h + 1) * 64, :]))
            P.dma(kT[b], V(d["kT"][h * 64:(h + 1) * 64, :]), q="act")
            P.dma(v64[b], V(vview[:, :, h, :]))
            P.dma(qcT[b], V(d["qT_c"][h * 64:(h + 1) * 64, :]), q="act")
            P.dma(kcT[b], V(d["kT_c"][h * 64:(h + 1) * 64, :]))
            P.dma(vc64[b], V(vcview[:, :, h, :]), q="act")
            P.dma(nabf, V(d["nab2"][h]))
            P.copy(nabb[b], nabf, q="pool")

        units = []
        for h in range(8):
            b = h % 2
            npair = len(pairs)
            for i, (u0, key) in enumerate(pairs):
                si = keys.index(key)
                ql = qT[b][:, i * 128:(i + 1) * 128]
                mms = [(0, 512, [(ql, kT[b][:, u0 * 64:u0 * 64 + 512]), (identb, nabb[b][:, si, 0:512])]),
                       (512, 64, [(ql, kT[b][:, (u0 + 8) * 64:(u0 + 9) * 64]), (identb, nabb[b][:, si, 512:576])]),
                       (576, L, [(ql, kcT[b][:, :])])]
                vch = [v64[b][:, u0 + j, :] for j in range(9)] + [vc64[b][:, j, :] for j in range(L // 64)]
                units.append(dict(h=h, first=(i == 0), mms=mms, nkeys=576 + L, vch=vch, slot=i % RB,
                                  newst=(i % RB == 0),
                                  store=((oview, (i // RB) * RB, i % RB + 1) if (i % RB == RB - 1 or i == npair - 1) else None)))
            nc_ = L // 128
            for i in range(nc_):
                ql = qcT[b][:, i * 128:(i + 1) * 128]
                mms = [(0, L, [(ql, kcT[b][:, :])])]
                vch = [vc64[b][:, j, :] for j in range(L // 64)]
                units.append(dict(h=h, first=False, mms=mms, nkeys=L, vch=vch, slot=i, newst=(i == 0),
                                  store=((ocview, 0, nc_) if i == nc_ - 1 else None)))

        def s1(u):
            ps = psr.next(); u["ps"] = ps
            for (c0, ncol, mm) in u["mms"]:
                for j, (l, r) in enumerate(mm):
                    P.mm(ps[:, c0:c0 + ncol], l, r, start=(j == 0), stop=(j == len(mm) - 1))
            sm = smr.next(); u["sm"] = sm
            nk = u["nkeys"]
            P.reduce(sm[:, 0:1], ps[:, 0:nk], ALU.max)
            P.ts(sm[:, 1:2], sm[:, 0:1], -1.0, None, ALU.mult, q="pool")
            P.memset(sm[:, 2:3], 0.0)
            pe_ = pexr.next(); u["pexp"] = pe_
            P.act(pe_[:, 0:nk], ps[:, 0:nk], AF.Exp, bias=sm[:, 1:2], accum_out=sm[:, 2:3])

        def s2(u):
            nch = u["nkeys"] // 64
            pe_ = u["pexp"]
            for j in range(nch):
                P.tr(ptp[:, j, :], pe_[:, j * 64:(j + 1) * 64], identb)
            pts = ptsr.next(); u["pts"] = pts
            hlf = (nch + 1) // 2
            P.copy(pts[:, 0:hlf, :], ptp[:, 0:hlf, :], q="dve")
            P.copy(pts[:, hlf:nch, :], ptp[:, hlf:nch, :], q="act")

        cur_st = {}

        def s3(u):
            nch = u["nkeys"] // 64
            po = por.next(); pts = u["pts"]; sm = u["sm"]
            for j in range(nch):
                P.mm(po, pts[:, j, :], u["vch"][j], start=(j == 0), stop=(j == nch - 1))
            P.recip(sm[:, 3:4], sm[:, 2:3])
            if u["newst"]:
                cur_st["t"] = ostr.next()
            ost = cur_st["t"]
            P.ts(ost[:, u["slot"], :], po, sm[:, 3:4], None, ALU.mult)
            if u["store"] is not None:
                view, i0, n = u["store"]
                hh = u["h"]
                P.dma(V(view[:, i0:i0 + n, 512 + hh * 64:512 + (hh + 1) * 64]), ost[:, 0:n, :], q="pool")

        loads(0)
        n = len(units)
        for it in range(n + 2):
            if it < n:
                u = units[it]
                if u["first"] and u["h"] + 1 < 8:
                    pass
                s1(u)
                if it >= 3 and units[it - 3]["first"] and units[it - 3]["h"] + 1 < 8:
                    loads(units[it - 3]["h"] + 1)
            if 0 <= it - 1 < n:
                s2(units[it - 1])
            if 0 <= it - 2 < n:
                s3(units[it - 2])
            yield


def declare(C, dbg):
    nc = C.nc
    T, L = C.T, C.L
    d = {}

    def din(name, shape, dt=F32):
        d[name] = nc.dram_tensor(name, list(shape), dt, kind="ExternalInput").ap()

    def scr(name, shape, dt=F32):
        kind = "ExternalOutput" if name in dbg else "Internal"
        d[name] = nc.dram_tensor(name, list(shape), dt, kind=kind).ap()

    din("x", [T, D]); din("ctx", [L, D]); din("cT", [128, 8, 2])
    din("adaw", [2, D, 6 * D]); din("adabT", [128, 2, 48, 2])
    din("gmixT", [128, 2, 8]); din("gmlpT", [128, 2, 8])
    din("ident", [128, 128])
    din("w_in0", [D, 3328]); din("nab2", list(na_bias2(np.zeros((8, 15, 31), np.float32), T // 64)[0].shape))
    din("mupB", [128, 1536]); din("munB", [128, 1536]); din("rwB", [128, 6, 512])
    din("w_up", [64, 2, 512]); din("a_up", [64, 2, 512]); din("g_up", [128, 512]); din("browB", [128, 4, 512])
    din("mulo", [128, 2, 2]); din("cm", [2, 128, 9, 128])
    din("w_out0", [D, D]); din("w_out1", [D, D]); din("mlp_w1", [2, D, 4096]); din("mlp_w2", [2, 4096, D])
    din("w_in1", [D, 5120]); din("fnormB", [128, D])
    d["out"] = nc.dram_tensor("out", [T, D], F32, kind="ExternalOutput").ap()
    scr("mix1", [T, D]); scr("hmid1", [T, D]); scr("hg_o0", [T, D]); scr("hg_o1", [T, D])
    din("hglB", [128, 2, D]); din("hgnB", [128, D]); din("cmh", [2, 128, 3, 128])
    scr("aff", [128, 2, 2, 4, 8]); scr("gateB", [2, 2, 2, 128, D])
    for sfx, Tn in (("", T), ("_c", L)):
        scr("lo_raw" + sfx, [256, Tn]); scr("rkv_raw" + sfx, [Tn, 1536])
        scr("qT" + sfx, [512, Tn], BF16); scr("kT" + sfx, [512, Tn], BF16); scr("vna" + sfx, [Tn, 512], BF16)
        scr("mix0" + sfx, [Tn, D])
        scr("rw_g" + sfx, [Tn, 512]); scr("rw_sh" + sfx, [Tn, 1536]); scr("rw_kk" + sfx, [Tn, 512]); scr("rw_lo" + sfx, [128, Tn])
        for dr_ in range(2):
            scr("rw_o%d" % dr_ + sfx, [Tn, 512]); scr("rw_b%d" % dr_ + sfx, [Tn, 512])
        scr("hmid0" + sfx, [Tn, D]); scr("h1" + sfx, [Tn, D]); scr("p1" + sfx, [Tn, 5120])
    C.d = d


def build(T, L, phases, dbg=()):
    nc = bass.Bass("TRN2", target_bir_lowering=False)
    C = Cx()
    C.nc = nc; C.P = Prog(nc); C.A = Alloc(nc); C.T = T; C.L = L
    declare(C, dbg)
    for ph in phases:
        ph(C)
    return nc, C


def fm_cols(v):
    v = np.asarray(v, np.float32)
    lead = v.shape[:-1]
    return np.ascontiguousarray(np.moveaxis(v.reshape(lead + (8, 128)), -1, 0))


def na_bias(rpb, rows):
    H = rpb.shape[0]
    out = np.full((H, 64, 8, 512), NA_MASK, np.float32)
    q = np.arange(64)
    cs = np.clip(q - 8, 0, 48)
    reps = [0, 1, 2, 3, 4, rows - 3, rows - 2, rows - 1]
    for si, r in enumerate(reps):
        r0 = min(max(r - 4, 0), rows - 8)
        for kr in range(8):
            rr = r0 + kr - r + 7
            for kc in range(16):
                kcol = cs + kc
                out[:, q, si, kr * 64 + kcol] = rpb[:, rr, kcol - q + 15]
    return out


def rw_consts():
    u = np.arange(128)[:, None]; t = np.arange(128)[None, :]
    cm = np.zeros((2, 128, 9, 128), np.float32)
    for dr in range(2):
        if dr == 0:
            bef = (u < t); befeq = (u <= t); half = (u <= 63); aft = (u > t)
        else:
            bef = (u > t); befeq = (u >= t); half = (u >= 64); aft = (u < t)
        cm[dr, :, 0, :] = befeq.astype(np.float32) - half.astype(np.float32)
        cm[dr, :, 1, :] = bef.astype(np.float32) - half.astype(np.float32)
        cm[dr, :, 2, :] = aft
        cm[dr, :, 3, :] = bef
        cm[dr, :, 4, :] = befeq
        cm[dr, :, 5, :] = -bef.astype(np.float32)
        cm[dr, :, 6, :] = -(bef.T).astype(np.float32)
        cm[dr, :, 7, :] = np.broadcast_to(half, (128, 128))
        cm[dr, :, 8, :] = 1.0
    return cm


def prep_inputs(inp, b, T, L):
    f = lambda a: np.ascontiguousarray(np.asarray(a, np.float32))
    m = {}
    m["x"] = f(inp["x"][b][:T]); m["ctx"] = f(inp["ctx"][b][:L])
    cs = np.stack([np.asarray(inp["c"][b], np.float32), np.asarray(inp["c_ctx"], np.float32)], 0)
    m["cT"] = np.ascontiguousarray(np.transpose(fm_cols(cs), (0, 2, 1)))
    m["adaw"] = f(inp["ada_w"])
    ab = fm_cols(np.asarray(inp["ada_b"], np.float32).reshape(2, 6, D))
    ab = ab.reshape(128, 2, 48)
    m["adabT"] = np.ascontiguousarray(np.repeat(ab[:, :, :, None], 2, axis=3))
    m["gmixT"] = fm_cols(inp["norm_mix_g"]); m["gmlpT"] = fm_cols(inp["norm_mlp_g"])
    m["ident"] = np.eye(128, dtype=np.float32)
    m["w_in0"] = f(inp["ev_w_in"][0])
    m["nab2"] = na_bias2(np.asarray(inp["na_rpb"][0], np.float32), T // 64)[0]
    rep = lambda v: np.ascontiguousarray(np.broadcast_to(np.asarray(v, np.float32).reshape(1, -1), (128, np.asarray(v).size)))
    mup = np.asarray(inp["rw_mu_prev"][0], np.float32); mun = np.asarray(inp["rw_mu_next"][0], np.float32)
    m["mupB"] = rep(mup[:1536]); m["munB"] = rep(mun[:1536])
    ka = np.asarray(inp["rw_k_a"][0], np.float32)
    m["rwB"] = np.ascontiguousarray(np.stack([rep(inp["rw_k_k"][0]), rep(ka), rep(inp["rw_r_k"][0].reshape(-1)),
                                              rep(inp["rw_ln_g"][0]), rep(inp["rw_ln_b"][0]), rep(ka)], axis=1))
    m["w_up"] = np.ascontiguousarray(np.transpose(np.asarray(inp["rw_w_up"][0], np.float32), (1, 0, 2)))
    m["a_up"] = np.ascontiguousarray(np.transpose(np.asarray(inp["rw_a_up"][0], np.float32), (1, 0, 2)))
    m["g_up"] = f(inp["rw_g_up"][0])
    brow = np.concatenate([np.asarray(inp["rw_w0"][0], np.float32), np.asarray(inp["rw_a0"][0], np.float32)], 0)
    m["browB"] = np.ascontiguousarray(np.broadcast_to(brow[None], (128, 4, 512)))
    mulo = np.zeros((128, 2, 2), np.float32)
    mulo[:, 0, 0] = mup[1536:1664]; mulo[:, 0, 1] = mun[1536:1664]
    mulo[:, 1, 0] = mup[1664:1792]; mulo[:, 1, 1] = mun[1664:1792]
    m["mulo"] = mulo
    m["cm"] = rw_consts()
    m["w_out0"] = f(inp["ev_w_out"][0]); m["w_out1"] = f(inp["od_w_out"][0])
    m["mlp_w1"] = f(inp["mlp_w1"]); m["mlp_w2"] = f(inp["mlp_w2"]); m["w_in1"] = f(inp["od_w_in"][0])
    m["fnormB"] = rep(inp["final_norm_g"])
    m["hglB"] = np.ascontiguousarray(np.stack([rep(inp["hg_lower"][0]), rep(inp["hg_lower"][1])], axis=1))
    m["hgnB"] = rep(inp["hg_norm_g"][0]); m["cmh"] = hg_consts()
    return m


CDEC = -0.6065306597126334


def bc(v, shape, axis):
    return V(v.ap.unsqueeze(axis).to_broadcast(list(shape)), v.key)


def interleave(gens):
    gens = list(gens)
    while gens:
        for g in list(gens):
            try:
                next(g)
            except StopIteration:
                gens.remove(g)


def rwpre_gen(C, es, NB):
    P, A, d = C.P, C.A, C.d
    sb = lambda shape, dt=F32: A.sb(es, shape, dt)
    mupB = sb([128, 1536]); munB = sb([128, 1536]); c0B = sb([128, 1536])
    P.dma(mupB, V(d["mupB"])); P.dma(munB, V(d["munB"]))
    P.tt(c0B, mupB, munB, ALU.add, q="pool")
    P.ts(c0B, c0B, -1.0, 1.0, ALU.mult, ALU.add, q="pool")
    kkB = sb([128, 512]); P.dma(kkB, V(d["rwB"][:, 0, :]))
    gup = sb([128, 512]); P.dma(gup, V(d["g_up"]))
    mulo = sb([128, 2, 3]); P.dma(mulo[:, :, 1:3], V(d["mulo"]))
    P.tt(mulo[:, :, 0:1], mulo[:, :, 1:2], mulo[:, :, 2:3], ALU.add, q="pool")
    P.ts(mulo[:, :, 0:1], mulo[:, :, 0:1], -1.0, 1.0, ALU.mult, ALU.add, q="pool")
    NG = 3
    curR = [sb([128, 512]) for _ in range(NG)]; prvR = [sb([128, 512]) for _ in range(NG)]; nxtR = [sb([128, 512]) for _ in range(NG)]
    shR = [sb([128, 512]) for _ in range(NG)]
    tA = sb([128, 512]); tB = sb([128, 512])
    loAR = [sb([128, 130]) for _ in range(2)]; loGR = [sb([128, 130]) for _ in range(2)]
    loAs = sb([128, 128]); loGs = sb([128, 128])
    kk = sb([128, 512]); gsb = sb([128, 512]); hs = sb([128, 8])
    pg = A.ps(es, [128, 512])
    h8 = lambda v: v.re("p (h e) -> p h e", e=64)
    units = []
    for seq, Tn in ((1, C.L), (0, C.T)):
        for t0 in range(0, Tn, 128):
            for g in range(3):
                units.append((seq, Tn, t0, g))

    def loads(ui):
        seq, Tn, t0, g = units[ui]
        sfx = "_c" if seq else ""
        rkv = d["rkv_raw" + sfx]
        sl = slice(g * 512, (g + 1) * 512)
        cur = curR[ui % NG]; prv = prvR[ui % NG]; nxt = nxtR[ui % NG]
        P.dma(cur, V(rkv[t0:t0 + 128, sl]))
        if t0 == 0:
            P.memset(prv, 0.0)
            P.dma(prv[1:128, :], V(rkv[0:127, sl]))
        else:
            P.dma(prv, V(rkv[t0 - 1:t0 + 127, sl]))
        if t0 + 128 == Tn:
            P.memset(nxt, 0.0)
            P.dma(nxt[0:127, :], V(rkv[t0 + 1:t0 + 128, sl]))
        else:
            P.dma(nxt, V(rkv[t0 + 1:t0 + 129, sl]))
        if g == 0:
            lo = d["lo_raw" + sfx]
            ti = t0 // 128
            loA = loAR[ti % 2]; loG = loGR[ti % 2]
            lo_a = max(t0 - 1, 0); lo_b = min(t0 + 129, Tn)
            c_a = lo_a - (t0 - 1); c_b = c_a + (lo_b - lo_a)
            if c_a > 0 or c_b < 130:
                P.memset(loA, 0.0); P.memset(loG, 0.0)
            P.dma(loA[:, c_a:c_b], V(lo[0:128, lo_a:lo_b]))
            P.dma(loG[:, c_a:c_b], V(lo[128:256, lo_a:lo_b]))

    loads(0)
    if len(units) > 1:
        loads(1)
    for ui, (seq, Tn, t0, g) in enumerate(units):
        sfx = "_c" if seq else ""
        if ui + 2 < len(units):
            loads(ui + 2)
        cur = curR[ui % NG]; prv = prvR[ui % NG]; nxt = nxtR[ui % NG]; sh = shR[ui % NG]
        sl = slice(g * 512, (g + 1) * 512)
        P.tt(tA, cur, c0B[:, sl], ALU.mult, q="pool")
        P.tt(tB, prv, mupB[:, sl], ALU.mult, q="dve")
        P.tt(tA, tA, tB, ALU.add, q="pool")
        P.tt(tB, nxt, munB[:, sl], ALU.mult, q="dve")
        P.tt(sh, tA, tB, ALU.add, q="pool")
        P.dma(V(d["rw_sh" + sfx][t0:t0 + 128, sl]), sh)
        yield
        if g == 1:
            P.tt(kk, sh, kkB, ALU.mult, q="pool")
            P.tt(tA, kk, kk, ALU.mult, q="dve")
            P.reduce(hs, h8(tA), ALU.add)
            P.act(hs, hs, AF.Sqrt)
            P.ts(hs, hs, 1e-12, None, ALU.max)
            P.recip(hs, hs)
            P.tt(h8(kk), h8(kk), bc(hs, [128, 8, 64], 2), ALU.mult)
            P.dma(V(d["rw_kk" + sfx][t0:t0 + 128, :]), kk)
            yield
        if g == 0:
            ti = t0 // 128
            loA = loAR[ti % 2]; loG = loGR[ti % 2]
            for (src, dst, gi) in ((loA, loAs, 0), (loG, loGs, 1)):
                P.ts(dst, src[:, 1:129], mulo[:, gi, 0:1], None, ALU.mult)
                P.stt(dst, src[:, 0:128], mulo[:, gi, 1:2], dst, ALU.mult, ALU.add)
                P.stt(dst, src[:, 2:130], mulo[:, gi, 2:3], dst, ALU.mult, ALU.add)
            P.act(loAs[0:64, :], loAs[0:64, :], AF.Tanh)
            P.act(loGs, loGs, AF.Sigmoid)
            P.dma(V(d["rw_lo" + sfx][:, t0:t0 + 128]), loAs)
            yield
            P.mm(pg, loGs, gup)
            P.copy(gsb, pg, q="act")
            P.dma(V(d["rw_g" + sfx][t0:t0 + 128, :]), gsb)
            yield


def phase_na_pre(C):
    with ExitStack() as es:
        g1 = na_gen(C, es)
        g2 = rwpre_gen(C, es, 1)
        done1 = done2 = False
        while not (done1 and done2):
            if not done1:
                try:
                    next(g1)
                except StopIteration:
                    done1 = True
            for _ in range(2):
                if not done2:
                    try:
                        next(g2)
                    except StopIteration:
                        done2 = True
        C.P.flush()


def phase_rwkv(C):
    P, A, d = C.P, C.A, C.d
    with ExitStack() as es:
        sb = lambda shape, dt=F32: A.sb(es, shape, dt)
        ident = sb([128, 128]); P.dma(ident, V(d["ident"]))
        identb = sb([128, 128], BF16); P.copy(identb, ident)
        pB = sb([128, 6, 512]); P.dma(pB, V(d["rwB"]))
        kkB, kaB, rkB, lngB, lnbB, omkaB = [pB[:, i, :] for i in range(6)]
        P.ts(omkaB, kaB, -1.0, 1.0, ALU.mult, ALU.add, q="pool")
        wup = sb([64, 2, 512]); P.dma(wup, V(d["w_up"]))
        aup = sb([128, 2, 512]); P.dma(aup[64:128], V(d["a_up"]), q="act")
        browB = sb([128, 4, 512]); P.dma(browB, V(d["browB"]), q="act")
        psr = Ring([A.ps(es, [128, 1024]) for _ in range(4)])
        h8 = lambda v: v.re("p (h e) -> p h e", e=64)

        def stream(dr):
            cm = sb([128, 9, 128]); P.dma(cm, V(d["cm"][dr]))
            shR = [sb([128, 1536]) for _ in range(2)]; tA = sb([128, 512]); tB = sb([128, 512]); loR = [sb([128, 128]) for _ in range(2)]
            sw = sb([128, 512]); ad = sb([128, 512]); kkR = [sb([128, 512]) for _ in range(2)]; kd = sb([128, 512]); bt = sb([128, 512])
            hs = sb([128, 8])
            xr = sb([128, 512], BF16); xa = sb([128, 512], BF16); xb = sb([128, 512], BF16); xk = sb([128, 512], BF16)
            BH = sb([128, 512], BF16); KH = sb([128, 512], BF16); vb = sb([128, 512], BF16)
            ART = sb([64, 8, 256], BF16); BWT = sb([64, 8, 128], BF16); KWT = sb([64, 8, 128], BF16)
            Qr = Ring([sb([128, 8, 128], BF16) for _ in range(2)]); Xr = Ring([sb([128, 8, 128], BF16) for _ in range(2)])
            Acc = sb([128, 8, 128]); Accb = sb([128, 8, 128], BF16)
            ArbT = sb([128, 8, 128], BF16); AakT = sb([128, 8, 128], BF16); ArkT = sb([128, 8, 128], BF16)
            WtotB = sb([64, 8, 64]); WmidB = sb([64, 8, 64])
            bon = sb([128, 512])
            ST = [sb([64, 8, 64]) for _ in range(2)]
            S0m = sb([64, 8, 64], BF16); Stmp = sb([64, 8, 64]); Ysb = sb([128, 8, 64]); Usb = sb([128, 8, 64], BF16); Ot = sb([128, 512])
            P.memset(ST[0], 0.0)
            si = 0
            work = []
            for seq, Tn in ((1, C.L), (0, C.T)):
                nt = Tn // 128
                order = range(nt) if dr == 0 else range(nt - 1, -1, -1)
                work += [("_c" if seq else "", ti * 128) for ti in order]

            def issue_loads(wi):
                sfx_, t0_ = work[wi]
                P.dma(shR[wi % 2], V(d["rw_sh" + sfx_][t0_:t0_ + 128, :]), q="sp")
                P.dma(kkR[wi % 2], V(d["rw_kk" + sfx_][t0_:t0_ + 128, :]), q="sp")
                P.dma(loR[wi % 2], V(d["rw_lo" + sfx_][:, t0_:t0_ + 128]), q="sp")

            issue_loads(0)
            for wi, (sfx, t0) in enumerate(work):
                if True:
                    sh = shR[wi % 2]; kk = kkR[wi % 2]; lo = loR[wi % 2]
                    if wi + 1 < len(work):
                        issue_loads(wi + 1)
                    r_, k_, v_ = sh[:, 0:512], sh[:, 512:1024], sh[:, 1024:1536]
                    yield
                    pz = psr.next()
                    P.mm(pz[:, 0:512], lo[0:64, :], wup[:, dr, :])
                    P.mm(pz[:, 512:1024], lo[64:128, :], aup[64:128, dr, :])
                    P.tt(sw, pz[:, 0:512], browB[:, dr, :], ALU.add)
                    P.tt(ad, pz[:, 512:1024], browB[:, 2 + dr, :], ALU.add)
                    P.act(sw, sw, AF.Sigmoid)
                    P.act(ad, ad, AF.Sigmoid)
                    P.copy(vb, v_, q="act")
                    yield
                    P.tt(tA, ad, kaB, ALU.mult, q="dve")
                    P.tt(tA, tA, omkaB, ALU.add, q="dve")
                    P.tt(kd, k_, tA, ALU.mult, q="pool")
                    P.tt(bt, kk, ad, ALU.mult, q="pool")
                    yield
                    P.tt(tB, r_, kd, ALU.mult)
                    P.tt(tB, tB, rkB, ALU.mult)
                    P.reduce(hs, h8(tB), ALU.add)
                    P.tt(h8(bon), h8(v_), bc(hs, [128, 8, 64], 2), ALU.mult)
                    P.dma(V(d["rw_b%d" % dr + sfx][t0:t0 + 128, :]), bon, q="pool")
                    yield
                    pl = psr.next(); pl2 = psr.next()
                    P.mm(pl[:, 0:512], cm[:, 0, :], sw)
                    P.mm(pl[:, 512:1024], cm[:, 1, :], sw)
                    P.mm(pl2[:, 0:512], cm[:, 2, :], sw)
                    for h in range(8):
                        P.mm(pl2[0:64, 512 + h * 64:512 + (h + 1) * 64], sw[:, h * 64:(h + 1) * 64], cm[:, 8, 0:64])
                    P.act(tA, pl[:, 0:512], AF.Exp, scale=CDEC)
                    P.tt(xr, r_, tA, ALU.mult, q="pool")
                    P.act(tA, pl[:, 0:512], AF.Exp, scale=-CDEC)
                    P.tt(xb, bt, tA, ALU.mult, q="dve")
                    P.tt(xk, kd, tA, ALU.mult, q="pool")
                    P.act(tB, pl[:, 512:1024], AF.Exp, scale=CDEC)
                    P.tt(xa, kk, tB, ALU.mult, q="pool")
                    P.act(tB, pl2[:, 0:512], AF.Exp, scale=CDEC)
                    P.tt(BH, bt, tB, ALU.mult, q="dve")
                    P.tt(KH, kd, tB, ALU.mult, q="pool")
                    P.act(WtotB.re("p h e -> p (h e)"), pl2[0:64, 512:1024], AF.Exp, scale=CDEC)
                    yield
                    pw = psr.next()
                    for h in range(8):
                        P.mm(pw[0:64, h * 64:(h + 1) * 64], sw[:, h * 64:(h + 1) * 64], cm[:, 7, 0:64])
                    P.act(WmidB.re("p h e -> p (h e)"), pw[0:64, 0:512], AF.Exp, scale=CDEC)
                    yield
                    for (src, dst, off, eng) in ((xa, ART, 0, "dve"), (xr, ART, 128, "act"), (xb, BWT, 0, "dve"), (xk, KWT, 0, "act")):
                        pt = psr.next()
                        ptb = V(pt.ap.bitcast(BF16), pt.key)
                        for h in range(8):
                            P.tr(ptb[0:64, h * 128:(h + 1) * 128], src[:, h * 64:(h + 1) * 64], identb)
                        P.copy(dst[:, :, off:off + 128], ptb[0:64, 0:1024].re("p (h t) -> p h t", t=128), q=eng)
                        yield
                    Q = Qr.next(); X = Xr.next()
                    for hp in range(4):
                        pa = psr.next(); pn = psr.next()
                        for j in range(2):
                            h = hp * 2 + j
                            P.mm(pa[:, j * 256:(j + 1) * 256], BWT[:, h, :], ART[:, h, :])
                            P.mm(pa[:, 512 + j * 256:512 + (j + 1) * 256], KWT[:, h, :], ART[:, h, :])
                            P.mm(pn[:, j * 128:(j + 1) * 128], ART[:, h, 0:128], BWT[:, h, :])
                        hsl = slice(hp * 2, hp * 2 + 2)
                        pav = pa.re("p (a j c t) -> p a j c t", a=2, j=2, c=2)
                        P.tt(Q[:, hsl, :], pav[:, 0, :, 0, :], bc(cm[:, 5, :], [128, 2, 128], 1), ALU.mult)
                        P.tt(ArbT[:, hsl, :], pav[:, 0, :, 1, :], bc(cm[:, 4, :], [128, 2, 128], 1), ALU.mult)
                        P.tt(AakT[:, hsl, :], pav[:, 1, :, 0, :], bc(cm[:, 3, :], [128, 2, 128], 1), ALU.mult)
                        P.tt(ArkT[:, hsl, :], pav[:, 1, :, 1, :], bc(cm[:, 4, :], [128, 2, 128], 1), ALU.mult)
                        P.tt(X[:, hsl, :], pn[:, 0:256].re("p (j t) -> p j t", j=2), bc(cm[:, 6, :], [128, 2, 128], 1), ALU.mult)
                        yield
                    P.tt(Acc, Q, bc(ident, [128, 8, 128], 1), ALU.add, q="pool")
                    P.copy(Accb, Acc, q="act")
                    for lev in range(6):
                        px = psr.next()
                        for h in range(8):
                            P.mm(px[:, h * 128:(h + 1) * 128], Q[:, h, :], X[:, h, :])
                        Xn = Xr.next()
                        P.copy(Xn.re("p h t -> p (h t)"), px, q="act")
                        if lev < 5:
                            pq = psr.next()
                            for h in range(8):
                                P.mm(pq[:, h * 128:(h + 1) * 128], X[:, h, :], Q[:, h, :])
                            Qn = Qr.next()
                            P.copy(Qn.re("p h t -> p (h t)"), pq, q="dve")
                        yield
                        pc = psr.next()
                        for h in range(8):
                            P.mm(pc[:, h * 128:(h + 1) * 128], Xn[:, h, :], Accb[:, h, :])
                        P.tt(Acc.re("p h t -> p (h t)"), Acc.re("p h t -> p (h t)"), pc, ALU.add)
                        if lev < 5:
                            P.copy(Accb, Acc, q="act")
                        X = Xn
                        if lev < 5:
                            Q = Qn
                        yield
                    S_in = ST[si % 2]; S_out = ST[(si + 1) % 2]
                    si += 1
                    P.tt(S0m, S_in, WmidB, ALU.mult)
                    P.tt(Stmp, S_in, WtotB, ALU.mult, q="pool")
                    py = psr.next()
                    for h in range(8):
                        P.mm(py[:, h * 64:(h + 1) * 64], ART[:, h, 0:128], S0m[:, h, :], start=True, stop=False)
                        P.mm(py[:, h * 64:(h + 1) * 64], AakT[:, h, :], vb[:, h * 64:(h + 1) * 64], start=False, stop=True)
                    P.copy(Ysb.re("p h e -> p (h e)"), py[:, 0:512], q="act")
                    yield
                    pu = psr.next()
                    for h in range(8):
                        P.mm(pu[:, h * 64:(h + 1) * 64], Acc[:, h, :], Ysb[:, h, :])
                    P.ts(Usb.re("p h e -> p (h e)"), pu[:, 0:512], -1.0, None, ALU.mult)
                    yield
                    pss = psr.next()
                    for h in range(8):
                        P.mm(pss[0:64, h * 64:(h + 1) * 64], BH[:, h * 64:(h + 1) * 64], Usb[:, h, :], start=True, stop=False)
                        P.mm(pss[0:64, h * 64:(h + 1) * 64], KH[:, h * 64:(h + 1) * 64], vb[:, h * 64:(h + 1) * 64], start=False, stop=True)
                    P.tt(S_out.re("p h e -> p (h e)"), Stmp.re("p h e -> p (h e)"), pss[0:64, 0:512], ALU.add)
                    po = psr.next()
                    for h in range(8):
                        P.mm(po[:, h * 64:(h + 1) * 64], ART[:, h, 128:256], S0m[:, h, :], start=True, stop=False)
                        P.mm(po[:, h * 64:(h + 1) * 64], ArbT[:, h, :], Usb[:, h, :], start=False, stop=False)
                        P.mm(po[:, h * 64:(h + 1) * 64], ArkT[:, h, :], vb[:, h * 64:(h + 1) * 64], start=False, stop=True)
                    P.copy(Ot, po[:, 0:512], q="act")
                    P.dma(V(d["rw_o%d" % dr + sfx][t0:t0 + 128, :]), Ot, q="pool")
                    yield

        interleave([stream(0), stream(1)])
        P.flush()


def phase_rwkv_post(C):
    P, A, d = C.P, C.A, C.d
    with ExitStack() as es:
        sb = lambda shape, dt=F32: A.sb(es, shape, dt)
        pB = sb([128, 6, 512]); P.dma(pB, V(d["rwB"]))
        lngB = pB[:, 3, :]; lnbB = pB[:, 4, :]
        NB = 2
        R = lambda: Ring([sb([128, 512]) for _ in range(NB)])
        o0R, o1R, b0R, b1R, gR, resR, cenR = R(), R(), R(), R(), R(), R(), R()
        hsR = Ring([sb([128, 8, 2]) for _ in range(NB)])
        h8 = lambda v: v.re("p (h e) -> p h e", e=64)
        for seq, Tn in ((1, C.L), (0, C.T)):
            sfx = "_c" if seq else ""
            for t0 in range(0, Tn, 128):
                o0, o1, b0, b1, g_, res, cen, hs = (o0R.next(), o1R.next(), b0R.next(), b1R.next(), gR.next(),
                                                     resR.next(), cenR.next(), hsR.next())
                P.dma(o0, V(d["rw_o0" + sfx][t0:t0 + 128, :]))
                P.dma(o1, V(d["rw_o1" + sfx][t0:t0 + 128, :]), q="act")
                P.dma(b0, V(d["rw_b0" + sfx][t0:t0 + 128, :]))
                P.dma(b1, V(d["rw_b1" + sfx][t0:t0 + 128, :]), q="act")
                P.dma(g_, V(d["rw_g" + sfx][t0:t0 + 128, :]))
                P.tt(res, o0, o1, ALU.add, q="pool")
                P.tt(b0, b0, b1, ALU.add, q="pool")
                P.reduce(hs[:, :, 0], h8(res), ALU.add)
                P.ts(hs[:, :, 0], hs[:, :, 0], 1.0 / 64, None, ALU.mult)
                P.tt(h8(cen), h8(res), bc(hs[:, :, 0], [128, 8, 64], 2), ALU.subtract)
                P.tt(res, cen, cen, ALU.mult, q="pool")
                P.reduce(hs[:, :, 1], h8(res), ALU.add)
                P.ts(hs[:, :, 1], hs[:, :, 1], 1.0 / 64, 64e-5, ALU.mult, ALU.add)
                P.act(hs[:, :, 1], hs[:, :, 1], AF.Sqrt)
                P.recip(hs[:, :, 1], hs[:, :, 1])
                P.tt(h8(cen), h8(cen), bc(hs[:, :, 1], [128, 8, 64], 2), ALU.mult)
                P.tt(cen, cen, lngB, ALU.mult, q="pool")
                P.tt(cen, cen, lnbB, ALU.add, q="dve")
                P.tt(cen, cen, b0, ALU.add, q="pool")
                P.tt(res, cen, g_, ALU.mult, q="dve")
                P.dma(V(d["mix0" + sfx][t0:t0 + 128, 0:512]), res, q="pool")
        P.flush()


def phase_outproj(C, l):
    P, A, d = C.P, C.A, C.d
    with ExitStack() as es:
        W = A.sb(es, [128, 8, 1024], BF16)
        sring = Ring([A.sb(es, [128, 8, 512]) for _ in range(2)])
        load_w_bf16(C, es, W, d["w_out%d" % l], 1024, sring)
        ident = A.sb(es, [128, 128]); P.dma(ident, V(d["ident"]))
        gB = A.sb(es, [128, 1024])
        mring = Ring([A.sb(es, [128, 1024]) for _ in range(2)])
        hring = Ring([A.sb(es, [128, 1024]) for _ in range(2)])
        oring = Ring([A.sb(es, [128, 1024]) for _ in range(2)])
        mT = Ring([A.sb(es, [128, 8, 128], BF16) for _ in range(2)])
        ptr = Ring([A.ps(es, [128, 8, 128]) for _ in range(2)])
        pyr = Ring([A.ps(es, [128, 512]) for _ in range(4)])
        seqs = [(0, C.T, "")] + ([(1, C.L, "_c")] if l == 0 else [])
        for s, Tn, sfx in seqs:
            hin = d["x" if s == 0 else "ctx"] if l == 0 else d["h1" + sfx]
            P.dma(gB, V(d["gateB"][l, s, 0]))
            for t0 in range(0, Tn, 128):
                mt = mring.next(); ht = hring.next(); ot = oring.next(); mt_T = mT.next(); pt = ptr.next()
                P.dma(mt, V(d["mix%d" % l + sfx][t0:t0 + 128, :]))
                P.dma(ht, V(hin[t0:t0 + 128, :]), q="act")
                for c in range(8):
                    P.tr(pt[:, c, :], mt[:, c * 128:(c + 1) * 128], ident)
                P.copy(mt_T[:, 0:4, :], pt[:, 0:4, :], q="dve")
                P.copy(mt_T[:, 4:8, :], pt[:, 4:8, :], q="act")
                for half in range(2):
                    py = pyr.next()
                    sl = slice(half * 512, (half + 1) * 512)
                    for kc in range(8):
                        P.mm(py, mt_T[:, kc, :], W[:, kc, sl], start=(kc == 0), stop=(kc == 7))
                    P.tt(ot[:, sl], py, gB[:, sl], ALU.mult)
                    P.tt(ot[:, sl], ot[:, sl], ht[:, sl], ALU.add, q="pool")
                P.dma(V(d["hmid%d" % l + sfx][t0:t0 + 128, :]), ot, q="pool")
        P.flush()


def phase_mlp(C, l):
    P, A, d = C.P, C.A, C.d
    last = (l == 1)
    with ExitStack() as es:
        W1 = A.sb(es, [128, 8, 4096], BF16)
        W2 = A.sb(es, [128, 32, 1024], BF16)
        with ExitStack() as es2:
            sring = Ring([A.sb(es2, [128, 8, 256]) for _ in range(2)])
            load_w_bf16(C, es2, W1, d["mlp_w1"][l], 4096, sring, blk=256)
            w2v = d["mlp_w2"][l].rearrange("(c p) n -> p c n", p=128)
            for i in range(16):
                st = sring.next()
                stv = st.re("p c n -> p (c n)").re("p (c n) -> p c n", c=2)
                P.dma(stv, V(w2v[:, 2 * i:2 * i + 2, :]), q=("sp" if i % 2 == 0 else "act"))
                for j in range(2):
                    P.copy(W2[:, 2 * i + j, :], stv[:, j, :], q=("dve", "pool", "act")[(2 * i + j) % 3])
            P.flush()
        with ExitStack() as es2:
            sb = lambda shape, dt=F32: A.sb(es2, shape, dt)
            ident = sb([128, 128]); P.dma(ident, V(d["ident"]))
            affall = sb([128, 2, 2, 4, 8]); P.dma(affall, V(d["aff"]))
            gB = sb([128, 1024]); fnB = sb([128, 1024])
            if last:
                P.dma(fnB, V(d["fnormB"]))
            xring = Ring([sb([128, 1024]) for _ in range(3)])
            xnring = Ring([sb([128, 1024]) for _ in range(3)])
            junk = sb([128, 1024], BF16)
            smring = Ring([sb([128, 2]) for _ in range(6)])
            aring = Ring([sb([128, 8, 128], BF16) for _ in range(2)])
            hTr = Ring([sb([128, 32, 128], BF16) for _ in range(2)])
            rl = Ring([sb([128, 512]) for _ in range(2)])
            ptr = Ring([A.ps(es2, [128, 8, 128]) for _ in range(1)])
            phr = Ring([A.ps(es2, [128, 512]) for _ in range(2)])
            pyr = Ring([A.ps(es2, [128, 512]) for _ in range(2)])
            seqs = [(0, C.T, "")] + ([(1, C.L, "_c")] if l == 0 else [])
            for s, Tn, sfx in seqs:
                affv = affall[:, l, s]
                P.dma(gB, V(d["gateB"][l, s, 1]))
                tiles = list(range(0, Tn, 128))
                st = {}

                def prep_a(t0):
                    xt = xring.next(); xn = xnring.next()
                    P.dma(xt, V(d["hmid%d" % l + sfx][t0:t0 + 128, :]))
                    norm_a(C, xt, smring.next(), junk, xn)
                    st[t0] = [xt, xn, None]

                def prep_b(t0):
                    aT = aring.next()
                    norm_b(C, st[t0][1], aT, affv, 1, ident, ptr.next())
                    st[t0][2] = aT

                prep_a(tiles[0]); prep_b(tiles[0])
                for ti, t0 in enumerate(tiles):
                    xt, xn, aT = st.pop(t0)
                    hT = hTr.next()
                    nxt_t = tiles[ti + 1] if ti + 1 < len(tiles) else None
                    if nxt_t is not None:
                        prep_a(nxt_t)
                    for hq in range(8):
                        ph = phr.next()
                        for j in range(4):
                            hc = hq * 4 + j
                            for kc in range(8):
                                P.mm(ph[:, j * 128:(j + 1) * 128], W1[:, kc, hc * 128:(hc + 1) * 128], aT[:, kc, :],
                                     start=(kc == 0), stop=(kc == 7))
                        r = rl.next()
                        P.act(r, ph, AF.Relu)
                        P.tt(hT[:, hq * 4:(hq + 1) * 4, :].re("p c n -> p (c n)"), r, r, ALU.mult, q=("pool" if hq % 2 == 0 else "dve"))
                    if nxt_t is not None:
                        prep_b(nxt_t)
                    for half in range(2):
                        py = pyr.next()
                        sl = slice(half * 512, (half + 1) * 512)
                        for hc in range(32):
                            P.mm(py, hT[:, hc, :], W2[:, hc, sl], start=(hc == 0), stop=(hc == 31))
                        P.tt(xn[:, sl], py, gB[:, sl], ALU.mult)
                        P.tt(xn[:, sl], xn[:, sl], xt[:, sl], ALU.add, q="pool")
                    if not last:
                        P.dma(V(d["h1" + sfx][t0:t0 + 128, :]), xn, q="pool")
                    else:
                        sm = smring.next()
                        ss = sm[:, 0:1]; rs = sm[:, 1:2]
                        P.memset(ss, 0.0)
                        P.act(junk, xn, AF.Square, accum_out=ss)
                        P.ts(rs, ss, 1.0 / 1024, 1e-6, ALU.mult, ALU.add)
                        P.act(rs, rs, AF.Sqrt)
                        P.recip(rs, rs)
                        P.stt(xn, xn, rs, fnB, ALU.mult, ALU.mult)
                        P.dma(V(d["out"][t0:t0 + 128, :]), xn, q="pool")
            P.flush()


def phase_inproj1(C):
    P, A, d = C.P, C.A, C.d
    with ExitStack() as es:
        W = A.sb(es, [128, 8, 5120], BF16)
        sring = Ring([A.sb(es, [128, 8, 256]) for _ in range(2)])
        load_w_bf16(C, es, W, d["w_in1"], 5120, sring, blk=256)
        ident = A.sb(es, [128, 128]); P.dma(ident, V(d["ident"]))
        affall = A.sb(es, [128, 2, 2, 4, 8]); P.dma(affall, V(d["aff"]))
        xring = Ring([A.sb(es, [128, 1024]) for _ in range(3)])
        xnring = Ring([A.sb(es, [128, 1024]) for _ in range(3)])
        junk = A.sb(es, [128, 1024], BF16)
        smring = Ring([A.sb(es, [128, 2]) for _ in range(6)])
        aring = Ring([A.sb(es, [128, 8, 128], BF16) for _ in range(2)])
        ptr = Ring([A.ps(es, [128, 8, 128]) for _ in range(1)])
        pfr = Ring([A.ps(es, [128, 512]) for _ in range(4)])
        st32 = Ring([A.sb(es, [128, 512]) for _ in range(4)])
        for s, (Tn, sfx) in enumerate(((C.T, ""), (C.L, "_c"))):
            affv = affall[:, 1, s]
            tiles = list(range(0, Tn, 128))
            st = {}

            def prep_a(t0):
                xt = xring.next(); xn = xnring.next()
                P.dma(xt, V(d["h1" + sfx][t0:t0 + 128, :]))
                norm_a(C, xt, smring.next(), junk, xn)
                st[t0] = xn

            def prep_b(t0):
                aT = aring.next()
                norm_b(C, st[t0], aT, affv, 0, ident, ptr.next())
                st[t0] = aT

            prep_a(tiles[0]); prep_b(tiles[0])
            for ti, t0 in enumerate(tiles):
                aT = st.pop(t0)
                nxt_t = tiles[ti + 1] if ti + 1 < len(tiles) else None
                if nxt_t is not None:
                    prep_a(nxt_t)
                for g in range(10):
                    if g == 5 and nxt_t is not None:
                        prep_b(nxt_t)
                    if s == 1 and (g < 2 or g >= 8):
                        continue
                    pf = pfr.next()
                    for kc in range(8):
                        P.mm(pf, aT[:, kc, :], W[:, kc, g * 512:(g + 1) * 512], start=(kc == 0), stop=(kc == 7))
                    stg = st32.next()
                    P.copy(stg, pf, q=("dve" if g % 2 == 0 else "act"))
                    P.dma(V(d["p1" + sfx][t0:t0 + 128, g * 512:(g + 1) * 512]), stg, q="pool")
        P.flush()


def hg_consts():
    u = np.arange(128)[:, None]; t = np.arange(128)[None, :]
    same = (u // 64) == (t // 64)
    cm = np.zeros((2, 128, 3, 128), np.float32)
    for dr in range(2):
        if dr == 0:
            befeq = (u <= t); aft = (u > t)
        else:
            befeq = (u >= t); aft = (u < t)
        cm[dr, :, 0, :] = same & befeq
        cm[dr, :, 1, :] = same & aft
        cm[dr, :, 2, :] = same & befeq
    return cm


def phase_hgrn(C):
    P, A, d = C.P, C.A, C.d
    with ExitStack() as es:
        sb = lambda shape, dt=F32: A.sb(es, shape, dt)
        ident = sb([128, 128]); P.dma(ident, V(d["ident"]))
        identb = sb([128, 128], BF16); P.copy(identb, ident)
        ones = sb([128, 128]); P.memset(ones, 1.0)
        lbB = sb([128, 1024]); omlbB = sb([128, 1024])
        with ExitStack() as es2:
            hgl = A.sb(es2, [128, 2, 1024]); P.dma(hgl, V(d["hglB"]))
            P.tt(lbB, hgl[:, 1, :], hgl[:, 0, :], ALU.subtract)
            P.act(lbB, lbB, AF.Sigmoid)
            P.ts(omlbB, lbB, -1.0, 1.0, ALU.mult, ALU.add, q="pool")
            P.flush()
        psr = Ring([A.ps(es, [128, 1024]) for _ in range(4)])
        fl = lambda v: v.re("p h e -> p (h e)")

        def stream(dr):
            cm = sb([128, 3, 128]); P.dma(cm, V(d["cmh"][dr]))
            pqR = [sb([128, 1024]) for _ in range(2)]; pfR = [sb([128, 1024]) for _ in range(2)]; piR = [sb([128, 1024]) for _ in range(2)]
            fg = sb([128, 1024]); gl = sb([128, 1024]); kx = sb([128, 1024]); ex = sb([128, 1024])
            qt_ = sb([128, 1024], BF16); kt_ = sb([128, 1024], BF16); kh = sb([128, 1024], BF16); pib = sb([128, 1024], BF16)
            QT = sb([128, 8, 128], BF16); KT = sb([128, 8, 128], BF16); QT0 = sb([128, 8, 128], BF16); QT1 = sb([128, 8, 128], BF16)
            attT = sb([128, 8, 128], BF16)
            Sbf = [sb([128, 8, 128], BF16) for _ in range(2)]
            P.memset(QT0, 0.0); P.memset(QT1, 0.0); P.memset(attT, 0.0)
            Wt = [sb([128, 8, 128]) for _ in range(2)]
            S = [sb([128, 8, 128]) for _ in range(3)]
            Stmp = sb([128, 8, 128]); Ot = sb([128, 1024])
            P.memset(S[0], 0.0)
            si = 0
            work = []
            for seq, Tn in ((1, C.L), (0, C.T)):
                nt = Tn // 128
                tiles = range(nt) if dr == 0 else range(nt - 1, -1, -1)
                work += [(seq, ti * 128) for ti in tiles]
            c0 = 1024 + dr * 1024

            def issue_loads(wi):
                seq_, t0_ = work[wi]
                p1_ = d["p1" + ("_c" if seq_ else "")]
                if seq_ == 0:
                    P.dma(pqR[wi % 2], V(p1_[t0_:t0_ + 128, 0:1024]), q="sp")
                P.dma(pfR[wi % 2], V(p1_[t0_:t0_ + 128, c0:c0 + 1024]), q="sp")
                P.dma(piR[wi % 2], V(p1_[t0_:t0_ + 128, 3072:4096]), q="sp")

            issue_loads(0)
            for wi, (seq, t0) in enumerate(work):
                if True:
                    want_o = (seq == 0)
                    pq = pqR[wi % 2]; pf = pfR[wi % 2]; pi = piR[wi % 2]
                    Sl = [S[si % 3], S[(si + 1) % 3], S[(si + 2) % 3]]
                    si += 2
                    if wi + 1 < len(work):
                        issue_loads(wi + 1)
                    yield
                    P.copy(pib, pi, q="act")
                    P.act(fg, pf, AF.Sigmoid)
                    P.tt(fg, fg, omlbB, ALU.mult, q="pool")
                    P.tt(fg, fg, lbB, ALU.add, q="dve")
                    yield
                    P.act(gl, fg, AF.Ln)
                    P.ts(kx, fg, -1.0, 1.0, ALU.mult, ALU.add, q="pool")
                    yield
                    pl = psr.next(); pd = psr.next()
                    for hf in range(2):
                        sl = slice(hf * 512, (hf + 1) * 512)
                        if want_o:
                            P.mm(pl[:, sl], cm[:, 0, :], gl[:, sl])
                        P.mm(pd[:, sl], cm[:, 1, :], gl[:, sl])
                    P.act(ex, pd, AF.Exp)
                    P.tt(kh, kx, ex, ALU.mult, q="dve")
                    if want_o:
                        P.act(fg, pq, AF.Silu)
                        P.act(ex, pl, AF.Exp)
                        P.tt(qt_, fg, ex, ALU.mult, q="pool")
                        P.act(ex, pl, AF.Exp, scale=-1.0)
                        P.tt(kt_, kx, ex, ALU.mult, q="pool")
                    yield
                    order = (0, 1) if dr == 0 else (1, 0)
                    for c in order:
                        tsl = slice(c * 64, (c + 1) * 64)
                        pw = psr.next()
                        for h in range(8):
                            hs = slice(h * 128, (h + 1) * 128)
                            P.mm(pw[:, hs], gl[tsl, hs], ones[tsl, :])
                        P.act(fl(Wt[c]), pw, AF.Exp)
                        yield
                    if want_o:
                        for (src, dst, eng) in ((qt_, QT, "dve"), (kt_, KT, "act")):
                            pt = psr.next()
                            ptb = V(pt.ap.bitcast(BF16), pt.key)
                            for h in range(8):
                                P.tr(ptb[:, h * 128:(h + 1) * 128], src[:, h * 128:(h + 1) * 128], identb)
                            P.copy(fl(dst), ptb[:, 0:1024], q=eng)
                            yield
                        P.copy(QT0[:, :, 0:64], QT[:, :, 0:64], q="dve")
                        P.copy(QT1[:, :, 64:128], QT[:, :, 64:128], q="pool")
                        for c in (0, 1):
                            tsl = slice(c * 64, (c + 1) * 64)
                            pa = psr.next()
                            for h in range(8):
                                P.mm(pa[:, h * 64:(h + 1) * 64], KT[:, h, :], QT[:, h, tsl])
                            P.tt(attT[tsl, :, tsl], pa[tsl, 0:512].re("p (h t) -> p h t", t=64),
                                 bc(cm[tsl, 2, tsl], [64, 8, 64], 1), ALU.mult)
                            yield
                    QTc = [QT0, QT1]
                    for i, c in enumerate(order):
                        tsl = slice(c * 64, (c + 1) * 64)
                        pk = psr.next()
                        for h in range(8):
                            hs = slice(h * 128, (h + 1) * 128)
                            P.mm(pk[:, hs], kh[tsl, hs], pib[tsl, hs])
                        if want_o:
                            P.copy(Sbf[i], Sl[i], q="act")
                        P.tt(Stmp, Sl[i], Wt[c], ALU.mult, q=("dve" if i == 0 else "pool"))
                        P.tt(fl(Sl[i + 1]), fl(Stmp), pk, ALU.add)
                        yield
                    if want_o:
                        po = psr.next()
                        for h in range(8):
                            hs = slice(h * 128, (h + 1) * 128)
                            P.mm(po[:, hs], QTc[order[0]][:, h, :], Sbf[0][:, h, :], start=True, stop=False)
                            P.mm(po[:, hs], attT[:, h, :], pib[:, hs], start=False, stop=False)
                            P.mm(po[:, hs], QTc[order[1]][:, h, :], Sbf[1][:, h, :], start=False, stop=True)
                        P.copy(Ot, po, q="act")
                        P.dma(V(d["hg_o%d" % dr][t0:t0 + 128, :]), Ot, q="pool")
                        yield

        interleave([stream(0), stream(1)])
        P.flush()


def phase_hgrn_post(C):
    P, A, d = C.P, C.A, C.d
    with ExitStack() as es:
        sb = lambda shape, dt=F32: A.sb(es, shape, dt)
        hgnB = sb([128, 1024]); P.dma(hgnB, V(d["hgnB"]))
        NB = 2
        R = lambda: Ring([sb([128, 1024]) for _ in range(NB)])
        o0R, o1R, gR, jR = R(), R(), R(), R()
        smR = Ring([sb([128, 2]) for _ in range(4)])
        for t0 in range(0, C.T, 128):
            o0, o1, pg, junk, sm = o0R.next(), o1R.next(), gR.next(), jR.next(), smR.next()
            P.dma(o0, V(d["hg_o0"][t0:t0 + 128, :]))
            P.dma(o1, V(d["hg_o1"][t0:t0 + 128, :]), q="act")
            P.dma(pg, V(d["p1"][t0:t0 + 128, 4096:5120]))
            P.tt(o0, o0, o1, ALU.add, q="pool")
            ss = sm[:, 0:1]; rs = sm[:, 1:2]
            P.memset(ss, 0.0)
            P.act(junk, o0, AF.Square, accum_out=ss)
            P.ts(rs, ss, 1.0 / 1024, 1e-6, ALU.mult, ALU.add)
            P.act(rs, rs, AF.Sqrt)
            P.recip(rs, rs)
            P.stt(o0, o0, rs, hgnB, ALU.mult, ALU.mult)
            P.act(pg, pg, AF.Silu)
            P.tt(o1, o0, pg, ALU.mult, q="pool")
            P.dma(V(d["mix1"][t0:t0 + 128, :]), o1, q="pool")
        P.flush()


ALL_PHASES = None


def all_phases():
    return [phase_consts, phase_inproj0, phase_na_pre, phase_rwkv, phase_rwkv_post,
            lambda C: phase_outproj(C, 0), lambda C: phase_mlp(C, 0), phase_inproj1, phase_hgrn, phase_hgrn_post,
            lambda C: phase_outproj(C, 1), lambda C: phase_mlp(C, 1)]


def run(inputs, T, L, nb, dbg=()):
    nc, C = build(T, L, all_phases(), dbg=dbg)
    maps = [prep_inputs(inputs, b, T, L) for b in range(nb)]
    res = run_bass_kernel_spmd(nc, maps, core_ids=list(range(nb)))
    return res


def kernel(**inputs):
    T = inputs["x"].shape[1]; L = inputs["ctx"].shape[1]; B = inputs["x"].shape[0]
    res = run(inputs, T, L, B)
    return np.stack([np.asarray(r["out"], np.float32) for r in res.results], axis=0)
```

```python
import numpy as np
from contextlib import ExitStack
import concourse.bass as bass
import concourse.mybir as mybir
from concourse.bass_utils import run_bass_kernel_spmd

F32 = mybir.dt.float32
BF16 = mybir.dt.bfloat16
AF = mybir.ActivationFunctionType
ALU = mybir.AluOpType
AX = mybir.AxisListType

QUEUES = ["pe", "act", "dve", "pool", "sp"]
COMPUTE = ["pe", "act", "dve", "pool"]
NS_DMA = 8


class V:
    __slots__ = ("ap", "key")

    def __init__(self, ap, key=None):
        self.ap = ap
        self.key = key

    def __getitem__(self, idx):
        return V(self.ap[idx], self.key)

    def k(self, key):
        return V(self.ap, key)

    def re(self, pat, **kw):
        return V(self.ap.rearrange(pat, **kw), self.key)


class Op:
    __slots__ = ("q", "fn", "deps", "dma", "marked", "cnt", "sem_i", "semval")


def _ap(x):
    return x.ap if isinstance(x, V) else x


class Prog:
    def __init__(self, nc):
        self.nc = nc
        self.phase = 0
        self.total = 0
        self.es = ExitStack()
        self.csem = {q: self.es.enter_context(nc.semaphore(f"c_{q}")) for q in COMPUTE}
        self.dsem = {q: [self.es.enter_context(nc.semaphore(f"d_{q}{i}")) for i in range(NS_DMA)] for q in QUEUES}
        self.cbase = {q: 0 for q in COMPUTE}
        self.dbase = {q: 0 for q in QUEUES}
        self._reset()

    def close(self):
        self.es.close()

    def _reset(self):
        self.q = {e: [] for e in QUEUES}
        self.last_w = {}
        self.readers = {}

    def add(self, q, fn, reads=(), writes=(), dma=False):
        op = Op()
        op.q = q; op.fn = fn; op.dma = dma; op.marked = False; op.cnt = 0
        deps = {}
        rk = [x.key for x in reads if isinstance(x, V) and x.key is not None]
        wk = [x.key for x in writes if isinstance(x, V) and x.key is not None]
        for k in rk:
            w = self.last_w.get(k)
            if w is not None:
                deps[w] = True
        for k in wk:
            w = self.last_w.get(k)
            if w is not None and w not in deps:
                deps[w] = False
            for r in self.readers.get(k, ()):
                if r not in deps:
                    deps[r] = False
        deps.pop(op, None)
        for k in rk:
            self.readers.setdefault(k, []).append(op)
        for k in wk:
            self.last_w[k] = op
            self.readers[k] = []
        op.deps = deps
        self.q[q].append(op)
        self.total += 1
        return op

    def flush(self):
        nc = self.nc
        self.phase += 1
        ph = self.phase
        for q in QUEUES:
            for op in self.q[q]:
                need = []
                for d, raw in op.deps.items():
                    if d.dma or op.dma:
                        need.append(d)
                    elif d.q != op.q:
                        need.append(d)
                    elif raw and op.q != "pe":
                        need.append(d)
                op.deps = need
                for d in need:
                    d.marked = True
        for q in COMPUTE:
            comp = [o for o in self.q[q] if not o.dma]
            if comp:
                comp[-1].marked = True
        fin = {}
        for q in COMPUTE:
            c = self.cbase[q]
            for o in self.q[q]:
                if not o.dma and o.marked:
                    c += 1
                    o.cnt = c
            fin[q] = c
            self.cbase[q] = c
        dfin = {}
        for q in QUEUES:
            i = self.dbase[q]
            for o in self.q[q]:
                if o.dma:
                    o.sem_i = i % NS_DMA
                    o.semval = 16 * (i // NS_DMA + 1)
                    i += 1
            dfin[q] = i
            self.dbase[q] = i
        csem = self.csem
        dsem = self.dsem
        with ExitStack() as es:
            block = es.enter_context(nc.Block())
            bname = {"pe": "tensor", "act": "scalar", "dve": "vector", "pool": "gpsimd", "sp": "sync"}
            for q in QUEUES:
                ops = self.q[q]

                def body(eng, q=q, ops=ops):
                    known = {}

                    def wait(sem, val):
                        kk = id(sem)
                        if known.get(kk, 0) < val:
                            eng.wait_ge(sem, val)
                            known[kk] = val
                    for op in ops:
                        for d in op.deps:
                            if d.dma:
                                wait(dsem[d.q][d.sem_i], d.semval)
                            else:
                                wait(csem[d.q], d.cnt)
                        if op.dma:
                            if op.semval > 16:
                                wait(dsem[q][op.sem_i], op.semval - 16)
                            ins = op.fn(eng)
                            ins.then_inc(dsem[q][op.sem_i], 16)
                        else:
                            ins = op.fn(eng)
                            if op.marked:
                                ins.then_inc(csem[q], 1)
                    for q2 in COMPUTE:
                        if fin[q2] > 0:
                            wait(csem[q2], fin[q2])
                    for q2 in QUEUES:
                        n = dfin[q2]
                        for i in range(min(n, NS_DMA)):
                            cntv = (n - i + NS_DMA - 1) // NS_DMA
                            wait(dsem[q2][i], 16 * cntv)
                getattr(block, bname[q])(body)
        self._reset()

    def dma(self, out, in_, q="sp", **kw):
        o, i = _ap(out), _ap(in_)
        return self.add(q, lambda e: e.dma_start(out=o, in_=i, **kw), reads=[in_], writes=[out], dma=True)

    def mm(self, out, lhsT, rhs, start=True, stop=True):
        o, l, r = _ap(out), _ap(lhsT), _ap(rhs)
        rd = [lhsT, rhs]
        if not start:
            rd.append(out)
        return self.add("pe", lambda e: e.matmul(o, l, r, start=start, stop=stop), reads=rd, writes=[out])

    def tr(self, out, in_, ident):
        o, i, d = _ap(out), _ap(in_), _ap(ident)
        return self.add("pe", lambda e: e.transpose(o, i, d), reads=[in_, ident], writes=[out])

    def act(self, out, in_, func, bias=None, scale=None, accum_out=None, q="act"):
        o, i = _ap(out), _ap(in_)
        kw = {}
        rd = [in_]
        wr = [out]
        if bias is not None:
            kw["bias"] = _ap(bias); rd.append(bias)
        if scale is not None:
            kw["scale"] = _ap(scale); rd.append(scale)
        if accum_out is not None:
            kw["accum_out"] = _ap(accum_out); wr.append(accum_out)
        return self.add(q, lambda e: e.activation(o, i, func, **kw), reads=rd, writes=wr)

    def tt(self, out, in0, in1, op, q="dve"):
        o, a, b = _ap(out), _ap(in0), _ap(in1)
        return self.add(q, lambda e: e.tensor_tensor(o, a, b, op), reads=[in0, in1], writes=[out])

    def ts(self, out, in0, s1, s2, op0, op1=None, accum_out=None, q="dve"):
        o, a = _ap(out), _ap(in0)
        rd = [in0, s1, s2]
        wr = [out]
        kw = {}
        if op1 is not None:
            kw["op1"] = op1
        if accum_out is not None:
            kw["accum_out"] = _ap(accum_out); wr.append(accum_out)
        return self.add(q, lambda e: e.tensor_scalar(o, a, _ap(s1), _ap(s2), op0, **kw), reads=rd, writes=wr)

    def stt(self, out, in0, scalar, in1, op0, op1, q="dve"):
        o, a, b = _ap(out), _ap(in0), _ap(in1)
        return self.add(q, lambda e: e.scalar_tensor_tensor(o, a, _ap(scalar), b, op0, op1),
                        reads=[in0, scalar, in1], writes=[out])

    def copy(self, out, in_, q="dve"):
        o, i = _ap(out), _ap(in_)
        if q == "act":
            return self.add(q, lambda e: e.copy(o, i), reads=[in_], writes=[out])
        return self.add(q, lambda e: e.tensor_copy(o, i), reads=[in_], writes=[out])

    def memset(self, out, val, q="pool"):
        o = _ap(out)
        return self.add(q, lambda e: e.memset(o, val), reads=[], writes=[out])

    def reduce(self, out, in_, op, axis=None, q="dve"):
        o, i = _ap(out), _ap(in_)
        ax = axis if axis is not None else AX.X
        return self.add(q, lambda e: e.tensor_reduce(o, i, ax, op), reads=[in_], writes=[out])

    def recip(self, out, in_):
        o, i = _ap(out), _ap(in_)
        return self.add("dve", lambda e: e.reciprocal(o, i), reads=[in_], writes=[out])


class Alloc:
    def __init__(self, nc):
        self.nc = nc
        self.n = 0

    def sb(self, es, shape, dt=F32, name=None):
        self.n += 1
        nm = f"{name or 't'}_{self.n}"
        t = es.enter_context(self.nc.sbuf_tensor(nm, list(shape), dt))
        return V(t[:], nm)

    def ps(self, es, shape, dt=F32, name=None):
        self.n += 1
        nm = f"{name or 'p'}_{self.n}"
        t = es.enter_context(self.nc.psum_tensor(nm, list(shape), dt))
        return V(t[:], nm)


class Ring:
    def __init__(self, items):
        self.items = items
        self.i = 0

    def next(self):
        x = self.items[self.i % len(self.items)]
        self.i += 1
        return x


D = 1024
KC = 8
NA_MASK = -30000.0


class Cx:
    pass


def load_w_bf16(C, es_outer, dst, src_ap, ncols, stage_ring, blk=512):
    P = C.P
    srcv = src_ap.rearrange("(c p) n -> p c n", p=128)
    i = 0
    for c0 in range(0, ncols, blk):
        n = min(blk, ncols - c0)
        st = stage_ring.next()
        P.dma(st[:, :, 0:n], V(srcv[:, :, c0:c0 + n]), q=("sp" if i % 2 == 0 else "act"))
        for kc in range(8):
            P.copy(dst[:, kc, c0:c0 + n], st[:, kc, 0:n], q=("dve", "pool", "act")[(i * 8 + kc) % 3])
        i += 1


def phase_consts(C):
    P, A, d = C.P, C.A, C.d
    with ExitStack() as es:
        cT = A.sb(es, [128, 8, 2]); sc = A.sb(es, [128, 8, 2])
        P.dma(cT, V(d["cT"]))
        P.act(sc, cT, AF.Silu)
        adab = A.sb(es, [128, 2, 48, 2]); P.dma(adab, V(d["adabT"]))
        gm = A.sb(es, [128, 2, 2, 8])
        P.dma(gm[:, :, 0, :], V(d["gmixT"])); P.dma(gm[:, :, 1, :], V(d["gmlpT"]))
        ident = A.sb(es, [128, 128]); P.dma(ident, V(d["ident"]))
        ones = A.sb(es, [128, 128]); P.memset(ones, 1.0)
        mT = A.sb(es, [128, 2, 48, 2])
        ring = Ring([A.sb(es, [128, 8, 512]) for _ in range(2)])
        pm = A.ps(es, [128, 4, 2])
        for l in range(2):
            wv = d["adaw"][l].rearrange("(c p) n -> p c n", p=128)
            for cb in range(12):
                w = ring.next()
                P.dma(w, V(wv[:, :, cb * 512:(cb + 1) * 512]), q=("sp" if cb % 2 == 0 else "act"))
                for j in range(4):
                    for kc in range(8):
                        P.mm(pm[:, j, :], w[:, kc, j * 128:(j + 1) * 128], sc[:, kc, :], start=(kc == 0), stop=(kc == 7))
                P.tt(mT[:, l, cb * 4:(cb + 1) * 4, :], pm, adab[:, l, cb * 4:(cb + 1) * 4, :], ALU.add)
        aff = A.sb(es, [128, 2, 2, 4, 8])
        for l in range(2):
            for s in range(2):
                P.stt(aff[:, l, s, 0, :], mT[:, l, 8:16, s], 1.0, gm[:, l, 0, :], ALU.add, ALU.mult)
                P.copy(aff[:, l, s, 1, :], mT[:, l, 0:8, s])
                P.stt(aff[:, l, s, 2, :], mT[:, l, 32:40, s], 1.0, gm[:, l, 1, :], ALU.add, ALU.mult)
                P.copy(aff[:, l, s, 3, :], mT[:, l, 24:32, s])
        P.dma(V(d["aff"]), aff, q="pool")
        dring = Ring([A.sb(es, [128, 128]) for _ in range(4)])
        gring = Ring([A.sb(es, [128, 1024]) for _ in range(2)])
        pb = A.ps(es, [128, 1024])
        for l in range(2):
            for s in range(2):
                for g, base in ((0, 16), (1, 40)):
                    for c in range(8):
                        dg = dring.next()
                        P.ts(dg, ident, mT[:, l, base + c, s:s + 1], None, ALU.mult)
                        P.mm(pb[:, c * 128:(c + 1) * 128], ones, dg)
                    gb = gring.next()
                    P.copy(gb, pb, q="act")
                    P.dma(V(d["gateB"][l, s, g]), gb, q="pool")
        P.flush()


def norm_a(C, xt, small, junk, xn):
    P = C.P
    ss = small[:, 0:1]; rs = small[:, 1:2]
    P.memset(ss, 0.0)
    P.act(junk, xt, AF.Square, accum_out=ss)
    P.ts(rs, ss, 1.0 / 1024, 1e-6, ALU.mult, ALU.add)
    P.act(rs, rs, AF.Sqrt)
    P.recip(rs, rs)
    P.ts(xn, xt, rs, None, ALU.mult)


def norm_b(C, xn, aT_dst, affv, which, ident, pt):
    P = C.P
    for c in range(8):
        P.tr(pt[:, c, :], xn[:, c * 128:(c + 1) * 128], ident)
    for c in range(8):
        g = affv[:, 2 * which, c:c + 1]; b = affv[:, 2 * which + 1, c:c + 1]
        if c % 2 == 0:
            P.ts(aT_dst[:, c, :], pt[:, c, :], g, b, ALU.mult, ALU.add)
        else:
            P.act(aT_dst[:, c, :], pt[:, c, :], AF.Identity, bias=b, scale=g)


def norm_to_aT(C, xt, aT_dst, affv, which, ident, pt, small, junk, xn):
    norm_a(C, xt, small, junk, xn)
    norm_b(C, xn, aT_dst, affv, which, ident, pt)


def phase_inproj0(C):
    P, A, d = C.P, C.A, C.d
    with ExitStack() as es:
        W = A.sb(es, [128, 8, 3328], BF16)
        sring = Ring([A.sb(es, [128, 8, 512]) for _ in range(2)])
        load_w_bf16(C, es, W, d["w_in0"], 3328, sring)
        ident = A.sb(es, [128, 128]); P.dma(ident, V(d["ident"]))
        affall = A.sb(es, [128, 2, 2, 4, 8]); P.dma(affall, V(d["aff"]))
        xring = Ring([A.sb(es, [128, 1024]) for _ in range(2)])
        xnring = Ring([A.sb(es, [128, 1024]) for _ in range(2)])
        junk = A.sb(es, [128, 1024])
        smring = Ring([A.sb(es, [128, 2]) for _ in range(4)])
        aring = Ring([A.sb(es, [128, 8, 512], BF16) for _ in range(2)])
        ptring = Ring([A.ps(es, [128, 8, 128]) for _ in range(1)])
        pfring = Ring([A.ps(es, [128, 512]) for _ in range(4)])
        st32 = Ring([A.sb(es, [128, 512]) for _ in range(4)])
        st16 = Ring([A.sb(es, [128, 512], BF16) for _ in range(4)])
        for s, (X, Tn, sfx) in enumerate(((d["x"], C.T, ""), (d["ctx"], C.L, "_c"))):
            affv = affall[:, 0, s]
            sts = list(range(0, Tn, 512))
            pend = {}

            def prep_st(t0):
                nt = min(512, Tn - t0)
                aT = aring.next()
                for i in range(nt // 128):
                    xt = xring.next()
                    P.dma(xt, V(X[t0 + i * 128:t0 + (i + 1) * 128, :]))
                    norm_to_aT(C, xt, aT[:, :, i * 128:(i + 1) * 128], affv, 0, ident, ptring.next(),
                               smring.next(), junk, xnring.next())
                pend[t0] = aT

            prep_st(sts[0])
            for sti, t0 in enumerate(sts):
                nt = min(512, Tn - t0)
                aT = pend.pop(t0)
                fm = [(1536, 64, "lo", 0), (1600, 64, "lo", 64), (1664, 128, "lo", 128)]
                fm += [(1792 + j * 128, 128, "q", j * 128) for j in range(4)]
                fm += [(2304 + j * 128, 128, "k", j * 128) for j in range(4)]
                for (c0, ncol, kind, r0) in fm:
                    pf = pfring.next()
                    for kc in range(8):
                        P.mm(pf[0:ncol, 0:nt], W[:, kc, c0:c0 + ncol], aT[:, kc, 0:nt], start=(kc == 0), stop=(kc == 7))
                    if kind == "lo":
                        st = st32.next()
                        P.copy(st[0:ncol, 0:nt], pf[0:ncol, 0:nt], q="act")
                        P.dma(V(d["lo_raw" + sfx][r0:r0 + ncol, t0:t0 + nt]), st[0:ncol, 0:nt], q="pool")
                    elif kind == "q":
                        st = st16.next()
                        P.act(st[0:ncol, 0:nt], pf[0:ncol, 0:nt], AF.Copy, scale=0.125)
                        P.dma(V(d["qT" + sfx][r0:r0 + ncol, t0:t0 + nt]), st[0:ncol, 0:nt], q="pool")
                    else:
                        st = st16.next()
                        P.copy(st[0:ncol, 0:nt], pf[0:ncol, 0:nt], q="dve")
                        P.dma(V(d["kT" + sfx][r0:r0 + ncol, t0:t0 + nt]), st[0:ncol, 0:nt], q="pool")
                if sti + 1 < len(sts):
                    prep_st(sts[sti + 1])
                for i in range(nt // 128):
                    for g in range(4):
                        c0 = g * 512 if g < 3 else 2816
                        pf = pfring.next()
                        for kc in range(8):
                            P.mm(pf, aT[:, kc, i * 128:(i + 1) * 128], W[:, kc, c0:c0 + 512], start=(kc == 0), stop=(kc == 7))
                        r0 = t0 + i * 128
                        if g < 3:
                            st = st32.next()
                            P.copy(st, pf, q=("dve" if g % 2 == 0 else "act"))
                            P.dma(V(d["rkv_raw" + sfx][r0:r0 + 128, g * 512:(g + 1) * 512]), st, q="pool")
                        else:
                            st = st16.next()
                            P.copy(st, pf, q="dve")
                            P.dma(V(d["vna" + sfx][r0:r0 + 128, :]), st, q="pool")
        P.flush()


def na_pairs(rows):
    out = []
    for i in range(rows // 2):
        r = 2 * i
        r0a = min(max(r - 4, 0), rows - 8); r0b = min(max(r - 3, 0), rows - 8)
        u0 = min(r0a, rows - 9)
        out.append((u0, (r0a - u0, r0a - r + 7, r0b - u0, r0b - (r + 1) + 7)))
    return out


def na_bias2(rpb, rows):
    pairs = na_pairs(rows)
    keys = []
    for _, k in pairs:
        if k not in keys:
            keys.append(k)
    H = rpb.shape[0]
    out = np.full((H, 128, len(keys), 576), NA_MASK, np.float32)
    q = np.arange(64)
    cs = np.clip(q - 8, 0, 48)
    for si, (da, oa, db, ob) in enumerate(keys):
        for half, (dd, oo) in enumerate(((da, oa), (db, ob))):
            for kr in range(8):
                j = dd + kr
                for kc in range(16):
                    kcol = cs + kc
                    out[:, half * 64 + q, si, j * 64 + kcol] = rpb[:, oo + kr, kcol - q + 15]
    return out, keys


def na_gen(C, es):
    P, A, d = C.P, C.A, C.d
    T, L = C.T, C.L
    rows = T // 64
    pairs = na_pairs(rows)
    keys = []
    for _, k in pairs:
        if k not in keys:
            keys.append(k)
    ns = len(keys)
    if True:
        sb = lambda shape, dt=F32: A.sb(es, shape, dt)
        ident = sb([128, 128]); P.dma(ident, V(d["ident"]))
        identb = sb([128, 128], BF16); P.copy(identb, ident)
        qT = [sb([64, T], BF16) for _ in range(2)]
        kT = [sb([64, T], BF16) for _ in range(2)]
        v64 = [sb([64, rows, 64], BF16) for _ in range(2)]
        qcT = [sb([64, L], BF16) for _ in range(2)]
        kcT = [sb([64, L], BF16) for _ in range(2)]
        vc64 = [sb([64, L // 64, 64], BF16) for _ in range(2)]
        nabf = sb([128, ns, 576])
        nabb = [sb([128, ns, 576], BF16) for _ in range(2)]
        psr = Ring([A.ps(es, [128, 1024]) for _ in range(2)])
        ptp = A.ps(es, [64, 16, 128], BF16)
        pot = A.ps(es, [128, 512])
        por = Ring([V(pot.ap[:, k * 64:(k + 1) * 64], "po%d" % k) for k in range(4)])
        pexr = Ring([sb([128, 832], BF16) for _ in range(3)])
        ptsr = Ring([sb([64, 13, 128], BF16) for _ in range(2)])
        smr = Ring([sb([128, 4]) for _ in range(6)])
        RB = 16
        ostr = Ring([sb([128, RB, 64]) for _ in range(3)])
        vview = d["vna"].rearrange("(r p) (h e) -> p r h e", p=64, e=64)
        vcview = d["vna_c"].rearrange("(r p) (h e) -> p r h e", p=64, e=64)
        oview = d["mix0"].rearrange("(i p) c -> p i c", p=128)
        ocview = d["mix0_c"].rearrange("(i p) c -> p i c", p=128)

        def loads(h):
            b = h % 2
            P.dma(qT[b], V(d["qT"][h * 64:(h + 1) * 64, :]))
            P.dma(kT[b], V(d["kT"][h * 64:(h + 1) * 64, :]), q="act")
            P.dma(v64[b], V(vview[:, :, h, :]))
            P.dma(qcT[b], V(d["qT_c"][h * 64:(h + 1) * 64, :]), q="act")
            P.dma(kcT[b], V(d["kT_c"][h * 64:(h + 1) * 64, :]))
            P.dma(vc64[b], V(vcview[:, :, h, :]), q="act")
            P.dma(nabf, V(d["nab2"][h]))
            P.copy(nabb[b], nabf, q="pool")

        units = []
        for h in range(8):
            b = h % 2
            npair = len(pairs)
            for i, (u0, key) in enumerate(pairs):
                si = keys.index(key)
                ql = qT[b][:, i * 128:(i + 1) * 128]
                mms = [(0, 512, [(ql, kT[b][:, u0 * 64:u0 * 64 + 512]), (identb, nabb[b][:, si, 0:512])]),
                       (512, 64, [(ql, kT[b][:, (u0 + 8) * 64:(u0 + 9) * 64]), (identb, nabb[b][:, si, 512:576])]),
                       (576, L, [(ql, kcT[b][:, :])])]
                vch = [v64[b][:, u0 + j, :] for j in range(9)] + [vc64[b][:, j, :] for j in range(L // 64)]
                units.append(dict(h=h, first=(i == 0), mms=mms, nkeys=576 + L, vch=vch, slot=i % RB,
                                  newst=(i % RB == 0),
                                  store=((oview, (i // RB) * RB, i % RB + 1) if (i % RB == RB - 1 or i == npair - 1) else None)))
            nc_ = L // 128
            for i in range(nc_):
                ql = qcT[b][:, i * 128:(i + 1) * 128]
                mms = [(0, L, [(ql, kcT[b][:, :])])]
                vch = [vc64[b][:, j, :] for j in range(L // 64)]
                units.append(dict(h=h, first=False, mms=mms, nkeys=L, vch=vch, slot=i, newst=(i == 0),
                                  store=((ocview, 0, nc_) if i == nc_ - 1 else None)))

        def s1(u):
            ps = psr.next(); u["ps"] = ps
            for (c0, ncol, mm) in u["mms"]:
                for j, (l, r) in enumerate(mm):
                    P.mm(ps[:, c0:c0 + ncol], l, r, start=(j == 0), stop=(j == len(mm) - 1))
            sm = smr.next(); u["sm"] = sm
            nk = u["nkeys"]
            P.reduce(sm[:, 0:1], ps[:, 0:nk], ALU.max)
            P.ts(sm[:, 1:2], sm[:, 0:1], -1.0, None, ALU.mult, q="pool")
            P.memset(sm[:, 2:3], 0.0)
            pe_ = pexr.next(); u["pexp"] = pe_
            P.act(pe_[:, 0:nk], ps[:, 0:nk], AF.Exp, bias=sm[:, 1:2], accum_out=sm[:, 2:3])

        def s2(u):
            nch = u["nkeys"] // 64
            pe_ = u["pexp"]
            for j in range(nch):
                P.tr(ptp[:, j, :], pe_[:, j * 64:(j + 1) * 64], identb)
            pts = ptsr.next(); u["pts"] = pts
            hlf = (nch + 1) // 2
            P.copy(pts[:, 0:hlf, :], ptp[:, 0:hlf, :], q="dve")
            P.copy(pts[:, hlf:nch, :], ptp[:, hlf:nch, :], q="act")

        cur_st = {}

        def s3(u):
            nch = u["nkeys"] // 64
            po = por.next(); pts = u["pts"]; sm = u["sm"]
            for j in range(nch):
                P.mm(po, pts[:, j, :], u["vch"][j], start=(j == 0), stop=(j == nch - 1))
            P.recip(sm[:, 3:4], sm[:, 2:3])
            if u["newst"]:
                cur_st["t"] = ostr.next()
            ost = cur_st["t"]
            P.ts(ost[:, u["slot"], :], po, sm[:, 3:4], None, ALU.mult)
            if u["store"] is not None:
                view, i0, n = u["store"]
                hh = u["h"]
                P.dma(V(view[:, i0:i0 + n, 512 + hh * 64:512 + (hh + 1) * 64]), ost[:, 0:n, :], q="pool")

        loads(0)
        n = len(units)
        for it in range(n + 2):
            if it < n:
                u = units[it]
                if u["first"] and u["h"] + 1 < 8:
                    pass
                s1(u)
                if it >= 3 and units[it - 3]["first"] and units[it - 3]["h"] + 1 < 8:
                    loads(units[it - 3]["h"] + 1)
            if 0 <= it - 1 < n:
                s2(units[it - 1])
            if 0 <= it - 2 < n:
                s3(units[it - 2])
            yield


def declare(C, dbg):
    nc = C.nc
    T, L = C.T, C.L
    d = {}

    def din(name, shape, dt=F32):
        d[name] = nc.dram_tensor(name, list(shape), dt, kind="ExternalInput").ap()

    def scr(name, shape, dt=F32):
        kind = "ExternalOutput" if name in dbg else "Internal"
        d[name] = nc.dram_tensor(name, list(shape), dt, kind=kind).ap()

    din("x", [T, D]); din("ctx", [L, D]); din("cT", [128, 8, 2])
    din("adaw", [2, D, 6 * D]); din("adabT", [128, 2, 48, 2])
    din("gmixT", [128, 2, 8]); din("gmlpT", [128, 2, 8])
    din("ident", [128, 128])
    din("w_in0", [D, 3328]); din("nab2", list(na_bias2(np.zeros((8, 15, 31), np.float32), T // 64)[0].shape))
    din("mupB", [128, 1536]); din("munB", [128, 1536]); din("rwB", [128, 6, 512])
    din("w_up", [64, 2, 512]); din("a_up", [64, 2, 512]); din("g_up", [128, 512]); din("browB", [128, 4, 512])
    din("mulo", [128, 2, 2]); din("cm", [2, 128, 9, 128])
    din("w_out0", [D, D]); din("w_out1", [D, D]); din("mlp_w1", [2, D, 4096]); din("mlp_w2", [2, 4096, D])
    din("w_in1", [D, 5120]); din("fnormB", [128, D])
    d["out"] = nc.dram_tensor("out", [T, D], F32, kind="ExternalOutput").ap()
    scr("mix1", [T, D]); scr("hmid1", [T, D]); scr("hg_o0", [T, D]); scr("hg_o1", [T, D])
    din("hglB", [128, 2, D]); din("hgnB", [128, D]); din("cmh", [2, 128, 3, 128])
    scr("aff", [128, 2, 2, 4, 8]); scr("gateB", [2, 2, 2, 128, D])
    for sfx, Tn in (("", T), ("_c", L)):
        scr("lo_raw" + sfx, [256, Tn]); scr("rkv_raw" + sfx, [Tn, 1536])
        scr("qT" + sfx, [512, Tn], BF16); scr("kT" + sfx, [512, Tn], BF16); scr("vna" + sfx, [Tn, 512], BF16)
        scr("mix0" + sfx, [Tn, D])
        scr("rw_g" + sfx, [Tn, 512]); scr("rw_sh" + sfx, [Tn, 1536]); scr("rw_kk" + sfx, [Tn, 512]); scr("rw_lo" + sfx, [128, Tn])
        for dr_ in range(2):
            scr("rw_o%d" % dr_ + sfx, [Tn, 512]); scr("rw_b%d" % dr_ + sfx, [Tn, 512])
        scr("hmid0" + sfx, [Tn, D]); scr("h1" + sfx, [Tn, D]); scr("p1" + sfx, [Tn, 5120])
    C.d = d


def build(T, L, phases, dbg=()):
    nc = bass.Bass("TRN2", target_bir_lowering=False)
    C = Cx()
    C.nc = nc; C.P = Prog(nc); C.A = Alloc(nc); C.T = T; C.L = L
    declare(C, dbg)
    for ph in phases:
        ph(C)
    return nc, C


def fm_cols(v):
    v = np.asarray(v, np.float32)
    lead = v.shape[:-1]
    return np.ascontiguousarray(np.moveaxis(v.reshape(lead + (8, 128)), -1, 0))


def na_bias(rpb, rows):
    H = rpb.shape[0]
    out = np.full((H, 64, 8, 512), NA_MASK, np.float32)
    q = np.arange(64)
    cs = np.clip(q - 8, 0, 48)
    reps = [0, 1, 2, 3, 4, rows - 3, rows - 2, rows - 1]
    for si, r in enumerate(reps):
        r0 = min(max(r - 4, 0), rows - 8)
        for kr in range(8):
            rr = r0 + kr - r + 7
            for kc in range(16):
                kcol = cs + kc
                out[:, q, si, kr * 64 + kcol] = rpb[:, rr, kcol - q + 15]
    return out


def rw_consts():
    u = np.arange(128)[:, None]; t = np.arange(128)[None, :]
    cm = np.zeros((2, 128, 9, 128), np.float32)
    for dr in range(2):
        if dr == 0:
            bef = (u < t); befeq = (u <= t); half = (u <= 63); aft = (u > t)
        else:
            bef = (u > t); befeq = (u >= t); half = (u >= 64); aft = (u < t)
        cm[dr, :, 0, :] = befeq.astype(np.float32) - half.astype(np.float32)
        cm[dr, :, 1, :] = bef.astype(np.float32) - half.astype(np.float32)
        cm[dr, :, 2, :] = aft
        cm[dr, :, 3, :] = bef
        cm[dr, :, 4, :] = befeq
        cm[dr, :, 5, :] = -bef.astype(np.float32)
        cm[dr, :, 6, :] = -(bef.T).astype(np.float32)
        cm[dr, :, 7, :] = np.broadcast_to(half, (128, 128))
        cm[dr, :, 8, :] = 1.0
    return cm


def prep_inputs(inp, b, T, L):
    f = lambda a: np.ascontiguousarray(np.asarray(a, np.float32))
    m = {}
    m["x"] = f(inp["x"][b][:T]); m["ctx"] = f(inp["ctx"][b][:L])
    cs = np.stack([np.asarray(inp["c"][b], np.float32), np.asarray(inp["c_ctx"], np.float32)], 0)
    m["cT"] = np.ascontiguousarray(np.transpose(fm_cols(cs), (0, 2, 1)))
    m["adaw"] = f(inp["ada_w"])
    ab = fm_cols(np.asarray(inp["ada_b"], np.float32).reshape(2, 6, D))
    ab = ab.reshape(128, 2, 48)
    m["adabT"] = np.ascontiguousarray(np.repeat(ab[:, :, :, None], 2, axis=3))
    m["gmixT"] = fm_cols(inp["norm_mix_g"]); m["gmlpT"] = fm_cols(inp["norm_mlp_g"])
    m["ident"] = np.eye(128, dtype=np.float32)
    m["w_in0"] = f(inp["ev_w_in"][0])
    m["nab2"] = na_bias2(np.asarray(inp["na_rpb"][0], np.float32), T // 64)[0]
    rep = lambda v: np.ascontiguousarray(np.broadcast_to(np.asarray(v, np.float32).reshape(1, -1), (128, np.asarray(v).size)))
    mup = np.asarray(inp["rw_mu_prev"][0], np.float32); mun = np.asarray(inp["rw_mu_next"][0], np.float32)
    m["mupB"] = rep(mup[:1536]); m["munB"] = rep(mun[:1536])
    ka = np.asarray(inp["rw_k_a"][0], np.float32)
    m["rwB"] = np.ascontiguousarray(np.stack([rep(inp["rw_k_k"][0]), rep(ka), rep(inp["rw_r_k"][0].reshape(-1)),
                                              rep(inp["rw_ln_g"][0]), rep(inp["rw_ln_b"][0]), rep(ka)], axis=1))
    m["w_up"] = np.ascontiguousarray(np.transpose(np.asarray(inp["rw_w_up"][0], np.float32), (1, 0, 2)))
    m["a_up"] = np.ascontiguousarray(np.transpose(np.asarray(inp["rw_a_up"][0], np.float32), (1, 0, 2)))
    m["g_up"] = f(inp["rw_g_up"][0])
    brow = np.concatenate([np.asarray(inp["rw_w0"][0], np.float32), np.asarray(inp["rw_a0"][0], np.float32)], 0)
    m["browB"] = np.ascontiguousarray(np.broadcast_to(brow[None], (128, 4, 512)))
    mulo = np.zeros((128, 2, 2), np.float32)
    mulo[:, 0, 0] = mup[1536:1664]; mulo[:, 0, 1] = mun[1536:1664]
    mulo[:, 1, 0] = mup[1664:1792]; mulo[:, 1, 1] = mun[1664:1792]
    m["mulo"] = mulo
    m["cm"] = rw_consts()
    m["w_out0"] = f(inp["ev_w_out"][0]); m["w_out1"] = f(inp["od_w_out"][0])
    m["mlp_w1"] = f(inp["mlp_w1"]); m["mlp_w2"] = f(inp["mlp_w2"]); m["w_in1"] = f(inp["od_w_in"][0])
    m["fnormB"] = rep(inp["final_norm_g"])
    m["hglB"] = np.ascontiguousarray(np.stack([rep(inp["hg_lower"][0]), rep(inp["hg_lower"][1])], axis=1))
    m["hgnB"] = rep(inp["hg_norm_g"][0]); m["cmh"] = hg_consts()
    return m


CDEC = -0.6065306597126334


def bc(v, shape, axis):
    return V(v.ap.unsqueeze(axis).to_broadcast(list(shape)), v.key)


def interleave(gens):
    gens = list(gens)
    while gens:
        for g in list(gens):
            try:
                next(g)
            except StopIteration:
                gens.remove(g)


def rwpre_gen(C, es, NB):
    P, A, d = C.P, C.A, C.d
    sb = lambda shape, dt=F32: A.sb(es, shape, dt)
    mupB = sb([128, 1536]); munB = sb([128, 1536]); c0B = sb([128, 1536])
    P.dma(mupB, V(d["mupB"])); P.dma(munB, V(d["munB"]))
    P.tt(c0B, mupB, munB, ALU.add, q="pool")
    P.ts(c0B, c0B, -1.0, 1.0, ALU.mult, ALU.add, q="pool")
    kkB = sb([128, 512]); P.dma(kkB, V(d["rwB"][:, 0, :]))
    gup = sb([128, 512]); P.dma(gup, V(d["g_up"]))
    mulo = sb([128, 2, 3]); P.dma(mulo[:, :, 1:3], V(d["mulo"]))
    P.tt(mulo[:, :, 0:1], mulo[:, :, 1:2], mulo[:, :, 2:3], ALU.add, q="pool")
    P.ts(mulo[:, :, 0:1], mulo[:, :, 0:1], -1.0, 1.0, ALU.mult, ALU.add, q="pool")
    NG = 3
    curR = [sb([128, 512]) for _ in range(NG)]; prvR = [sb([128, 512]) for _ in range(NG)]; nxtR = [sb([128, 512]) for _ in range(NG)]
    shR = [sb([128, 512]) for _ in range(NG)]
    tA = sb([128, 512]); tB = sb([128, 512])
    loAR = [sb([128, 130]) for _ in range(2)]; loGR = [sb([128, 130]) for _ in range(2)]
    loAs = sb([128, 128]); loGs = sb([128, 128])
    kk = sb([128, 512]); gsb = sb([128, 512]); hs = sb([128, 8])
    pg = A.ps(es, [128, 512])
    h8 = lambda v: v.re("p (h e) -> p h e", e=64)
    units = []
    for seq, Tn in ((1, C.L), (0, C.T)):
        for t0 in range(0, Tn, 128):
            for g in range(3):
                units.append((seq, Tn, t0, g))

    def loads(ui):
        seq, Tn, t0, g = units[ui]
        sfx = "_c" if seq else ""
        rkv = d["rkv_raw" + sfx]
        sl = slice(g * 512, (g + 1) * 512)
        cur = curR[ui % NG]; prv = prvR[ui % NG]; nxt = nxtR[ui % NG]
        P.dma(cur, V(rkv[t0:t0 + 128, sl]))
        if t0 == 0:
            P.memset(prv, 0.0)
            P.dma(prv[1:128, :], V(rkv[0:127, sl]))
        else:
            P.dma(prv, V(rkv[t0 - 1:t0 + 127, sl]))
        if t0 + 128 == Tn:
            P.memset(nxt, 0.0)
            P.dma(nxt[0:127, :], V(rkv[t0 + 1:t0 + 128, sl]))
        else:
            P.dma(nxt, V(rkv[t0 + 1:t0 + 129, sl]))
        if g == 0:
            lo = d["lo_raw" + sfx]
            ti = t0 // 128
            loA = loAR[ti % 2]; loG = loGR[ti % 2]
            lo_a = max(t0 - 1, 0); lo_b = min(t0 + 129, Tn)
            c_a = lo_a - (t0 - 1); c_b = c_a + (lo_b - lo_a)
            if c_a > 0 or c_b < 130:
                P.memset(loA, 0.0); P.memset(loG, 0.0)
            P.dma(loA[:, c_a:c_b], V(lo[0:128, lo_a:lo_b]))
            P.dma(loG[:, c_a:c_b], V(lo[128:256, lo_a:lo_b]))

    loads(0)
    if len(units) > 1:
        loads(1)
    for ui, (seq, Tn, t0, g) in enumerate(units):
        sfx = "_c" if seq else ""
        if ui + 2 < len(units):
            loads(ui + 2)
        cur = curR[ui % NG]; prv = prvR[ui % NG]; nxt = nxtR[ui % NG]; sh = shR[ui % NG]
        sl = slice(g * 512, (g + 1) * 512)
        P.tt(tA, cur, c0B[:, sl], ALU.mult, q="pool")
        P.tt(tB, prv, mupB[:, sl], ALU.mult, q="dve")
        P.tt(tA, tA, tB, ALU.add, q="pool")
        P.tt(tB, nxt, munB[:, sl], ALU.mult, q="dve")
        P.tt(sh, tA, tB, ALU.add, q="pool")
        P.dma(V(d["rw_sh" + sfx][t0:t0 + 128, sl]), sh)
        yield
        if g == 1:
            P.tt(kk, sh, kkB, ALU.mult, q="pool")
            P.tt(tA, kk, kk, ALU.mult, q="dve")
            P.reduce(hs, h8(tA), ALU.add)
            P.act(hs, hs, AF.Sqrt)
            P.ts(hs, hs, 1e-12, None, ALU.max)
            P.recip(hs, hs)
            P.tt(h8(kk), h8(kk), bc(hs, [128, 8, 64], 2), ALU.mult)
            P.dma(V(d["rw_kk" + sfx][t0:t0 + 128, :]), kk)
            yield
        if g == 0:
            ti = t0 // 128
            loA = loAR[ti % 2]; loG = loGR[ti % 2]
            for (src, dst, gi) in ((loA, loAs, 0), (loG, loGs, 1)):
                P.ts(dst, src[:, 1:129], mulo[:, gi, 0:1], None, ALU.mult)
                P.stt(dst, src[:, 0:128], mulo[:, gi, 1:2], dst, ALU.mult, ALU.add)
                P.stt(dst, src[:, 2:130], mulo[:, gi, 2:3], dst, ALU.mult, ALU.add)
            P.act(loAs[0:64, :], loAs[0:64, :], AF.Tanh)
            P.act(loGs, loGs, AF.Sigmoid)
            P.dma(V(d["rw_lo" + sfx][:, t0:t0 + 128]), loAs)
            yield
            P.mm(pg, loGs, gup)
            P.copy(gsb, pg, q="act")
            P.dma(V(d["rw_g" + sfx][t0:t0 + 128, :]), gsb)
            yield


def phase_na_pre(C):
    with ExitStack() as es:
        g1 = na_gen(C, es)
        g2 = rwpre_gen(C, es, 1)
        done1 = done2 = False
        while not (done1 and done2):
            if not done1:
                try:
                    next(g1)
                except StopIteration:
                    done1 = True
            for _ in range(2):
                if not done2:
                    try:
                        next(g2)
                    except StopIteration:
                        done2 = True
        C.P.flush()


def phase_rwkv(C):
    P, A, d = C.P, C.A, C.d
    with ExitStack() as es:
        sb = lambda shape, dt=F32: A.sb(es, shape, dt)
        ident = sb([128, 128]); P.dma(ident, V(d["ident"]))
        identb = sb([128, 128], BF16); P.copy(identb, ident)
        pB = sb([128, 6, 512]); P.dma(pB, V(d["rwB"]))
        kkB, kaB, rkB, lngB, lnbB, omkaB = [pB[:, i, :] for i in range(6)]
        P.ts(omkaB, kaB, -1.0, 1.0, ALU.mult, ALU.add, q="pool")
        wup = sb([64, 2, 512]); P.dma(wup, V(d["w_up"]))
        aup = sb([128, 2, 512]); P.dma(aup[64:128], V(d["a_up"]), q="act")
        browB = sb([128, 4, 512]); P.dma(browB, V(d["browB"]), q="act")
        psr = Ring([A.ps(es, [128, 1024]) for _ in range(4)])
        h8 = lambda v: v.re("p (h e) -> p h e", e=64)

        def stream(dr):
            cm = sb([128, 9, 128]); P.dma(cm, V(d["cm"][dr]))
            shR = [sb([128, 1536]) for _ in range(2)]; tA = sb([128, 512]); tB = sb([128, 512]); loR = [sb([128, 128]) for _ in range(2)]
            sw = sb([128, 512]); ad = sb([128, 512]); kkR = [sb([128, 512]) for _ in range(2)]; kd = sb([128, 512]); bt = sb([128, 512])
            hs = sb([128, 8])
            xr = sb([128, 512], BF16); xa = sb([128, 512], BF16); xb = sb([128, 512], BF16); xk = sb([128, 512], BF16)
            BH = sb([128, 512], BF16); KH = sb([128, 512], BF16); vb = sb([128, 512], BF16)
            ART = sb([64, 8, 256], BF16); BWT = sb([64, 8, 128], BF16); KWT = sb([64, 8, 128], BF16)
            Qr = Ring([sb([128, 8, 128], BF16) for _ in range(2)]); Xr = Ring([sb([128, 8, 128], BF16) for _ in range(2)])
            Acc = sb([128, 8, 128]); Accb = sb([128, 8, 128], BF16)
            ArbT = sb([128, 8, 128], BF16); AakT = sb([128, 8, 128], BF16); ArkT = sb([128, 8, 128], BF16)
            WtotB = sb([64, 8, 64]); WmidB = sb([64, 8, 64])
            bon = sb([128, 512])
            ST = [sb([64, 8, 64]) for _ in range(2)]
            S0m = sb([64, 8, 64], BF16); Stmp = sb([64, 8, 64]); Ysb = sb([128, 8, 64]); Usb = sb([128, 8, 64], BF16); Ot = sb([128, 512])
            P.memset(ST[0], 0.0)
            si = 0
            work = []
            for seq, Tn in ((1, C.L), (0, C.T)):
                nt = Tn // 128
                order = range(nt) if dr == 0 else range(nt - 1, -1, -1)
                work += [("_c" if seq else "", ti * 128) for ti in order]

            def issue_loads(wi):
                sfx_, t0_ = work[wi]
                P.dma(shR[wi % 2], V(d["rw_sh" + sfx_][t0_:t0_ + 128, :]), q="sp")
                P.dma(kkR[wi % 2], V(d["rw_kk" + sfx_][t0_:t0_ + 128, :]), q="sp")
                P.dma(loR[wi % 2], V(d["rw_lo" + sfx_][:, t0_:t0_ + 128]), q="sp")

            issue_loads(0)
            for wi, (sfx, t0) in enumerate(work):
                if True:
                    sh = shR[wi % 2]; kk = kkR[wi % 2]; lo = loR[wi % 2]
                    if wi + 1 < len(work):
                        issue_loads(wi + 1)
                    r_, k_, v_ = sh[:, 0:512], sh[:, 512:1024], sh[:, 1024:1536]
                    yield
                    pz = psr.next()
                    P.mm(pz[:, 0:512], lo[0:64, :], wup[:, dr, :])
                    P.mm(pz[:, 512:1024], lo[64:128, :], aup[64:128, dr, :])
                    P.tt(sw, pz[:, 0:512], browB[:, dr, :], ALU.add)
                    P.tt(ad, pz[:, 512:1024], browB[:, 2 + dr, :], ALU.add)
                    P.act(sw, sw, AF.Sigmoid)
                    P.act(ad, ad, AF.Sigmoid)
                    P.copy(vb, v_, q="act")
                    yield
                    P.tt(tA, ad, kaB, ALU.mult, q="dve")
                    P.tt(tA, tA, omkaB, ALU.add, q="dve")
                    P.tt(kd, k_, tA, ALU.mult, q="pool")
                    P.tt(bt, kk, ad, ALU.mult, q="pool")
                    yield
                    P.tt(tB, r_, kd, ALU.mult)
                    P.tt(tB, tB, rkB, ALU.mult)
                    P.reduce(hs, h8(tB), ALU.add)
                    P.tt(h8(bon), h8(v_), bc(hs, [128, 8, 64], 2), ALU.mult)
                    P.dma(V(d["rw_b%d" % dr + sfx][t0:t0 + 128, :]), bon, q="pool")
                    yield
                    pl = psr.next(); pl2 = psr.next()
                    P.mm(pl[:, 0:512], cm[:, 0, :], sw)
                    P.mm(pl2[:, 0:512], cm[:, 2, :], sw)
                    for h in range(8):
                        P.mm(pl2[0:64, 512 + h * 64:512 + (h + 1) * 64], sw[:, h * 64:(h + 1) * 64], cm[:, 8, 0:64])
                    P.act(tA, pl[:, 0:512], AF.Exp, scale=CDEC)
                    P.tt(xr, r_, tA, ALU.mult, q="pool")
                    P.act(tA, pl[:, 0:512], AF.Exp, scale=-CDEC)
                    P.tt(xb, bt, tA, ALU.mult, q="dve")
                    P.tt(xk, kd, tA, ALU.mult, q="pool")
                    P.tt(tB, pl[:, 0:512], sw, ALU.subtract)
                    P.act(tB, tB, AF.Exp, scale=CDEC)
                    P.tt(xa, kk, tB, ALU.mult, q="pool")
                    P.act(tB, pl2[:, 0:512], AF.Exp, scale=CDEC)
                    P.tt(BH, bt, tB, ALU.mult, q="dve")
                    P.tt(KH, kd, tB, ALU.mult, q="pool")
                    P.act(WtotB.re("p h e -> p (h e)"), pl2[0:64, 512:1024], AF.Exp, scale=CDEC)
                    yield
                    pw = psr.next()
                    for h in range(8):
                        P.mm(pw[0:64, h * 64:(h + 1) * 64], sw[:, h * 64:(h + 1) * 64], cm[:, 7, 0:64])
                    P.act(WmidB.re("p h e -> p (h e)"), pw[0:64, 0:512], AF.Exp, scale=CDEC)
                    yield
                    for (src, dst, off, eng) in ((xa, ART, 0, "dve"), (xr, ART, 128, "act"), (xb, BWT, 0, "dve"), (xk, KWT, 0, "act")):
                        pt = psr.next()
                        ptb = V(pt.ap.bitcast(BF16), pt.key)
                        for h in range(8):
                            P.tr(ptb[0:64, h * 128:(h + 1) * 128], src[:, h * 64:(h + 1) * 64], identb)
                        P.copy(dst[:, :, off:off + 128], ptb[0:64, 0:1024].re("p (h t) -> p h t", t=128), q=eng)
                        yield
                    Q = Qr.next(); X = Xr.next()
                    for hp in range(4):
                        pa = psr.next(); pn = psr.next()
                        for j in range(2):
                            h = hp * 2 + j
                            P.mm(pa[:, j * 256:(j + 1) * 256], BWT[:, h, :], ART[:, h, :])
                            P.mm(pa[:, 512 + j * 256:512 + (j + 1) * 256], KWT[:, h, :], ART[:, h, :])
                            P.mm(pn[:, j * 128:(j + 1) * 128], ART[:, h, 0:128], BWT[:, h, :])
                        hsl = slice(hp * 2, hp * 2 + 2)
                        pav = pa.re("p (a j c t) -> p a j c t", a=2, j=2, c=2)
                        P.tt(Q[:, hsl, :], pav[:, 0, :, 0, :], bc(cm[:, 5, :], [128, 2, 128], 1), ALU.mult)
                        P.tt(ArbT[:, hsl, :], pav[:, 0, :, 1, :], bc(cm[:, 4, :], [128, 2, 128], 1), ALU.mult)
                        P.tt(AakT[:, hsl, :], pav[:, 1, :, 0, :], bc(cm[:, 3, :], [128, 2, 128], 1), ALU.mult)
                        P.tt(ArkT[:, hsl, :], pav[:, 1, :, 1, :], bc(cm[:, 4, :], [128, 2, 128], 1), ALU.mult)
                        P.tt(X[:, hsl, :], pn[:, 0:256].re("p (j t) -> p j t", j=2), bc(cm[:, 6, :], [128, 2, 128], 1), ALU.mult)
                        yield
                    P.tt(Acc, Q, bc(ident, [128, 8, 128], 1), ALU.add, q="pool")
                    P.copy(Accb, Acc, q="act")
                    for lev in range(6):
                        px = psr.next()
                        for h in range(8):
                            P.mm(px[:, h * 128:(h + 1) * 128], Q[:, h, :], X[:, h, :])
                        Xn = Xr.next()
                        P.copy(Xn.re("p h t -> p (h t)"), px, q="act")
                        if lev < 5:
                            pq = psr.next()
                            for h in range(8):
                                P.mm(pq[:, h * 128:(h + 1) * 128], X[:, h, :], Q[:, h, :])
                            Qn = Qr.next()
                            P.copy(Qn.re("p h t -> p (h t)"), pq, q="dve")
                        yield
                        pc = psr.next()
                        for h in range(8):
                            P.mm(pc[:, h * 128:(h + 1) * 128], Xn[:, h, :], Accb[:, h, :])
                        P.tt(Acc.re("p h t -> p (h t)"), Acc.re("p h t -> p (h t)"), pc, ALU.add)
                        if lev < 5:
                            P.copy(Accb, Acc, q="act")
                        X = Xn
                        if lev < 5:
                            Q = Qn
                        yield
                    S_in = ST[si % 2]; S_out = ST[(si + 1) % 2]
                    si += 1
                    P.tt(S0m, S_in, WmidB, ALU.mult)
                    P.tt(Stmp, S_in, WtotB, ALU.mult, q="pool")
                    py = psr.next()
                    for h in range(8):
                        P.mm(py[:, h * 64:(h + 1) * 64], ART[:, h, 0:128], S0m[:, h, :], start=True, stop=False)
                        P.mm(py[:, h * 64:(h + 1) * 64], AakT[:, h, :], vb[:, h * 64:(h + 1) * 64], start=False, stop=True)
                    P.copy(Ysb.re("p h e -> p (h e)"), py[:, 0:512], q="act")
                    yield
                    pu = psr.next()
                    for h in range(8):
                        P.mm(pu[:, h * 64:(h + 1) * 64], Acc[:, h, :], Ysb[:, h, :])
                    P.ts(Usb.re("p h e -> p (h e)"), pu[:, 0:512], -1.0, None, ALU.mult)
                    yield
                    pss = psr.next()
                    for h in range(8):
                        P.mm(pss[0:64, h * 64:(h + 1) * 64], BH[:, h * 64:(h + 1) * 64], Usb[:, h, :], start=True, stop=False)
                        P.mm(pss[0:64, h * 64:(h + 1) * 64], KH[:, h * 64:(h + 1) * 64], vb[:, h * 64:(h + 1) * 64], start=False, stop=True)
                    P.tt(S_out.re("p h e -> p (h e)"), Stmp.re("p h e -> p (h e)"), pss[0:64, 0:512], ALU.add)
                    po = psr.next()
                    for h in range(8):
                        P.mm(po[:, h * 64:(h + 1) * 64], ART[:, h, 128:256], S0m[:, h, :], start=True, stop=False)
                        P.mm(po[:, h * 64:(h + 1) * 64], ArbT[:, h, :], Usb[:, h, :], start=False, stop=False)
                        P.mm(po[:, h * 64:(h + 1) * 64], ArkT[:, h, :], vb[:, h * 64:(h + 1) * 64], start=False, stop=True)
                    P.copy(Ot, po[:, 0:512], q="act")
                    P.dma(V(d["rw_o%d" % dr + sfx][t0:t0 + 128, :]), Ot, q="pool")
                    yield

        interleave([stream(0), stream(1)])
        P.flush()


def phase_rwkv_post(C):
    P, A, d = C.P, C.A, C.d
    with ExitStack() as es:
        sb = lambda shape, dt=F32: A.sb(es, shape, dt)
        pB = sb([128, 6, 512]); P.dma(pB, V(d["rwB"]))
        lngB = pB[:, 3, :]; lnbB = pB[:, 4, :]
        NB = 2
        R = lambda: Ring([sb([128, 512]) for _ in range(NB)])
        o0R, o1R, b0R, b1R, gR, resR, cenR = R(), R(), R(), R(), R(), R(), R()
        hsR = Ring([sb([128, 8, 2]) for _ in range(NB)])
        h8 = lambda v: v.re("p (h e) -> p h e", e=64)
        for seq, Tn in ((1, C.L), (0, C.T)):
            sfx = "_c" if seq else ""
            for t0 in range(0, Tn, 128):
                o0, o1, b0, b1, g_, res, cen, hs = (o0R.next(), o1R.next(), b0R.next(), b1R.next(), gR.next(),
                                                     resR.next(), cenR.next(), hsR.next())
                P.dma(o0, V(d["rw_o0" + sfx][t0:t0 + 128, :]))
                P.dma(o1, V(d["rw_o1" + sfx][t0:t0 + 128, :]), q="act")
                P.dma(b0, V(d["rw_b0" + sfx][t0:t0 + 128, :]))
                P.dma(b1, V(d["rw_b1" + sfx][t0:t0 + 128, :]), q="act")
                P.dma(g_, V(d["rw_g" + sfx][t0:t0 + 128, :]))
                P.tt(res, o0, o1, ALU.add, q="pool")
                P.tt(b0, b0, b1, ALU.add, q="pool")
                P.reduce(hs[:, :, 0], h8(res), ALU.add)
                P.ts(hs[:, :, 0], hs[:, :, 0], 1.0 / 64, None, ALU.mult)
                P.tt(h8(cen), h8(res), bc(hs[:, :, 0], [128, 8, 64], 2), ALU.subtract)
                P.tt(res, cen, cen, ALU.mult, q="pool")
                P.reduce(hs[:, :, 1], h8(res), ALU.add)
                P.ts(hs[:, :, 1], hs[:, :, 1], 1.0 / 64, 64e-5, ALU.mult, ALU.add)
                P.act(hs[:, :, 1], hs[:, :, 1], AF.Sqrt)
                P.recip(hs[:, :, 1], hs[:, :, 1])
                P.tt(h8(cen), h8(cen), bc(hs[:, :, 1], [128, 8, 64], 2), ALU.mult)
                P.tt(cen, cen, lngB, ALU.mult, q="pool")
                P.tt(cen, cen, lnbB, ALU.add, q="dve")
                P.tt(cen, cen, b0, ALU.add, q="pool")
                P.tt(res, cen, g_, ALU.mult, q="dve")
                P.dma(V(d["mix0" + sfx][t0:t0 + 128, 0:512]), res, q="pool")
        P.flush()


def phase_outproj(C, l):
    P, A, d = C.P, C.A, C.d
    with ExitStack() as es:
        W = A.sb(es, [128, 8, 1024], BF16)
        sring = Ring([A.sb(es, [128, 8, 512]) for _ in range(2)])
        load_w_bf16(C, es, W, d["w_out%d" % l], 1024, sring)
        ident = A.sb(es, [128, 128]); P.dma(ident, V(d["ident"]))
        gB = A.sb(es, [128, 1024])
        mring = Ring([A.sb(es, [128, 1024]) for _ in range(2)])
        hring = Ring([A.sb(es, [128, 1024]) for _ in range(2)])
        oring = Ring([A.sb(es, [128, 1024]) for _ in range(2)])
        mT = Ring([A.sb(es, [128, 8, 128], BF16) for _ in range(2)])
        ptr = Ring([A.ps(es, [128, 8, 128]) for _ in range(2)])
        pyr = Ring([A.ps(es, [128, 512]) for _ in range(4)])
        seqs = [(0, C.T, "")] + ([(1, C.L, "_c")] if l == 0 else [])
        for s, Tn, sfx in seqs:
            hin = d["x" if s == 0 else "ctx"] if l == 0 else d["h1" + sfx]
            P.dma(gB, V(d["gateB"][l, s, 0]))
            for t0 in range(0, Tn, 128):
                mt = mring.next(); ht = hring.next(); ot = oring.next(); mt_T = mT.next(); pt = ptr.next()
                P.dma(mt, V(d["mix%d" % l + sfx][t0:t0 + 128, :]))
                P.dma(ht, V(hin[t0:t0 + 128, :]), q="act")
                for c in range(8):
                    P.tr(pt[:, c, :], mt[:, c * 128:(c + 1) * 128], ident)
                P.copy(mt_T[:, 0:4, :], pt[:, 0:4, :], q="dve")
                P.copy(mt_T[:, 4:8, :], pt[:, 4:8, :], q="act")
                for half in range(2):
                    py = pyr.next()
                    sl = slice(half * 512, (half + 1) * 512)
                    for kc in range(8):
                        P.mm(py, mt_T[:, kc, :], W[:, kc, sl], start=(kc == 0), stop=(kc == 7))
                    P.tt(ot[:, sl], py, gB[:, sl], ALU.mult)
                    P.tt(ot[:, sl], ot[:, sl], ht[:, sl], ALU.add, q="pool")
                P.dma(V(d["hmid%d" % l + sfx][t0:t0 + 128, :]), ot, q="pool")
        P.flush()


def phase_mlp(C, l):
    P, A, d = C.P, C.A, C.d
    last = (l == 1)
    with ExitStack() as es:
        W1 = A.sb(es, [128, 8, 4096], BF16)
        W2 = A.sb(es, [128, 32, 1024], BF16)
        with ExitStack() as es2:
            sring = Ring([A.sb(es2, [128, 8, 256]) for _ in range(2)])
            load_w_bf16(C, es2, W1, d["mlp_w1"][l], 4096, sring, blk=256)
            w2v = d["mlp_w2"][l].rearrange("(c p) n -> p c n", p=128)
            for i in range(16):
                st = sring.next()
                stv = st.re("p c n -> p (c n)").re("p (c n) -> p c n", c=2)
                P.dma(stv, V(w2v[:, 2 * i:2 * i + 2, :]), q=("sp" if i % 2 == 0 else "act"))
                for j in range(2):
                    P.copy(W2[:, 2 * i + j, :], stv[:, j, :], q=("dve", "pool", "act")[(2 * i + j) % 3])
            P.flush()
        with ExitStack() as es2:
            sb = lambda shape, dt=F32: A.sb(es2, shape, dt)
            ident = sb([128, 128]); P.dma(ident, V(d["ident"]))
            affall = sb([128, 2, 2, 4, 8]); P.dma(affall, V(d["aff"]))
            gB = sb([128, 1024]); fnB = sb([128, 1024])
            if last:
                P.dma(fnB, V(d["fnormB"]))
            xring = Ring([sb([128, 1024]) for _ in range(3)])
            xnring = Ring([sb([128, 1024]) for _ in range(3)])
            junk = sb([128, 1024], BF16)
            smring = Ring([sb([128, 2]) for _ in range(6)])
            aring = Ring([sb([128, 8, 128], BF16) for _ in range(2)])
            hTr = Ring([sb([128, 32, 128], BF16) for _ in range(2)])
            rl = Ring([sb([128, 512]) for _ in range(2)])
            ptr = Ring([A.ps(es2, [128, 8, 128]) for _ in range(1)])
            phr = Ring([A.ps(es2, [128, 512]) for _ in range(2)])
            pyr = Ring([A.ps(es2, [128, 512]) for _ in range(2)])
            seqs = [(0, C.T, "")] + ([(1, C.L, "_c")] if l == 0 else [])
            for s, Tn, sfx in seqs:
                affv = affall[:, l, s]
                P.dma(gB, V(d["gateB"][l, s, 1]))
                tiles = list(range(0, Tn, 128))
                st = {}

                def prep_a(t0):
                    xt = xring.next(); xn = xnring.next()
                    P.dma(xt, V(d["hmid%d" % l + sfx][t0:t0 + 128, :]))
                    norm_a(C, xt, smring.next(), junk, xn)
                    st[t0] = [xt, xn, None]

                def prep_b(t0):
                    aT = aring.next()
                    norm_b(C, st[t0][1], aT, affv, 1, ident, ptr.next())
                    st[t0][2] = aT

                prep_a(tiles[0]); prep_b(tiles[0])
                for ti, t0 in enumerate(tiles):
                    xt, xn, aT = st.pop(t0)
                    hT = hTr.next()
                    nxt_t = tiles[ti + 1] if ti + 1 < len(tiles) else None
                    if nxt_t is not None:
                        prep_a(nxt_t)
                    for hq in range(8):
                        ph = phr.next()
                        for j in range(4):
                            hc = hq * 4 + j
                            for kc in range(8):
                                P.mm(ph[:, j * 128:(j + 1) * 128], W1[:, kc, hc * 128:(hc + 1) * 128], aT[:, kc, :],
                                     start=(kc == 0), stop=(kc == 7))
                        r = rl.next()
                        P.act(r, ph, AF.Relu)
                        P.tt(hT[:, hq * 4:(hq + 1) * 4, :].re("p c n -> p (c n)"), r, r, ALU.mult, q=("pool" if hq % 2 == 0 else "dve"))
                    if nxt_t is not None:
                        prep_b(nxt_t)
                    for half in range(2):
                        py = pyr.next()
                        sl = slice(half * 512, (half + 1) * 512)
                        for hc in range(32):
                            P.mm(py, hT[:, hc, :], W2[:, hc, sl], start=(hc == 0), stop=(hc == 31))
                        P.tt(xn[:, sl], py, gB[:, sl], ALU.mult)
                        P.tt(xn[:, sl], xn[:, sl], xt[:, sl], ALU.add, q="pool")
                    if not last:
                        P.dma(V(d["h1" + sfx][t0:t0 + 128, :]), xn, q="pool")
                    else:
                        sm = smring.next()
                        ss = sm[:, 0:1]; rs = sm[:, 1:2]
                        P.memset(ss, 0.0)
                        P.act(junk, xn, AF.Square, accum_out=ss)
                        P.ts(rs, ss, 1.0 / 1024, 1e-6, ALU.mult, ALU.add)
                        P.act(rs, rs, AF.Sqrt)
                        P.recip(rs, rs)
                        P.stt(xn, xn, rs, fnB, ALU.mult, ALU.mult)
                        P.dma(V(d["out"][t0:t0 + 128, :]), xn, q="pool")
            P.flush()


def phase_inproj1(C):
    P, A, d = C.P, C.A, C.d
    with ExitStack() as es:
        W = A.sb(es, [128, 8, 5120], BF16)
        sring = Ring([A.sb(es, [128, 8, 256]) for _ in range(2)])
        load_w_bf16(C, es, W, d["w_in1"], 5120, sring, blk=256)
        ident = A.sb(es, [128, 128]); P.dma(ident, V(d["ident"]))
        affall = A.sb(es, [128, 2, 2, 4, 8]); P.dma(affall, V(d["aff"]))
        xring = Ring([A.sb(es, [128, 1024]) for _ in range(3)])
        xnring = Ring([A.sb(es, [128, 1024]) for _ in range(3)])
        junk = A.sb(es, [128, 1024], BF16)
        smring = Ring([A.sb(es, [128, 2]) for _ in range(6)])
        aring = Ring([A.sb(es, [128, 8, 128], BF16) for _ in range(2)])
        ptr = Ring([A.ps(es, [128, 8, 128]) for _ in range(1)])
        pfr = Ring([A.ps(es, [128, 512]) for _ in range(4)])
        st32 = Ring([A.sb(es, [128, 512]) for _ in range(4)])
        for s, (Tn, sfx) in enumerate(((C.T, ""), (C.L, "_c"))):
            affv = affall[:, 1, s]
            tiles = list(range(0, Tn, 128))
            st = {}

            def prep_a(t0):
                xt = xring.next(); xn = xnring.next()
                P.dma(xt, V(d["h1" + sfx][t0:t0 + 128, :]))
                norm_a(C, xt, smring.next(), junk, xn)
                st[t0] = xn

            def prep_b(t0):
                aT = aring.next()
                norm_b(C, st[t0], aT, affv, 0, ident, ptr.next())
                st[t0] = aT

            prep_a(tiles[0]); prep_b(tiles[0])
            for ti, t0 in enumerate(tiles):
                aT = st.pop(t0)
                nxt_t = tiles[ti + 1] if ti + 1 < len(tiles) else None
                if nxt_t is not None:
                    prep_a(nxt_t)
                for g in range(10):
                    if g == 5 and nxt_t is not None:
                        prep_b(nxt_t)
                    if s == 1 and (g < 2 or g >= 8):
                        continue
                    pf = pfr.next()
                    for kc in range(8):
                        P.mm(pf, aT[:, kc, :], W[:, kc, g * 512:(g + 1) * 512], start=(kc == 0), stop=(kc == 7))
                    stg = st32.next()
                    P.copy(stg, pf, q=("dve" if g % 2 == 0 else "act"))
                    P.dma(V(d["p1" + sfx][t0:t0 + 128, g * 512:(g + 1) * 512]), stg, q="pool")
        P.flush()


def hg_consts():
    u = np.arange(128)[:, None]; t = np.arange(128)[None, :]
    same = (u // 64) == (t // 64)
    cm = np.zeros((2, 128, 3, 128), np.float32)
    for dr in range(2):
        if dr == 0:
            befeq = (u <= t); aft = (u > t)
        else:
            befeq = (u >= t); aft = (u < t)
        cm[dr, :, 0, :] = same & befeq
        cm[dr, :, 1, :] = same & aft
        cm[dr, :, 2, :] = same & befeq
    return cm


def phase_hgrn(C):
    P, A, d = C.P, C.A, C.d
    with ExitStack() as es:
        sb = lambda shape, dt=F32: A.sb(es, shape, dt)
        ident = sb([128, 128]); P.dma(ident, V(d["ident"]))
        identb = sb([128, 128], BF16); P.copy(identb, ident)
        ones = sb([128, 128]); P.memset(ones, 1.0)
        lbB = sb([128, 1024]); omlbB = sb([128, 1024])
        with ExitStack() as es2:
            hgl = A.sb(es2, [128, 2, 1024]); P.dma(hgl, V(d["hglB"]))
            P.tt(lbB, hgl[:, 1, :], hgl[:, 0, :], ALU.subtract)
            P.act(lbB, lbB, AF.Sigmoid)
            P.ts(omlbB, lbB, -1.0, 1.0, ALU.mult, ALU.add, q="pool")
            P.flush()
        psr = Ring([A.ps(es, [128, 1024]) for _ in range(4)])
        fl = lambda v: v.re("p h e -> p (h e)")

        def stream(dr):
            cm = sb([128, 3, 128]); P.dma(cm, V(d["cmh"][dr]))
            pqR = [sb([128, 1024]) for _ in range(2)]; pfR = [sb([128, 1024]) for _ in range(2)]; piR = [sb([128, 1024]) for _ in range(2)]
            fg = sb([128, 1024]); gl = sb([128, 1024]); kx = sb([128, 1024]); ex = sb([128, 1024])
            qt_ = sb([128, 1024], BF16); kt_ = sb([128, 1024], BF16); kh = sb([128, 1024], BF16); pib = sb([128, 1024], BF16)
            QT = sb([128, 8, 128], BF16); KT = sb([128, 8, 128], BF16); QT0 = sb([128, 8, 128], BF16); QT1 = sb([128, 8, 128], BF16)
            attT = sb([128, 8, 128], BF16)
            Sbf = [sb([128, 8, 128], BF16) for _ in range(2)]
            P.memset(QT0, 0.0); P.memset(QT1, 0.0); P.memset(attT, 0.0)
            Wt = [sb([128, 8, 128]) for _ in range(2)]
            S = [sb([128, 8, 128]) for _ in range(3)]
            Stmp = sb([128, 8, 128]); Ot = sb([128, 1024])
            P.memset(S[0], 0.0)
            si = 0
            work = []
            for seq, Tn in ((1, C.L), (0, C.T)):
                nt = Tn // 128
                tiles = range(nt) if dr == 0 else range(nt - 1, -1, -1)
                work += [(seq, ti * 128) for ti in tiles]
            c0 = 1024 + dr * 1024

            def issue_loads(wi):
                seq_, t0_ = work[wi]
                p1_ = d["p1" + ("_c" if seq_ else "")]
                if seq_ == 0:
                    P.dma(pqR[wi % 2], V(p1_[t0_:t0_ + 128, 0:1024]), q="sp")
                P.dma(pfR[wi % 2], V(p1_[t0_:t0_ + 128, c0:c0 + 1024]), q="sp")
                P.dma(piR[wi % 2], V(p1_[t0_:t0_ + 128, 3072:4096]), q="sp")

            issue_loads(0)
            for wi, (seq, t0) in enumerate(work):
                if True:
                    want_o = (seq == 0)
                    pq = pqR[wi % 2]; pf = pfR[wi % 2]; pi = piR[wi % 2]
                    Sl = [S[si % 3], S[(si + 1) % 3], S[(si + 2) % 3]]
                    si += 2
                    if wi + 1 < len(work):
                        issue_loads(wi + 1)
                    yield
                    P.copy(pib, pi, q="act")
                    P.act(fg, pf, AF.Sigmoid)
                    P.tt(fg, fg, omlbB, ALU.mult, q="pool")
                    P.tt(fg, fg, lbB, ALU.add, q="dve")
                    yield
                    P.act(gl, fg, AF.Ln)
                    P.ts(kx, fg, -1.0, 1.0, ALU.mult, ALU.add, q="pool")
                    yield
                    pl = psr.next(); pd = psr.next()
                    for hf in range(2):
                        sl = slice(hf * 512, (hf + 1) * 512)
                        if want_o:
                            P.mm(pl[:, sl], cm[:, 0, :], gl[:, sl])
                        P.mm(pd[:, sl], cm[:, 1, :], gl[:, sl])
                    P.act(ex, pd, AF.Exp)
                    P.tt(kh, kx, ex, ALU.mult, q="dve")
                    if want_o:
                        P.act(fg, pq, AF.Silu)
                        P.act(ex, pl, AF.Exp)
                        P.tt(qt_, fg, ex, ALU.mult, q="pool")
                        P.act(ex, pl, AF.Exp, scale=-1.0)
                        P.tt(kt_, kx, ex, ALU.mult, q="pool")
                    yield
                    order = (0, 1) if dr == 0 else (1, 0)
                    for c in order:
                        tsl = slice(c * 64, (c + 1) * 64)
                        pw = psr.next()
                        for h in range(8):
                            hs = slice(h * 128, (h + 1) * 128)
                            P.mm(pw[:, hs], gl[tsl, hs], ones[tsl, :])
                        P.act(fl(Wt[c]), pw, AF.Exp)
                        yield
                    if want_o:
                        for (src, dst, eng) in ((qt_, QT, "dve"), (kt_, KT, "act")):
                            pt = psr.next()
                            ptb = V(pt.ap.bitcast(BF16), pt.key)
                            for h in range(8):
                                P.tr(ptb[:, h * 128:(h + 1) * 128], src[:, h * 128:(h + 1) * 128], identb)
                            P.copy(fl(dst), ptb[:, 0:1024], q=eng)
                            yield
                        P.copy(QT0[:, :, 0:64], QT[:, :, 0:64], q="dve")
                        P.copy(QT1[:, :, 64:128], QT[:, :, 64:128], q="pool")
                        for c in (0, 1):
                            tsl = slice(c * 64, (c + 1) * 64)
                            pa = psr.next()
                            for h in range(8):
                                P.mm(pa[:, h * 64:(h + 1) * 64], KT[:, h, :], QT[:, h, tsl])
                            P.tt(attT[tsl, :, tsl], pa[tsl, 0:512].re("p (h t) -> p h t", t=64),
                                 bc(cm[tsl, 2, tsl], [64, 8, 64], 1), ALU.mult)
                            yield
                    QTc = [QT0, QT1]
                    for i, c in enumerate(order):
                        tsl = slice(c * 64, (c + 1) * 64)
                        pk = psr.next()
                        for h in range(8):
                            hs = slice(h * 128, (h + 1) * 128)
                            P.mm(pk[:, hs], kh[tsl, hs], pib[tsl, hs])
                        if want_o:
                            P.copy(Sbf[i], Sl[i], q="act")
                        P.tt(Stmp, Sl[i], Wt[c], ALU.mult, q=("dve" if i == 0 else "pool"))
                        P.tt(fl(Sl[i + 1]), fl(Stmp), pk, ALU.add)
                        yield
                    if want_o:
                        po = psr.next()
                        for h in range(8):
                            hs = slice(h * 128, (h + 1) * 128)
                            P.mm(po[:, hs], QTc[order[0]][:, h, :], Sbf[0][:, h, :], start=True, stop=False)
                            P.mm(po[:, hs], attT[:, h, :], pib[:, hs], start=False, stop=False)
                            P.mm(po[:, hs], QTc[order[1]][:, h, :], Sbf[1][:, h, :], start=False, stop=True)
                        P.copy(Ot, po, q="act")
                        P.dma(V(d["hg_o%d" % dr][t0:t0 + 128, :]), Ot, q="pool")
                        yield

        interleave([stream(0), stream(1)])
        P.flush()


def phase_hgrn_post(C):
    P, A, d = C.P, C.A, C.d
    with ExitStack() as es:
        sb = lambda shape, dt=F32: A.sb(es, shape, dt)
        hgnB = sb([128, 1024]); P.dma(hgnB, V(d["hgnB"]))
        NB = 2
        R = lambda: Ring([sb([128, 1024]) for _ in range(NB)])
        o0R, o1R, gR, jR = R(), R(), R(), R()
        smR = Ring([sb([128, 2]) for _ in range(4)])
        for t0 in range(0, C.T, 128):
            o0, o1, pg, junk, sm = o0R.next(), o1R.next(), gR.next(), jR.next(), smR.next()
            P.dma(o0, V(d["hg_o0"][t0:t0 + 128, :]))
            P.dma(o1, V(d["hg_o1"][t0:t0 + 128, :]), q="act")
            P.dma(pg, V(d["p1"][t0:t0 + 128, 4096:5120]))
            P.tt(o0, o0, o1, ALU.add, q="pool")
            ss = sm[:, 0:1]; rs = sm[:, 1:2]
            P.memset(ss, 0.0)
            P.act(junk, o0, AF.Square, accum_out=ss)
            P.ts(rs, ss, 1.0 / 1024, 1e-6, ALU.mult, ALU.add)
            P.act(rs, rs, AF.Sqrt)
            P.recip(rs, rs)
            P.stt(o0, o0, rs, hgnB, ALU.mult, ALU.mult)
            P.act(pg, pg, AF.Silu)
            P.tt(o1, o0, pg, ALU.mult, q="pool")
            P.dma(V(d["mix1"][t0:t0 + 128, :]), o1, q="pool")
        P.flush()


ALL_PHASES = None


def all_phases():
    return [phase_consts, phase_inproj0, phase_na_pre, phase_rwkv, phase_rwkv_post,
            lambda C: phase_outproj(C, 0), lambda C: phase_mlp(C, 0), phase_inproj1, phase_hgrn, phase_hgrn_post,
            lambda C: phase_outproj(C, 1), lambda C: phase_mlp(C, 1)]


def run(inputs, T, L, nb, dbg=()):
    nc, C = build(T, L, all_phases(), dbg=dbg)
    maps = [prep_inputs(inputs, b, T, L) for b in range(nb)]
    res = run_bass_kernel_spmd(nc, maps, core_ids=list(range(nb)))
    return res


def kernel(**inputs):
    T = inputs["x"].shape[1]; L = inputs["ctx"].shape[1]; B = inputs["x"].shape[0]
    res = run(inputs, T, L, B)
    return np.stack([np.asarray(r["out"], np.float32) for r in res.results], axis=0)
```
